# Optimizing a Trainium2 kernel written in Bass

```python
import math
import numpy as np
import jax
import jax.numpy as jnp
from jax import lax

D_MODEL = 1024
BATCH = 8
SEQ = 8192
DEPTH = 2

HEAD_DIM = 64
NSA_HEADS = 8
NSA_KV_HEADS = 2
NSA_GROUP = NSA_HEADS // NSA_KV_HEADS
CMP_BLOCK = 32
CMP_STRIDE = 16
CMP_HIDDEN = 256
SLC_BLOCK = 64
N_SELECT = 16
WINDOW = 512
NSA_Q_BLOCK = 64
N_BRANCH = 3
CONV_CHANNELS = 256
CONV_WIDTH = 31
GLA_HEADS = 4
GLA_KEY_DIM = 128
GLA_VALUE_DIM = 256
GLA_GATE_RANK = 16
GLA_TAU = 16.0
GLA_CHUNK = 64
REL_BUCKETS = 32
REL_MAX_DIST = 128
D_FF = -(-8 * D_MODEL // (3 * 256)) * 256
PLE_DIM = 256
EPS = 1e-6
NEG = -1e30
BIG = 1e30

D_MIX = NSA_HEADS * HEAD_DIM + CONV_CHANNELS + GLA_VALUE_DIM
IN_SPLITS = (
    NSA_HEADS * HEAD_DIM,
    NSA_KV_HEADS * HEAD_DIM, NSA_KV_HEADS * HEAD_DIM,
    NSA_KV_HEADS * HEAD_DIM, NSA_KV_HEADS * HEAD_DIM,
    NSA_KV_HEADS * HEAD_DIM, NSA_KV_HEADS * HEAD_DIM,
    NSA_HEADS * N_BRANCH,
    2 * CONV_CHANNELS,
    GLA_KEY_DIM, GLA_KEY_DIM, GLA_VALUE_DIM,
    GLA_GATE_RANK,
    GLA_VALUE_DIM,
)
D_IN = sum(IN_SPLITS)

kernel_name = "hymba_nsa_conformer_gla_block"


def rms_norm(x, g):
    x32 = x.astype(jnp.float32)
    y = x32 * lax.rsqrt(jnp.mean(x32 * x32, axis=-1, keepdims=True) + EPS)
    return (y * g.astype(jnp.float32)).astype(x.dtype)


def layer_norm(x, g, b):
    x32 = x.astype(jnp.float32)
    mu = jnp.mean(x32, axis=-1, keepdims=True)
    xc = x32 - mu
    y = xc * lax.rsqrt(jnp.mean(xc * xc, axis=-1, keepdims=True) + EPS)
    return (y * g.astype(jnp.float32) + b.astype(jnp.float32)).astype(x.dtype)


def t5_bucket(dist):
    n = jnp.maximum(dist, 0)
    max_exact = REL_BUCKETS // 2
    nf = jnp.maximum(n, 1).astype(jnp.float32)
    large = max_exact + (jnp.log(nf / max_exact) / math.log(REL_MAX_DIST / max_exact)
                         * (REL_BUCKETS - max_exact)).astype(jnp.int32)
    large = jnp.minimum(large, REL_BUCKETS - 1)
    return jnp.where(n < max_exact, n, large)


def masked_softmax(s, mask):
    p = jax.nn.softmax(jnp.where(mask, s, NEG), axis=-1)
    return jnp.where(mask, p, 0.0)


def compress_kv(kv, pos, w1, b1, w2):
    B, S, Hk, D = kv.shape
    r = CMP_BLOCK // CMP_STRIDE
    nsub = S // CMP_STRIDE
    ncmp = nsub - r + 1
    sub = kv.reshape(B, nsub, CMP_STRIDE, Hk, D)
    blocks = jnp.concatenate([sub[:, j:ncmp + j] for j in range(r)], axis=2)
    blocks = blocks + pos[None, None, :, None, :]
    flat = blocks.transpose(0, 1, 3, 2, 4).reshape(B, ncmp, Hk, CMP_BLOCK * D)
    return jax.nn.silu(flat @ w1 + b1) @ w2


def nsa_attention(q, kc, vc, ks, vs, kw, vw, gates, rel_bias):
    B, S = q.shape[0], q.shape[1]
    Hk, G, D = NSA_KV_HEADS, NSA_GROUP, HEAD_DIM
    ncmp = kc.shape[1]
    nsel = S // SLC_BLOCK
    n_top = min(N_SELECT, nsel)
    q = q.reshape(B, S, Hk, G, D) * (D ** -0.5)
    gates = gates.reshape(B, S, Hk, G, N_BRANCH)
    bias_h = rel_bias.T.astype(jnp.float32)
    bias_hkg = bias_h.reshape(Hk, G, REL_BUCKETS).transpose(0, 2, 1)
    cmp_end = jnp.arange(ncmp, dtype=jnp.int32) * CMP_STRIDE + (CMP_BLOCK - 1)
    rs, rc = SLC_BLOCK // CMP_STRIDE, CMP_BLOCK // CMP_STRIDE
    agg = (np.arange(nsel)[:, None, None] * rs + np.arange(rs)[None, :, None]
           - np.arange(rc)[None, None, :]).reshape(nsel, rs * rc)
    agg_valid = jnp.asarray((agg >= 0) & (agg < ncmp))
    agg_idx = jnp.asarray(np.clip(agg, 0, ncmp - 1), dtype=jnp.int32)
    ks_blk = ks.reshape(B, nsel, SLC_BLOCK, Hk, D).transpose(0, 3, 1, 2, 4)
    vs_blk = vs.reshape(B, nsel, SLC_BLOCK, Hk, D).transpose(0, 3, 1, 2, 4)
    kw_pad = jnp.pad(kw, ((0, 0), (WINDOW, 0), (0, 0), (0, 0)))
    vw_pad = jnp.pad(vw, ((0, 0), (WINDOW, 0), (0, 0), (0, 0)))
    b_idx = jnp.arange(B)[:, None, None, None]
    h_idx = jnp.arange(Hk)[None, :, None, None]
    blk_ids = jnp.arange(nsel, dtype=jnp.int32)
    offs = jnp.arange(SLC_BLOCK, dtype=jnp.int32)
    win_offs = jnp.arange(NSA_Q_BLOCK + WINDOW, dtype=jnp.int32)
    Q = NSA_Q_BLOCK

    def head_bias(dist):
        return bias_h[:, t5_bucket(dist)].reshape(Hk, G, *dist.shape)

    def block_fn(c):
        q0 = c * Q
        t = q0 + jnp.arange(Q, dtype=jnp.int32)
        qc = lax.dynamic_slice_in_dim(q, q0, Q, axis=1)
        gc = lax.dynamic_slice_in_dim(gates, q0, Q, axis=1)
        dist_c = t[:, None] - cmp_end[None, :]
        s_c = jnp.einsum('bqhgd,bnhd->bhgqn', qc, kc).astype(jnp.float32) + head_bias(dist_c)
        p_c = masked_softmax(s_c, dist_c >= 0)
        o_c = jnp.einsum('bhgqn,bnhd->bqhgd', p_c.astype(vc.dtype), vc)
        imp = p_c.sum(axis=2)
        imp = jnp.where(agg_valid, jnp.take(imp, agg_idx, axis=-1), 0.0).sum(-1)
        cur = t // SLC_BLOCK
        forced = ((blk_ids[None, :] == 0) | (blk_ids[None, :] == cur[:, None])
                  | (blk_ids[None, :] == cur[:, None] - 1))
        causal_blk = blk_ids[None, :] * SLC_BLOCK <= t[:, None]
        score = jnp.where(causal_blk, jnp.where(forced, BIG, imp), NEG)
        top_val, top_idx = lax.top_k(score, n_top)
        sel_ok = top_val > 0.5 * NEG
        k_sel = ks_blk[b_idx, h_idx, top_idx]
        v_sel = vs_blk[b_idx, h_idx, top_idx]
        key_pos = top_idx[..., None] * SLC_BLOCK + offs
        dist_s = t[None, None, :, None, None] - key_pos
        s_s = jnp.einsum('bqhgd,bhqnkd->bhgqnk', qc, k_sel).astype(jnp.float32)
        bias_s = bias_hkg[h_idx[..., None], t5_bucket(dist_s)]
        s_s = s_s + jnp.moveaxis(bias_s, -1, 2)
        mask_s = (sel_ok[..., None] & (dist_s >= 0))[:, :, None]
        p_s = masked_softmax(s_s.reshape(B, Hk, G, Q, n_top * SLC_BLOCK),
                             mask_s.reshape(B, Hk, 1, Q, n_top * SLC_BLOCK)).reshape(s_s.shape)
        o_s = jnp.einsum('bhgqnk,bhqnkd->bqhgd', p_s.astype(v_sel.dtype), v_sel)
        kwc = lax.dynamic_slice_in_dim(kw_pad, q0, Q + WINDOW, axis=1)
        vwc = lax.dynamic_slice_in_dim(vw_pad, q0, Q + WINDOW, axis=1)
        key_pos_w = q0 - WINDOW + win_offs
        dist_w = t[:, None] - key_pos_w[None, :]
        mask_w = (dist_w >= 0) & (dist_w < WINDOW) & (key_pos_w[None, :] >= 0)
        s_w = jnp.einsum('bqhgd,bkhd->bhgqk', qc, kwc).astype(jnp.float32) + head_bias(dist_w)
        p_w = masked_softmax(s_w, mask_w)
        o_w = jnp.einsum('bhgqk,bkhd->bqhgd', p_w.astype(vwc.dtype), vwc)
        return gc[..., 0:1] * o_c + gc[..., 1:2] * o_s + gc[..., 2:3] * o_w

    out = lax.map(block_fn, jnp.arange(S // Q, dtype=jnp.int32))
    return out.transpose(1, 0, 2, 3, 4, 5).reshape(B, S, NSA_HEADS * HEAD_DIM)


def conformer_conv(u, w_dw, b_dw, ln_g, ln_b):
    a, g = jnp.split(u, 2, axis=-1)
    h = a * jax.nn.sigmoid(g)
    h = lax.conv_general_dilated(h, w_dw[:, None, :], window_strides=(1,),
                                 padding=[(CONV_WIDTH - 1, 0)],
                                 dimension_numbers=('NWC', 'WIO', 'NWC'),
                                 feature_group_count=CONV_CHANNELS) + b_dw
    return jax.nn.silu(layer_norm(h, ln_g, ln_b))


def gla_chunked(q, k, v, log_a):
    B, S, H, dk = q.shape
    dv = v.shape[-1]
    C = GLA_CHUNK
    N = S // C

    def chunks(a):
        return a.astype(jnp.float32).reshape(B, N, C, H, a.shape[-1]).transpose(0, 3, 1, 2, 4)

    q, k, v, la = chunks(q), chunks(k), chunks(v), chunks(log_a)
    b = jnp.cumsum(la, axis=3)
    b_last = b[:, :, :, -1:]
    q_t = q * (dk ** -0.5) * jnp.exp(b)
    k_t = k * jnp.exp(-b)
    k_d = k * jnp.exp(b_last - b)
    causal = jnp.tril(jnp.ones((C, C), dtype=bool))
    attn = jnp.where(causal, jnp.einsum('bhncd,bhnsd->bhncs', q_t, k_t), 0.0)
    o_intra = jnp.einsum('bhncs,bhnsv->bhncv', attn, v)

    def step(state, xs):
        qn, kn, vn, dn = xs
        o = jnp.einsum('bhcd,bhdv->bhcv', qn, state)
        state = dn[..., None] * state + jnp.einsum('bhcd,bhcv->bhdv', kn, vn)
        return state, o

    xs = (jnp.moveaxis(q_t, 2, 0), jnp.moveaxis(k_d, 2, 0), jnp.moveaxis(v, 2, 0),
          jnp.moveaxis(jnp.exp(b_last[:, :, :, 0]), 2, 0))
    _, o_inter = lax.scan(step, jnp.zeros((B, H, dk, dv), jnp.float32), xs)
    o = o_intra + jnp.moveaxis(o_inter, 0, 2)
    return o.transpose(0, 2, 3, 1, 4).reshape(B, S, H, dv)


def setup_inputs(seed: int = 0) -> dict:
    key = jax.random.key(seed)
    ks = jax.random.split(key, 32)
    L = DEPTH

    def nrm(k, shape, scale):
        return jax.random.normal(k, shape, jnp.float32) * scale

    def gain(k, shape):
        return 1.0 + 0.05 * jax.random.normal(k, shape, jnp.float32)

    return {
        "x": nrm(ks[0], (BATCH, SEQ, D_MODEL), 1.0),
        "p": nrm(ks[1], (DEPTH, BATCH, SEQ, PLE_DIM), 1.0),
        "rel_bias": nrm(ks[2], (REL_BUCKETS, NSA_HEADS), 0.5),
        "mix_norm_g": gain(ks[3], (L, D_MODEL)),
        "w_in": nrm(ks[4], (L, D_MODEL, D_IN), D_MODEL ** -0.5),
        "w_out": nrm(ks[5], (L, D_MIX, D_MODEL), D_MIX ** -0.5),
        "cmp_pos_k": nrm(ks[6], (L, CMP_BLOCK, HEAD_DIM), 0.1),
        "cmp_w1_k": nrm(ks[7], (L, CMP_BLOCK * HEAD_DIM, CMP_HIDDEN), (CMP_BLOCK * HEAD_DIM) ** -0.5),
        "cmp_b1_k": nrm(ks[8], (L, CMP_HIDDEN), 0.02),
        "cmp_w2_k": nrm(ks[9], (L, CMP_HIDDEN, HEAD_DIM), CMP_HIDDEN ** -0.5),
        "cmp_pos_v": nrm(ks[10], (L, CMP_BLOCK, HEAD_DIM), 0.1),
        "cmp_w1_v": nrm(ks[11], (L, CMP_BLOCK * HEAD_DIM, CMP_HIDDEN), (CMP_BLOCK * HEAD_DIM) ** -0.5),
        "cmp_b1_v": nrm(ks[12], (L, CMP_HIDDEN), 0.02),
        "cmp_w2_v": nrm(ks[13], (L, CMP_HIDDEN, HEAD_DIM), CMP_HIDDEN ** -0.5),
        "conv_w": nrm(ks[14], (L, CONV_WIDTH, CONV_CHANNELS), CONV_WIDTH ** -0.5),
        "conv_b": nrm(ks[15], (L, CONV_CHANNELS), 0.02),
        "conv_ln_g": gain(ks[16], (L, CONV_CHANNELS)),
        "conv_ln_b": nrm(ks[17], (L, CONV_CHANNELS), 0.02),
        "gla_w_alpha": nrm(ks[18], (L, GLA_GATE_RANK, GLA_KEY_DIM), GLA_GATE_RANK ** -0.5),
        "gla_b_alpha": nrm(ks[19], (L, GLA_KEY_DIM), 0.02),
        "gla_norm_g": gain(ks[20], (L, GLA_VALUE_DIM)),
        "ffn_norm_g": gain(ks[21], (L, D_MODEL)),
        "ffn_w_gate": nrm(ks[22], (L, D_MODEL, D_FF), D_MODEL ** -0.5),
        "ffn_w_up": nrm(ks[23], (L, D_MODEL, D_FF), D_MODEL ** -0.5),
        "ffn_w_down": nrm(ks[24], (L, D_FF, D_MODEL), D_FF ** -0.5),
        "ple_norm_g": gain(ks[25], (L, D_MODEL)),
        "ple_w_gate": nrm(ks[26], (L, D_MODEL, D_MODEL), D_MODEL ** -0.5),
        "ple_w_proj": nrm(ks[27], (L, PLE_DIM, D_MODEL), PLE_DIM ** -0.5),
        "final_norm_g": gain(ks[28], (D_MODEL,)),
    }


def reference(x, p, rel_bias, mix_norm_g, w_in, w_out,
              cmp_pos_k, cmp_w1_k, cmp_b1_k, cmp_w2_k,
              cmp_pos_v, cmp_w1_v, cmp_b1_v, cmp_w2_v,
              conv_w, conv_b, conv_ln_g, conv_ln_b,
              gla_w_alpha, gla_b_alpha, gla_norm_g,
              ffn_norm_g, ffn_w_gate, ffn_w_up, ffn_w_down,
              ple_norm_g, ple_w_gate, ple_w_proj, final_norm_g):
    B, S, _ = x.shape
    split_at = np.cumsum(IN_SPLITS)[:-1].tolist()

    def heads(a, n):
        return a.reshape(B, S, n, -1)

    for i in range(DEPTH):
        h = rms_norm(x, mix_norm_g[i])
        z = h @ w_in[i]
        (nq, kc, vc, ks_, vs_, kw, vw, ng, cu, gq, gk, gv, ga, gr) = jnp.split(z, split_at, axis=-1)
        kc = compress_kv(heads(kc, NSA_KV_HEADS), cmp_pos_k[i], cmp_w1_k[i], cmp_b1_k[i], cmp_w2_k[i])
        vc = compress_kv(heads(vc, NSA_KV_HEADS), cmp_pos_v[i], cmp_w1_v[i], cmp_b1_v[i], cmp_w2_v[i])
        y_nsa = nsa_attention(heads(nq, NSA_HEADS), kc, vc,
                              heads(ks_, NSA_KV_HEADS), heads(vs_, NSA_KV_HEADS),
                              heads(kw, NSA_KV_HEADS), heads(vw, NSA_KV_HEADS),
                              jax.nn.sigmoid(ng).reshape(B, S, NSA_HEADS, N_BRANCH), rel_bias)
        y_conv = conformer_conv(cu, conv_w[i], conv_b[i], conv_ln_g[i], conv_ln_b[i])
        log_a = jax.nn.log_sigmoid((ga @ gla_w_alpha[i] + gla_b_alpha[i]).astype(jnp.float32)) / GLA_TAU
        o = gla_chunked(heads(gq, GLA_HEADS), heads(gk, GLA_HEADS), heads(gv, GLA_HEADS),
                        heads(log_a, GLA_HEADS))
        o = rms_norm(o, gla_norm_g[i].reshape(GLA_HEADS, -1)).reshape(B, S, GLA_VALUE_DIM)
        y_gla = (o * jax.nn.silu(gr)).astype(x.dtype)
        x = x + jnp.concatenate([y_nsa, y_conv, y_gla], axis=-1) @ w_out[i]
        h = rms_norm(x, ffn_norm_g[i])
        x = x + (jax.nn.silu(h @ ffn_w_gate[i]) * (h @ ffn_w_up[i])) @ ffn_w_down[i]
        gate = jax.nn.sigmoid(rms_norm(x, ple_norm_g[i]) @ ple_w_gate[i])
        x = x + (p[i] @ ple_w_proj[i]) * gate
    return rms_norm(x, final_norm_g)
```

```python
import math
from contextlib import ExitStack
import numpy as np
import ml_dtypes
import concourse.bass as bass
import concourse.mybir as mybir
from concourse.bass_utils import run_bass_kernel_spmd
from concourse.ap import AP

F32, BF16 = mybir.dt.float32, mybir.dt.bfloat16
AF = mybir.ActivationFunctionType
ALU = mybir.AluOpType
AX = mybir.AxisListType

D = 1024
DIN = 2600
DFF = 2816
NEGV = -30000.0
OFFS = 2063
NTS = 4592
NTW = 1024


class Buf:
    __slots__ = ("w", "r", "sem", "cnt", "scope")

    def __init__(self):
        self.w = {}
        self.r = {}
        self.sem = None
        self.cnt = 0
        self.scope = 0


def _merge(d, toks):
    for k, (sem, v) in toks.items():
        if k not in d or d[k][1] < v:
            d[k] = (sem, v)


class KB:
    def __init__(self, nc):
        self.nc = nc
        self.eng = {"pe": nc.tensor, "act": nc.scalar, "dve": nc.vector, "pool": nc.gpsimd, "sp": nc.sync}
        self.esem = {e: nc.alloc_semaphore("s_" + e) for e in ("pe", "act", "dve", "pool")}
        self.ecnt = {e: 0 for e in self.esem}
        self.waited = {e: {} for e in self.eng}
        self.bufs = {}
        self.free_sems = {}
        self.stacks = []
        self.uid = 0
        self.nsem = 0

    def push(self):
        self.stacks.append((ExitStack(), []))

    def pop(self):
        self.barrier()
        st, names = self.stacks.pop()
        for n in names:
            b = self.bufs.pop(n, None)
            if b is not None and b.sem is not None:
                for q, sc in b.sem.items():
                    self.free_sems.setdefault(q, []).append(tuple(sc))
        st.close()

    def sb(self, name, shape, dt):
        self.uid += 1
        nm = f"{name}_{self.uid}"
        t = self.stacks[-1][0].enter_context(self.nc.sbuf_tensor(nm, list(shape), dt))
        self.stacks[-1][1].append(nm)
        return t

    def ps(self, name, shape, dt=F32):
        self.uid += 1
        nm = f"{name}_{self.uid}"
        t = self.stacks[-1][0].enter_context(self.nc.psum_tensor(nm, list(shape), dt))
        self.stacks[-1][1].append(nm)
        return t

    def buf(self, ap):
        n = ap.tensor.name
        b = self.bufs.get(n)
        if b is None:
            b = self.bufs[n] = Buf()
        return b

    def _need(self, reads, writes, accum):
        need = {}
        for a in reads:
            _merge(need, self.buf(a).w)
        for a in writes:
            b = self.buf(a)
            if not accum:
                _merge(need, b.w)
            _merge(need, b.r)
        return need

    def _wait(self, e, need):
        own = self.esem["pe"].num if e == "pe" else None
        wd = self.waited[e]
        for k, (sem, v) in need.items():
            if k == own:
                continue
            if wd.get(k, 0) < v:
                self.eng[e].wait_ge(sem, v)
                wd[k] = v

    def _update(self, tok, reads, writes, accum):
        for a in writes:
            b = self.buf(a)
            if accum:
                _merge(b.w, tok)
            else:
                b.w = dict(tok)
                b.r = {}
        for a in reads:
            _merge(self.buf(a).r, tok)

    def op(self, e, fn, reads=(), writes=(), accum=False):
        reads = [a for a in reads if isinstance(a, AP)]
        self._wait(e, self._need(reads, writes, accum))
        ins = fn(self.eng[e])
        self.ecnt[e] += 1
        sem = self.esem[e]
        ins.then_inc(sem, 1)
        self._update({sem.num: (sem, self.ecnt[e])}, reads, writes, accum)
        return ins

    def mmg(self, out, pairs, reads_extra=()):
        reads = []
        for l, r in pairs:
            reads += [l, r]
        self._wait("pe", self._need(reads, [out], False))
        n = len(pairs)
        for i, (l, r) in enumerate(pairs):
            ins = self.nc.tensor.matmul(out, lhsT=l, rhs=r, start=(i == 0), stop=(i == n - 1))
        self.ecnt["pe"] += 1
        sem = self.esem["pe"]
        ins.then_inc(sem, 1)
        self._update({sem.num: (sem, self.ecnt["pe"])}, reads, [out], False)

    def mm(self, out, l, r, start, stop, last=True):
        self._wait("pe", self._need([l, r], [out], not start))
        ins = self.nc.tensor.matmul(out, lhsT=l, rhs=r, start=start, stop=stop, skip_group_check=True)
        self.ecnt["pe"] += 1
        sem = self.esem["pe"]
        ins.then_inc(sem, 1)
        self._update({sem.num: (sem, self.ecnt["pe"])}, [l, r], [out], not start)

    def tr(self, out, in_, ident):
        k = in_.shape[0]
        return self.op("pe", lambda e: e.transpose(out=out, in_=in_, identity=ident[0:k, 0:k]),
                       reads=[in_, ident], writes=[out], accum=True)

    def dma(self, q, out, in_, accum=False, sbuf_side=None, **kw):
        sbap = sbuf_side
        if sbap is None:
            sbap = out if self.is_sb(out) else in_
        b = self.buf(sbap)
        if b.sem is None:
            b.sem = {}
        if q not in b.sem:
            fl = self.free_sems.setdefault(q, [])
            if fl:
                b.sem[q] = list(fl.pop())
            else:
                self.nsem += 1
                b.sem[q] = [self.nc.alloc_semaphore(f"sd{self.nsem}"), 0]
        acc = accum or (not self.is_sb(out))
        self._wait(q, self._need([in_], [out], acc))
        ins = self.eng[q].dma_start(out=out, in_=in_, **kw)
        sc = b.sem[q]
        sc[1] += 16
        ins.then_inc(sc[0], 16)
        self._update({sc[0].num: (sc[0], sc[1])}, [in_], [out], acc)

    def is_sb(self, ap):
        return ap.tensor.name in self.sbnames

    sbnames = None

    def barrier(self):
        toks = {}
        for e, sem in self.esem.items():
            toks[sem.num] = (sem, self.ecnt[e])
        for b in self.bufs.values():
            if b.sem is not None:
                for sc in b.sem.values():
                    if sc[1] > 0:
                        toks[sc[0].num] = (sc[0], sc[1])
        for e in self.eng:
            wd = self.waited[e]
            for k, (sem, v) in toks.items():
                if wd.get(k, 0) < v:
                    self.eng[e].wait_ge(sem, v)
                    wd[k] = v
        for b in self.bufs.values():
            b.w = {}
            b.r = {}


class SBNames:
    def __init__(self, dram_names):
        self.d = dram_names

    def __contains__(self, n):
        return n not in self.d


def _t5_bucket_np(n):
    n = np.maximum(n, 0)
    nf = np.maximum(n, 1).astype(np.float32)
    large = 16 + (np.log(nf / np.float32(16)) / np.float32(math.log(8.0)) * np.float32(16)).astype(np.int32)
    large = np.minimum(large, 31)
    return np.where(n < 16, n, large)


def host_consts(S):
    NB = S // 64
    c = {}
    dist = np.arange(NTS) - OFFS
    oh = np.zeros((33, NTS), np.float32)
    bk = _t5_bucket_np(dist)
    for i in range(NTS):
        if dist[i] < 0:
            oh[32, i] = 1.0
        else:
            oh[bk[i], i] = 1.0
    c["oh_sel"] = oh
    dist = np.arange(NTW)
    ohw = np.zeros((33, NTW), np.float32)
    bk = _t5_bucket_np(dist)
    for i in range(NTW):
        if dist[i] >= 512:
            ohw[32, i] = 1.0
        else:
            ohw[bk[i], i] = 1.0
    c["oh_win"] = ohw
    E = np.zeros((128, S), np.float32)
    for j in range(NB):
        E[j, 64 * j:64 * j + 64] = 1.0
    c["e_tab"] = E.astype(ml_dtypes.bfloat16)
    W = 2 * NB - 1
    fw = np.zeros((128, W), np.float32)
    for p in range(128):
        b = 1 if p >= 64 else 0
        for cc in range(W):
            rel = cc - (NB - 1)
            if rel > b:
                fw[p, cc] = -1e4
            elif rel == b or rel == b - 1:
                fw[p, cc] = 1e4
    c["fw_tab"] = fw
    g = np.zeros((128, 3, 128), np.float32)
    for s in range(128):
        for t in range(128):
            if s // 64 == t // 64:
                if s <= t:
                    g[s, 0, t] = -1.0 / 16
                    g[s, 2, t] = 1.0
                else:
                    pass
                if s > t:
                    pass
    for s in range(128):
        for t in range(128):
            if s // 64 == t // 64 and s > t:
                g[s, 1, t] = -1.0 / 16
    c["gla_c"] = g
    bd = np.zeros((128, 256), np.float32)
    for p in range(128):
        bd[p, (p // 32) * 64:(p // 32) * 64 + 64] = 1.0
    c["gla_bd"] = bd
    hm = np.zeros((128, 4), np.float32)
    for p in range(128):
        hm[p, p // 32] = 1.0
    c["gla_hm"] = hm
    return c


WNAMES = ["rel_bias", "mix_norm_g", "w_in", "w_out", "cmp_pos_k", "cmp_w1_k", "cmp_b1_k", "cmp_w2_k",
          "cmp_pos_v", "cmp_w1_v", "cmp_b1_v", "cmp_w2_v", "conv_w", "conv_b", "conv_ln_g", "conv_ln_b",
          "gla_w_alpha", "gla_b_alpha", "gla_norm_g", "ffn_norm_g", "ffn_w_gate", "ffn_w_up", "ffn_w_down",
          "ple_norm_g", "ple_w_gate", "ple_w_proj", "final_norm_g"]
WSHAPES = {"rel_bias": [32, 8], "mix_norm_g": [2, 1024], "w_in": [2, 1024, 2600], "w_out": [2, 1024, 1024],
           "cmp_pos_k": [2, 32, 64], "cmp_w1_k": [2, 2048, 256], "cmp_b1_k": [2, 256], "cmp_w2_k": [2, 256, 64],
           "cmp_pos_v": [2, 32, 64], "cmp_w1_v": [2, 2048, 256], "cmp_b1_v": [2, 256], "cmp_w2_v": [2, 256, 64],
           "conv_w": [2, 31, 256], "conv_b": [2, 256], "conv_ln_g": [2, 256], "conv_ln_b": [2, 256],
           "gla_w_alpha": [2, 16, 128], "gla_b_alpha": [2, 128], "gla_norm_g": [2, 256],
           "ffn_norm_g": [2, 1024], "ffn_w_gate": [2, 1024, 2816], "ffn_w_up": [2, 1024, 2816],
           "ffn_w_down": [2, 2816, 1024], "ple_norm_g": [2, 1024], "ple_w_gate": [2, 1024, 1024],
           "ple_w_proj": [2, 256, 1024], "final_norm_g": [1024]}


FM = [(0, 128, .125), (128, 128, .125), (256, 128, .125), (384, 128, .125), (512, 128, 1.), (640, 128, 1.),
      (768, 128, 1.), (1024, 128, 1.), (1304, 128, 1.), (1432, 128, 1.), (1560, 128, 1.), (1688, 128, 1.),
      (1816, 128, 1.), (1944, 128, 1.), (2328, 16, 1.)]


def build(S=8192, layers=2, dbg_in=(), dbg_out=(), phases=None, final=True):
    nc = bass.Bass("TRN2", target_bir_lowering=False)
    kb = KB(nc)
    dram_names = set()
    kb.sbnames = SBNames(dram_names)
    NT = S // 128
    NQ5 = S // 512
    NB = S // 64
    NC = S // 16
    NCT = NC // 128
    NCV = NC - 1
    PBW = NC + 8 * (NT - 1)

    def dram(name, shape, dt, kind=None):
        if kind is None:
            kind = "ExternalInput" if name in dbg_in else ("ExternalOutput" if name in dbg_out else "Internal")
        t = nc.dram_tensor(name, list(shape), dt, kind=kind)
        dram_names.add(t.name)
        return t.ap()

    def on(ph):
        return phases is None or ph in phases

    x_in = dram("x", [S, D], F32, "ExternalInput")
    p_in = dram("p", [2, S, 256], F32, "ExternalInput")
    W = {n: dram(n, WSHAPES[n], F32, "ExternalInput") for n in WNAMES}
    hc = host_consts(S)
    C = {n: dram(n, list(v.shape), BF16 if v.dtype != np.float32 else F32, "ExternalInput") for n, v in hc.items()}
    out_d = dram("out", [S, D], F32, "ExternalOutput")
    qT = dram("qT", [8, 64, S], BF16)
    kcT = dram("kcT", [2, 128, S], BF16)
    ksT = dram("ksT", [128, S], BF16)
    kwT = dram("kwT", [128, S], BF16)
    cuT = dram("cuT", [512, S], BF16)
    gqT = dram("gqT", [128, S], BF16)
    gkT = dram("gkT", [128, S], BF16)
    gaT = dram("gaT", [16, S], BF16)
    vs_d = dram("vs", [S, 128], BF16)
    vw_d = dram("vw", [S, 128], BF16)
    gk_d = dram("gk", [S, 128], BF16)
    gv_d = dram("gv", [S, 256], BF16)
    grs_d = dram("grs", [S, 256], BF16)
    gates_d = dram("gates", [S, 24], F32)
    kcc_d = dram("kcc", [2, 64, NC], BF16)
    vcc_d = dram("vcc", [2, NC, 64], BF16)
    negT_d = dram("negT", [2, 128, S], BF16)
    tbs_d = dram("tbs", [8, NTS], F32)
    tbw_d = dram("tbw", [8, NTW], F32)
    ymixT = dram("ymixT", [1024, S], BF16)
    xa = dram("xa", [S, D], F32)
    xb = dram("xb", [S, D], F32)

    kb.push()
    cst = kb.sb("cst", [128, 8], F32)
    ident = kb.sb("ident", [128, 128], BF16)
    anti = kb.sb("anti", [128, 128], BF16)
    identf = kb.sb("identf", [128, 128], F32)
    onesf = kb.sb("onesf", [128, 128], F32)
    for col, v in enumerate([1e-6, 1.0, 0.0, 1e-30]):
        kb.op("pool", lambda e: e.memset(cst[:, col:col + 1], v), writes=[cst[:]], accum=True)
    kb.op("pool", lambda e: e.memset(onesf[:], 1.0), writes=[onesf[:]])
    for t, base in ((ident, 0), (identf, 0), (anti, -127)):
        kb.op("pool", lambda e: e.memset(t[:], 0.0), writes=[t[:]])
        pat = [[-1, 128]] if base == 0 else [[1, 128]]
        kb.op("pool", lambda e: e.affine_select(out=t[:], in_=t[:], pattern=pat, compare_op=ALU.not_equal,
                                                fill=1.0, base=base, channel_multiplier=1),
              reads=[t[:]], writes=[t[:]])
    EPS = cst[:, 0:1]

    def load_w(dst, src_rows, nchunks, ncols, gcol=None, stg=None):
        for c in range(nchunks):
            st = stg[c % len(stg)]
            kb.dma("sp", st[:, 0:ncols], src_rows[c * 128:(c + 1) * 128, :])
            if gcol is not None:
                kb.op("dve", lambda e: e.tensor_scalar(out=dst[:, c, :], in0=st[:, 0:ncols], scalar1=gcol[:, c:c + 1],
                                                       scalar2=None, op0=ALU.mult),
                      reads=[st[:], gcol[:]], writes=[dst[:]], accum=True)
            else:
                kb.op("dve", lambda e: e.tensor_copy(out=dst[:, c, :], in_=st[:, 0:ncols]),
                      reads=[st[:]], writes=[dst[:]], accum=True)

    def load_gcol(gcol, src1d):
        kb.dma("sp", gcol[:], src1d.rearrange("(c p) -> p c", p=128), allow_slow_non_contiguous=True)

    class NormBufs:
        def __init__(self):
            self.xt = [kb.sb("nxt", [128, D], F32) for _ in range(2)]
            self.sq = kb.sb("nsq", [128, D], F32)
            self.st4 = [kb.sb("nst4", [128, 4], F32) for _ in range(4)]
            self.xs = [kb.sb("nxs", [128, D], BF16) for _ in range(4)]
            self.pT = [kb.ps("npT", [128, 8, 128], BF16) for _ in range(2)]

    def norm_pre(nb, xsrc, t0):
        for s in range(4):
            xt = nb.xt[s % 2]
            st4 = nb.st4[s]
            xs = nb.xs[s]
            kb.dma("sp", xt[:], xsrc[t0 + s * 128:t0 + (s + 1) * 128, :])
            kb.op("act", lambda e: e.activation(out=nb.sq[:], in_=xt[:], func=AF.Square, accum_out=st4[:, 0:1]),
                  reads=[xt[:]], writes=[nb.sq[:], st4[:]])
            kb.op("act", lambda e: e.activation(out=st4[:, 1:2], in_=st4[:, 0:1], func=AF.Sqrt, scale=1.0 / D, bias=EPS),
                  reads=[st4[:], cst[:]], writes=[st4[:]])
            kb.op("dve", lambda e: e.reciprocal(out=st4[:, 2:3], in_=st4[:, 1:2]), reads=[st4[:]], writes=[st4[:]])
            kb.op("dve", lambda e: e.tensor_scalar(out=xs[:], in0=xt[:], scalar1=st4[:, 2:3], scalar2=None, op0=ALU.mult),
                  reads=[xt[:], st4[:]], writes=[xs[:]])

    def norm_tr(nb, hT):
        for s in range(4):
            pT = nb.pT[s % 2]
            for c in range(8):
                kb.tr(pT[:, c, :], nb.xs[s][:, c * 128:(c + 1) * 128], ident[:])
            kb.op("dve", lambda e: e.tensor_copy(out=hT[:, :, s * 128:(s + 1) * 128], in_=pT[:]),
                  reads=[pT[:]], writes=[hT[:]], accum=(s > 0))

    def phase_inproj(l, xsrc):
        kb.push()
        Wb = kb.sb("Wb", [128, 8, DIN], BF16)
        stg = [kb.sb("stg", [128, DIN], F32) for _ in range(2)]
        gcol = kb.sb("gcol", [128, 8], F32)
        load_gcol(gcol, W["mix_norm_g"][l])
        load_w(Wb, W["w_in"][l], 8, DIN, gcol, stg)
        nb = NormBufs()
        hTs = [kb.sb("hT", [128, 8, 512], BF16) for _ in range(2)]
        zTs = [kb.sb("zT", [128, 15, 512], BF16) for _ in range(2)]
        toks = [kb.sb("tok", [128, 4, 896], BF16) for _ in range(2)]
        gtt = [kb.sb("gt", [128, 4, 24], F32) for _ in range(2)]
        psA = [kb.ps("psA", [128, 512], F32) for _ in range(2)]
        psB = [kb.ps("psB", [128, 512], F32) for _ in range(4)]
        qT2 = qT.rearrange("h d t -> (h d) t")
        norm_pre(nb, xsrc, 0)
        norm_tr(nb, hTs[0])
        for i in range(NQ5):
            t0 = i * 512
            hT, zT, tok, gt = hTs[i % 2], zTs[i % 2], toks[i % 2], gtt[i % 2]
            for ci, (c0, m, scale) in enumerate(FM):
                if ci == 2 and i + 1 < NQ5:
                    norm_pre(nb, xsrc, t0 + 512)
                if ci == 12 and i + 1 < NQ5:
                    norm_tr(nb, hTs[(i + 1) % 2])
                ps = psA[ci % 2]
                kb.mmg(ps[0:m, :], [(Wb[:, c, c0:c0 + m], hT[:, c, :]) for c in range(8)])
                if ci % 2 == 0:
                    kb.op("act", lambda e: e.mul(out=zT[0:m, ci, :], in_=ps[0:m, :], mul=scale),
                          reads=[ps[:]], writes=[zT[:]], accum=True)
                else:
                    kb.op("dve", lambda e: e.tensor_scalar(out=zT[0:m, ci, :], in0=ps[0:m, :], scalar1=scale, scalar2=None,
                                                           op0=ALU.mult),
                          reads=[ps[:]], writes=[zT[:]], accum=True)
            for s in range(4):
                hs = lambda c: hT[:, c, s * 128:(s + 1) * 128]
                kb.mmg(psB[0][:, 0:128], [(hs(c), Wb[:, c, 896:1024]) for c in range(8)])
                kb.mmg(psB[1][:, 0:152], [(hs(c), Wb[:, c, 1152:1304]) for c in range(8)])
                kb.mmg(psB[2][:, 0:384], [(hs(c), Wb[:, c, 1944:2328]) for c in range(8)])
                kb.mmg(psB[3][:, 0:256], [(hs(c), Wb[:, c, 2344:2600]) for c in range(8)])
                kb.op("dve", lambda e: e.tensor_copy(out=tok[:, s, 0:128], in_=psB[0][:, 0:128]),
                      reads=[psB[0][:]], writes=[tok[:]], accum=True)
                kb.op("dve", lambda e: e.tensor_copy(out=tok[:, s, 128:256], in_=psB[1][:, 0:128]),
                      reads=[psB[1][:]], writes=[tok[:]], accum=True)
                kb.op("act", lambda e: e.activation(out=gt[:, s, :], in_=psB[1][:, 128:152], func=AF.Sigmoid),
                      reads=[psB[1][:]], writes=[gt[:]], accum=True)
                kb.op("dve", lambda e: e.tensor_copy(out=tok[:, s, 256:640], in_=psB[2][:, 0:384]),
                      reads=[psB[2][:]], writes=[tok[:]], accum=True)
                kb.op("act", lambda e: e.activation(out=tok[:, s, 640:896], in_=psB[3][:, 0:256], func=AF.Silu),
                      reads=[psB[3][:]], writes=[tok[:]], accum=True)
            sl = slice(t0, t0 + 512)
            kb.dma("pool", qT2[:, sl].rearrange("(c p) t -> p c t", p=128), zT[:, 0:4, :])
            kb.dma("pool", kcT[:, :, sl].rearrange("k p t -> p k t"), zT[:, 4:6, :])
            kb.dma("pool", ksT[:, sl], zT[:, 6, :])
            kb.dma("pool", kwT[:, sl], zT[:, 7, :])
            kb.dma("pool", cuT[:, sl].rearrange("(c p) t -> p c t", p=128), zT[:, 8:12, :])
            kb.dma("pool", gqT[:, sl], zT[:, 12, :])
            kb.dma("pool", gkT[:, sl], zT[:, 13, :])
            kb.dma("pool", gaT[:, sl], zT[0:16, 14, :])
            tv = lambda d: d[sl, :].rearrange("(s p) c -> p s c", p=128)
            kb.dma("pool", tv(vs_d), tok[:, :, 0:128])
            kb.dma("pool", tv(vw_d), tok[:, :, 128:256])
            kb.dma("pool", tv(gk_d), tok[:, :, 256:384])
            kb.dma("pool", tv(gv_d), tok[:, :, 384:640])
            kb.dma("pool", tv(grs_d), tok[:, :, 640:896])
            kb.dma("pool", tv(gates_d), gt[:])
        kb.pop()

    def phase_compress(l):
        kb.push()
        w1b = kb.sb("w1b", [64, 32, 256], BF16)
        w1s = kb.sb("w1s", [64, 8, 256], F32)
        posf = kb.sb("posf", [64, 32], F32)
        posb = kb.sb("posb", [64, 32], BF16)
        b1c = kb.sb("b1c", [128, 2], F32)
        bias = kb.sb("bias", [128, 2], F32)
        w2s = kb.sb("w2s", [128, 2, 64], F32)
        w2b = kb.sb("w2b", [128, 2, 64], BF16)
        src = kb.sb("src", [64, S], BF16)
        hid = kb.sb("hid", [128, 2, NC], BF16)
        osb = kb.sb("osb", [128, max(NC, NCT * 64)], BF16)
        pw = kb.ps("pw", [128, 2], F32)
        ph = kb.ps("ph", [128, 512], F32)
        po = kb.ps("po", [128, 512], F32)
        kb.op("pool", lambda e: e.memset(hid[:, :, NCV:NC], 0.0), writes=[hid[:]])
        for kv, sfx in ((0, "k"), (1, "v")):
            w1 = W["cmp_w1_" + sfx][l].rearrange("(l d) n -> d l n", d=64)
            for q4 in range(4):
                kb.dma("sp", w1s[:], w1[:, q4 * 8:(q4 + 1) * 8, :])
                kb.op("dve", lambda e: e.tensor_copy(out=w1b[:, q4 * 8:(q4 + 1) * 8, :], in_=w1s[:]),
                      reads=[w1s[:]], writes=[w1b[:]], accum=(q4 > 0))
            kb.dma("sp", posf[:], W["cmp_pos_" + sfx][l].rearrange("l d -> d l"), allow_slow_non_contiguous=True)
            kb.op("dve", lambda e: e.tensor_copy(out=posb[:], in_=posf[:]), reads=[posf[:]], writes=[posb[:]])
            kb.dma("sp", b1c[:], W["cmp_b1_" + sfx][l].rearrange("(c p) -> p c", p=128), allow_slow_non_contiguous=True)
            kb.dma("sp", w2s[:], W["cmp_w2_" + sfx][l].rearrange("(c p) d -> p c d", p=128))
            kb.op("dve", lambda e: e.tensor_copy(out=w2b[:], in_=w2s[:]), reads=[w2s[:]], writes=[w2b[:]])
            for half in range(2):
                kb.mmg(pw[:, half:half + 1], [(w1b[:, li, half * 128:(half + 1) * 128], posb[:, li:li + 1]) for li in range(32)])
            kb.op("dve", lambda e: e.tensor_tensor(out=bias[:], in0=pw[:], in1=b1c[:], op=ALU.add),
                  reads=[pw[:], b1c[:]], writes=[bias[:]])
            for h in range(2):
                kb.dma("sp", src[:], kcT[kv, h * 64:(h + 1) * 64, :])
                sv = src[:].rearrange("d (n s) -> d n s", s=16)
                for half in range(2):
                    pairs = []
                    for li in range(32):
                        rhs = sv[:, 0:NCV, li] if li < 16 else sv[:, 1:NCV + 1, li - 16]
                        pairs.append((w1b[:, li, half * 128:(half + 1) * 128], rhs))
                    kb.mmg(ph[:, 0:NCV], pairs)
                    kb.op("act", lambda e: e.activation(out=hid[:, half, 0:NCV], in_=ph[:, 0:NCV], func=AF.Silu,
                                                        bias=bias[:, half:half + 1]),
                          reads=[ph[:], bias[:]], writes=[hid[:]], accum=(half > 0))
                if kv == 0:
                    kb.mmg(po[0:64, 0:NC], [(w2b[:, half, :], hid[:, half, :]) for half in range(2)])
                    kb.op("dve", lambda e: e.tensor_copy(out=osb[0:64, 0:NC], in_=po[0:64, 0:NC]),
                          reads=[po[:]], writes=[osb[:]])
                    kb.dma("pool", kcc_d[h], osb[0:64, 0:NC])
                else:
                    for nt in range(NCT):
                        kb.mmg(po[:, nt * 64:(nt + 1) * 64],
                               [(hid[:, half, nt * 128:(nt + 1) * 128], w2b[:, half, :]) for half in range(2)])
                    kb.op("dve", lambda e: e.tensor_copy(out=osb[:, 0:NCT * 64], in_=po[:, 0:NCT * 64]),
                          reads=[po[:]], writes=[osb[:]])
                    kb.dma("pool", vcc_d[h].rearrange("(nt p) d -> p nt d", p=128),
                           osb[:, 0:NCT * 64].rearrange("p (nt d) -> p nt d", d=64))
        kb.pop()

    def phase_tables():
        kb.push()
        rel = kb.sb("rel", [33, 8], F32)
        r31 = kb.sb("r31", [32, 8], F32)
        oh = kb.sb("oh", [33, 512], F32)
        tsb = kb.sb("tsb", [8, 512], F32)
        pt = kb.ps("pt", [8, 512], F32)
        kb.dma("sp", rel[0:32, :], W["rel_bias"])
        kb.dma("sp", r31[:], AP(W["rel_bias"].tensor, 31 * 8, [[0, 32], [1, 8]]))
        kb.op("dve", lambda e: e.tensor_tensor(out=rel[0:32, :], in0=rel[0:32, :], in1=r31[:], op=ALU.subtract),
              reads=[rel[:], r31[:]], writes=[rel[:]])
        kb.op("pool", lambda e: e.memset(rel[32:33, :], NEGV), writes=[rel[:]])
        for tab, dst, n in ((C["oh_sel"], tbs_d, NTS), (C["oh_win"], tbw_d, NTW)):
            for c0 in range(0, n, 512):
                w = min(512, n - c0)
                kb.dma("sp", oh[:, 0:w], tab[:, c0:c0 + w])
                kb.mmg(pt[:, 0:w], [(rel[:], oh[:, 0:w])])
                kb.op("dve", lambda e: e.tensor_copy(out=tsb[:, 0:w], in_=pt[:, 0:w]), reads=[pt[:]], writes=[tsb[:]])
                kb.dma("pool", dst[:, c0:c0 + w], tsb[:, 0:w])
        kb.pop()

    def phase_select(l, gs):
        kc_sb = kb.sb("kc", [64, NC], BF16)
        pbw = kb.sb("pbw", [128, 4, PBW], F32)
        pbwb = kb.sb("pbwb", [128, 4, PBW], BF16)
        ptmp = kb.sb("ptmp", [128, 17], F32)
        fw = kb.sb("fw", [128, 2 * NB - 1], F32)
        qt_2 = [kb.sb("qt", [64, 4, 128], BF16) for _ in range(2)]
        sc_2 = [kb.sb("sc", [128, 4, 512], F32) for _ in range(2)]
        rs_2 = [kb.sb("rs", [128, 8], F32) for _ in range(2)]
        ipad_2 = [kb.sb("ipad", [128, NC + 4], F32) for _ in range(2)]
        sl_2 = [kb.sb("sl", [128, NB], F32) for _ in range(2)]
        t1_2 = [kb.sb("t1", [128, NB], F32) for _ in range(2)]
        sc2_2 = [kb.sb("sc2", [128, NB], F32) for _ in range(2)]
        m8_2 = [kb.sb("m8", [128, 16], F32) for _ in range(2)]
        negb_2 = [kb.sb("negb", [128, 128], BF16) for _ in range(2)]
        ntsb_2 = [kb.sb("ntsb", [128, 128], BF16) for _ in range(2)]
        S_ps = kb.ps("S_ps", [128, 2, 512], F32)
        tp = S_ps[:].rearrange("p a n -> p (a n)").bitcast(BF16)[:, 0:128]
        kb.dma("sp", fw[:], C["fw_tab"])
        yield
        cpat = 8 * (NT - 1) - 9
        for g in gs:
            kb.dma("sp", kc_sb[:], kcc_d[g])
            yield
            kb.op("pool", lambda e: e.memset(pbw[:], 0.0), writes=[pbw[:]])
            yield
            kb.op("pool", lambda e: e.memset(pbw[:, :, NC:PBW], NEGV), writes=[pbw[:]])
            yield
            for h in range(4):
                hg = 4 * g + h
                kb.dma("sp", ptmp[:], AP(tbs_d.tensor, hg * NTS + OFFS - 143, [[1, 128], [16, 17]]),
                       allow_slow_non_contiguous=True)
                yield
                for k2 in range(17):
                    col = cpat + 16 - k2
                    kb.op("pool", lambda e: e.tensor_copy(out=pbw[:, h, col:col + 1], in_=ptmp[:, k2:k2 + 1]),
                          reads=[ptmp[:]], writes=[pbw[:]])
                    yield
            kb.op("pool", lambda e: e.tensor_copy(out=pbwb[:], in_=pbw[:]), reads=[pbw[:]], writes=[pbwb[:]])
            yield
            for ipad in ipad_2:
                kb.op("pool", lambda e: e.memset(ipad[:], 0.0), writes=[ipad[:]])
                yield
            for ti in range(NT):
                t0 = ti * 128
                qt, sc, rs, ipad, sl, t1, sc2, m8, negb, ntsb = (qt_2[ti % 2], sc_2[ti % 2], rs_2[ti % 2], ipad_2[ti % 2], sl_2[ti % 2],
                                                                 t1_2[ti % 2], sc2_2[ti % 2], m8_2[ti % 2], negb_2[ti % 2], ntsb_2[ti % 2])
                ncol = min(NC, 8 * ti + 8)
                st = 8 * (NT - 1) - 8 * ti
                kb.dma("sp", qt[:], qT[4 * g:4 * g + 4, :, t0:t0 + 128].rearrange("h d t -> d h t"))
                yield
                for hh in range(2):
                    for h in (2 * hh, 2 * hh + 1):
                        kb.mmg(S_ps[:, h % 2, 0:ncol], [(qt[:, h, :], kc_sb[:, 0:ncol]), (ident[:], pbwb[:, h, st:st + ncol])])
                        yield
                    for h in (2 * hh, 2 * hh + 1):
                        kb.op("act", lambda e: e.activation(out=sc[:, h, 0:ncol], in_=S_ps[:, h % 2, 0:ncol], func=AF.Exp, accum_out=rs[:, h:h + 1]),
                              reads=[S_ps[:]], writes=[sc[:], rs[:]], accum=(h > 0))
                        yield
                kb.op("dve", lambda e: e.tensor_scalar_max(out=rs[:, 0:4], in0=rs[:, 0:4], scalar1=1e-30),
                      reads=[rs[:]], writes=[rs[:]])
                yield
                kb.op("dve", lambda e: e.reciprocal(out=rs[:, 4:8], in_=rs[:, 0:4]), reads=[rs[:]], writes=[rs[:]])
                yield
                iw = ipad[:, 1:1 + ncol]
                kb.op("dve", lambda e: e.tensor_scalar(out=iw, in0=sc[:, 0, 0:ncol], scalar1=rs[:, 4:5], scalar2=None, op0=ALU.mult),
                      reads=[sc[:], rs[:]], writes=[ipad[:]])
                yield
                for h in range(1, 4):
                    kb.op("dve", lambda e: e.scalar_tensor_tensor(out=iw, in0=sc[:, h, 0:ncol], scalar=rs[:, 4 + h:5 + h], in1=iw,
                                                                  op0=ALU.mult, op1=ALU.add),
                          reads=[sc[:], rs[:], ipad[:]], writes=[ipad[:]])
                    yield
                iv = ipad[:, 0:4 * NB].rearrange("p (j f) -> p j f", f=4)
                iv4 = ipad[:, 4:4 + 4 * NB].rearrange("p (j f) -> p j f", f=4)
                kb.op("dve", lambda e: e.tensor_tensor(out=sl[:], in0=iv[:, :, 0], in1=iv4[:, :, 0], op=ALU.add),
                      reads=[ipad[:]], writes=[sl[:]])
                yield
                kb.op("dve", lambda e: e.tensor_tensor(out=t1[:], in0=iv[:, :, 1], in1=iv[:, :, 2], op=ALU.add),
                      reads=[ipad[:]], writes=[t1[:]])
                yield
                kb.op("dve", lambda e: e.tensor_tensor(out=t1[:], in0=t1[:], in1=iv[:, :, 3], op=ALU.add),
                      reads=[ipad[:], t1[:]], writes=[t1[:]])
                yield
                kb.op("dve", lambda e: e.scalar_tensor_tensor(out=sl[:], in0=t1[:], scalar=2.0, in1=sl[:], op0=ALU.mult, op1=ALU.add),
                      reads=[t1[:], sl[:]], writes=[sl[:]])
                yield
                off = NB - 1 - 2 * ti
                kb.op("dve", lambda e: e.tensor_tensor(out=sl[:], in0=sl[:], in1=fw[:, off:off + NB], op=ALU.add),
                      reads=[sl[:], fw[:]], writes=[sl[:]])
                yield
                kb.op("dve", lambda e: e.tensor_scalar_add(out=sl[:, 0:1], in0=sl[:, 0:1], scalar1=1e4),
                      reads=[sl[:]], writes=[sl[:]])
                yield
                kb.op("dve", lambda e: e.max(out=m8[:, 0:8], in_=sl[:]), reads=[sl[:]], writes=[m8[:]])
                yield
                kb.op("dve", lambda e: e.match_replace(out=sc2[:], in_to_replace=m8[:, 0:8], in_values=sl[:], imm_value=-3e4),
                      reads=[sl[:], m8[:]], writes=[sc2[:]])
                yield
                kb.op("dve", lambda e: e.max(out=m8[:, 8:16], in_=sc2[:]), reads=[sc2[:]], writes=[m8[:]])
                yield
                kb.op("dve", lambda e: e.tensor_scalar(out=t1[:], in0=sl[:], scalar1=m8[:, 15:16], scalar2=None, op0=ALU.is_ge),
                      reads=[sl[:], m8[:]], writes=[t1[:]])
                yield
                kb.op("dve", lambda e: e.tensor_scalar(out=sc2[:], in0=sl[:], scalar1=-5000.0, scalar2=None, op0=ALU.is_gt),
                      reads=[sl[:]], writes=[sc2[:]])
                yield
                kb.op("dve", lambda e: e.tensor_tensor(out=t1[:], in0=t1[:], in1=sc2[:], op=ALU.mult),
                      reads=[t1[:], sc2[:]], writes=[t1[:]])
                yield
                kb.op("dve", lambda e: e.tensor_copy(out=negb[:, 0:NB], in_=t1[:]), reads=[t1[:]], writes=[negb[:]])
                yield
                kb.tr(tp[0:NB, :], negb[:, 0:NB], ident[:])
                yield
                kb.op("act", lambda e: e.copy(out=ntsb[0:NB, :], in_=tp[0:NB, :]), reads=[tp], writes=[ntsb[:]])
                yield
                kb.dma("pool", negT_d[g, 0:NB, t0:t0 + 128], ntsb[0:NB, :])
                yield

    def phase_attn(l):
        kb.push()
        ksT_sb = kb.sb("ksTs", [128, S], BF16)
        kwT_sb = kb.sb("kwTs", [128, S], BF16)
        kc_sb = kb.sb("kcs", [128, NC], BF16)
        vsa = kb.sb("vsa", [128, NT, 65], BF16)
        vwa = kb.sb("vwa", [128, NT, 65], BF16)
        vca = kb.sb("vca", [128, NCT, 65], BF16)
        hst = kb.sb("hst", [128, 512], F32)
        selH = [[kb.sb("sH", [128, 512], BF16) for r in range(5)] for h in range(4)]
        winS = [kb.sb("wS", [128, 512], BF16) for r in range(3)]
        winH1 = [kb.sb("wH", [128, 512], BF16) for h in range(4)]
        cmpH = [[kb.sb("cH", [128, 512], BF16) for u in range(5)] for h in range(4)]
        qsb = [kb.sb("qs", [128, 2, 512], BF16) for _ in range(2)]
        gtb = [kb.sb("gts", [128, 4, 24], F32) for _ in range(2)]
        mks = [kb.sb("mk", [128, 512], BF16) for _ in range(4)]
        pTs = [kb.sb("pTt", [128, 2, 512], BF16) for _ in range(4)]
        ynsa = kb.sb("ynsa", [128, 4, 256], F32)
        ynb = kb.sb("ynb", [128, 4, 256], BF16)
        ynT = kb.sb("ynT", [128, 2, 512], BF16)
        recs = [kb.sb("rec", [128, 8], F32) for _ in range(4)]
        tmps = [kb.sb("tmp", [128, 4, 64], F32) for _ in range(4)]
        Sps = [kb.ps("Sps", [128, 2, 512], F32) for _ in range(2)]
        accs = [kb.ps("acc", [128, 4, 65], F32) for _ in range(4)]
        tps = Sps[0][:].rearrange("p a n -> p (a n)").bitcast(BF16)[:, 0:512].rearrange("p (s q) -> p s q", q=128)
        for va in (vsa, vwa, vca):
            kb.op("pool", lambda e: e.memset(va[:, :, 64:65], 1.0), writes=[va[:]], accum=True)

        def load_h(dst, tens, offset, pstep):
            kb.dma("sp", hst[:], AP(tens, offset, [[pstep, 128], [1, 512]]))
            kb.op("dve", lambda e: e.tensor_copy(out=dst[:], in_=hst[:]), reads=[hst[:]], writes=[dst[:]])

        def load_q(g, qi, slot):
            q0 = qi * 512
            first = True
            for hp in range(2):
                for b2 in range(2):
                    kb.dma("sp", qsb[slot][b2 * 64:(b2 + 1) * 64, hp, :], qT[4 * g + 2 * hp + b2, :, q0:q0 + 512], accum=not first)
                    first = False
            kb.dma("sp", gtb[slot][:], gates_d[q0:q0 + 512, :].rearrange("(s p) c -> p s c", p=128))

        mkc = 0
        it = 0
        for g in range(2):
            for b2 in range(2):
                ps_ = slice(b2 * 64, (b2 + 1) * 64)
                kb.dma("sp", ksT_sb[ps_, :], ksT[g * 64:(g + 1) * 64, :], accum=(b2 > 0))
                kb.dma("sp", kwT_sb[ps_, :], kwT[g * 64:(g + 1) * 64, :], accum=(b2 > 0))
                kb.dma("sp", kc_sb[ps_, :], kcc_d[g], accum=(b2 > 0))
            kb.dma("sp", vsa[:, :, 0:64], vs_d[:, g * 64:(g + 1) * 64].rearrange("(kt p) d -> p kt d", p=128), accum=True)
            kb.dma("sp", vwa[:, :, 0:64], vw_d[:, g * 64:(g + 1) * 64].rearrange("(kt p) d -> p kt d", p=128), accum=True)
            kb.dma("sp", vca[:, :, 0:64], vcc_d[g].rearrange("(kt p) d -> p kt d", p=128), accum=True)
            for h in range(4):
                hg = 4 * g + h
                for r in range(-1, 4):
                    load_h(selH[h][r + 1], tbs_d.tensor, hg * NTS + OFFS - 127 - 128 * r, 1)
                load_h(winH1[h], tbw_d.tensor, hg * NTW + 1, 1)
                for u in range(5):
                    load_h(cmpH[h][u], tbs_d.tensor, hg * NTS + 512 * u, 16)
            for r in range(-4, -1):
                load_h(winS[r + 4], tbw_d.tensor, 4 * g * NTW - 127 - 128 * r, 1)
            load_q(g, 0, it % 2)
            for qi in range(NQ5):
                q0 = qi * 512
                kq = q0 // 128
                qs = qsb[it % 2]
                gts = gtb[it % 2]
                it += 1
                if qi + 1 < NQ5:
                    load_q(g, qi + 1, it % 2)
                for br in range(3):
                    if br == 0:
                        kts = [(nt, qi - 4 * nt) for nt in range(NCT) if qi - 4 * nt >= 0]
                    elif br == 1:
                        kts = [(kt, kt - kq) for kt in range(0, kq + 4)]
                    else:
                        kts = [(kt, kt - kq) for kt in range(max(0, kq - 4), kq + 4)]
                    units = []
                    for ki, (kt, r) in enumerate(kts):
                        for hp in range(2):
                            exs = []
                            for b2 in range(2):
                                h = 2 * hp + b2
                                if br == 0:
                                    exs.append(cmpH[h][r][:] if r <= 4 else None)
                                elif br == 1:
                                    exs.append(selH[h][r + 1][:] if r >= -1 else None)
                                elif r <= -2:
                                    exs.append(winS[r + 4][:])
                                elif r == -1:
                                    exs.append(winH1[h][:])
                                else:
                                    exs.append(selH[h][r + 1][:])
                            if br == 0:
                                units.append((hp, kc_sb, kt, exs, vca[:, kt, :], 0, 3, None, ki))
                            elif br == 1:
                                units.append((hp, ksT_sb, kt, exs, vsa[:, kt, :], max(r, 0), 3, kt, ki))
                            else:
                                units.append((hp, kwT_sb, kt, exs, vwa[:, kt, :], max(r, 0), min(r + 4, 3), None, ki))
                    lasts = {}
                    for ui, un in enumerate(units):
                        for s in range(un[5], un[6] + 1):
                            lasts[(un[0], s)] = ui
                    started = set()
                    n = len(units)
                    curmk = {}
                    for i in range(n + 2):
                        if i < n:
                            hp, ksb, kt, exs, va, smin, smax, mkt, ki = units[i]
                            if mkt is not None and hp == 0:
                                mk = mks[mkc % 4]
                                mkc += 1
                                curmk[ki] = mk
                                base = (g * 128 + 2 * mkt) * S + q0
                                kb.dma("sp", mk[0:64, :], AP(negT_d.tensor, base, [[0, 64], [1, 512]]))
                                kb.dma("sp", mk[64:128, :], AP(negT_d.tensor, base + S, [[0, 64], [1, 512]]), accum=True)
                            sp = Sps[i % 2]
                            pt = pTs[i % 4]
                            ksl = slice(kt * 128, (kt + 1) * 128)
                            cs = slice(smin * 128, (smax + 1) * 128)
                            ncs = (smax + 1 - smin) * 128
                            for b2 in range(2):
                                ps_ = slice(b2 * 64, (b2 + 1) * 64)
                                kb.mm(sp[:, b2, cs], ksb[ps_, ksl], qs[ps_, hp, cs], start=True, stop=(exs[b2] is None))
                            for b2 in range(2):
                                if exs[b2] is not None:
                                    kb.mm(sp[:, b2, cs], anti[:], exs[b2][:, cs], start=False, stop=True)
                            kb.op("act", lambda e: e.activation(out=pt[:, :, cs], in_=sp[:, :, cs], func=AF.Exp), reads=[sp[:]], writes=[pt[:]])
                            if mkt is not None:
                                mk = curmk[ki]
                                kb.op("dve", lambda e: e.tensor_tensor(out=pt[:, :, cs], in0=pt[:, :, cs],
                                                                       in1=mk[:, cs].unsqueeze(1).to_broadcast([128, 2, ncs]), op=ALU.mult),
                                      reads=[pt[:], mk[:]], writes=[pt[:]])
                        j = i - 2
                        if j >= 0:
                            hp, ksb, kt, exs, va, smin, smax, mkt, ki = units[j]
                            pt = pTs[j % 4]
                            for b2 in range(2):
                                h = 2 * hp + b2
                                a = accs[h]
                                for s in range(smin, smax + 1):
                                    st_ = h not in started
                                    started.add(h)
                                    kb.mm(a[:, s, :], pt[:, b2, s * 128:(s + 1) * 128], va, start=st_, stop=(lasts[(hp, s)] == j))
                    for h in range(4):
                        a = accs[h]
                        rec = recs[h]
                        tmp = tmps[h]
                        gcol = (4 * g + h) * 3 + br
                        kb.op("dve", lambda e: e.tensor_scalar_max(out=rec[:, 0:4], in0=a[:, :, 64], scalar1=1e-30),
                              reads=[a[:]], writes=[rec[:]])
                        kb.op("dve", lambda e: e.reciprocal(out=rec[:, 0:4], in_=rec[:, 0:4]), reads=[rec[:]], writes=[rec[:]])
                        kb.op("dve", lambda e: e.tensor_tensor(out=rec[:, 4:8], in0=rec[:, 0:4], in1=gts[:, :, gcol], op=ALU.mult),
                              reads=[rec[:], gts[:]], writes=[rec[:]])
                        dst = ynsa[:, :, h * 64:(h + 1) * 64]
                        rb = rec[:, 4:8].unsqueeze(2).to_broadcast([128, 4, 64])
                        if br == 0:
                            kb.op("dve", lambda e: e.tensor_tensor(out=dst, in0=a[:, :, 0:64], in1=rb, op=ALU.mult),
                                  reads=[a[:], rec[:]], writes=[ynsa[:]], accum=True)
                        else:
                            kb.op("dve", lambda e: e.tensor_tensor(out=tmp[:], in0=a[:, :, 0:64], in1=rb, op=ALU.mult),
                                  reads=[a[:], rec[:]], writes=[tmp[:]])
                            kb.op("pool", lambda e: e.tensor_tensor(out=dst, in0=dst, in1=tmp[:], op=ALU.add),
                                  reads=[ynsa[:], tmp[:]], writes=[ynsa[:]])
                kb.op("dve", lambda e: e.tensor_copy(out=ynb[:], in_=ynsa[:]), reads=[ynsa[:]], writes=[ynb[:]])
                for c in range(2):
                    for s in range(4):
                        kb.tr(tps[:, s, :], ynb[:, s, c * 128:(c + 1) * 128], ident[:])
                    kb.op("act", lambda e: e.copy(out=ynT[:, c, :], in_=tps.rearrange("p s q -> p (s q)")),
                          reads=[tps], writes=[ynT[:]], accum=(c > 0))
                kb.dma("pool", ymixT[g * 256:(g + 1) * 256, q0:q0 + 512].rearrange("(c p) t -> p c t", p=128), ynT[:])
        kb.pop()

    def phase_conv(l):
        kb.push()
        accs = [kb.sb("cacc", [128, S], F32) for _ in range(2)]
        cb = [kb.sb("cb", [128, 3], F32) for _ in range(2)]
        wT = kb.sb("wT", [128, 31], F32)
        dg = kb.sb("dg", [128, 31, 128], BF16)
        cps = [kb.ps("cps", [128, 512], F32) for _ in range(2)]
        for cc in range(2):
            kb.push()
            aT = kb.sb("aT", [128, S], BF16)
            gT = kb.sb("gT", [128, S], BF16)
            sg = kb.sb("sg", [128, S], BF16)
            hp = kb.sb("hp", [128, 32 + S], BF16)
            kb.dma("sp", aT[:], cuT[cc * 128:(cc + 1) * 128, :])
            kb.dma("sp", gT[:], cuT[256 + cc * 128:256 + (cc + 1) * 128, :])
            kb.dma("sp", wT[:], W["conv_w"][l][:, cc * 128:(cc + 1) * 128].rearrange("k c -> c k"), allow_slow_non_contiguous=True)
            for j, nm in enumerate(("conv_b", "conv_ln_g", "conv_ln_b")):
                kb.dma("sp", cb[cc][:, j:j + 1], W[nm][l][cc * 128:(cc + 1) * 128].rearrange("(c o) -> c o", o=1),
                       accum=(j > 0), allow_slow_non_contiguous=True)
            for k in range(31):
                kb.op("pool", lambda e: e.tensor_scalar(out=dg[:, k, :], in0=identf[:], scalar1=wT[:, k:k + 1], scalar2=None, op0=ALU.mult),
                      reads=[identf[:], wT[:]], writes=[dg[:]], accum=(k > 0))
            kb.op("pool", lambda e: e.memset(hp[:, 0:32], 0.0), writes=[hp[:]])
            kb.op("act", lambda e: e.activation(out=sg[:], in_=gT[:], func=AF.Sigmoid), reads=[gT[:]], writes=[sg[:]])
            kb.op("dve", lambda e: e.tensor_tensor(out=hp[:, 32:32 + S], in0=sg[:], in1=aT[:], op=ALU.mult),
                  reads=[sg[:], aT[:]], writes=[hp[:]], accum=True)
            acc = accs[cc]
            for j in range(NQ5):
                ps = cps[j % 2]
                kb.mmg(ps[:], [(dg[:, k, :], hp[:, j * 512 + 2 + k:j * 512 + 2 + k + 512]) for k in range(31)])
                kb.op("act", lambda e: e.activation(out=acc[:, j * 512:(j + 1) * 512], in_=ps[:], func=AF.Identity, bias=cb[cc][:, 0:1]),
                      reads=[ps[:], cb[cc][:]], writes=[acc[:]], accum=True)
            kb.pop()
        sq = [kb.sb("csq", [128, 512], F32) for _ in range(2)]
        mean = kb.sb("cmean", [128, 512], F32)
        var = kb.sb("cvar", [128, 512], F32)
        yv = kb.sb("cyv", [128, 512], F32)
        yo = kb.sb("cyo", [128, 2, 512], BF16)
        mps = kb.ps("mps", [128, 512], F32)
        sps = kb.ps("sps", [128, 512], F32)
        for j in range(NQ5):
            sl = slice(j * 512, (j + 1) * 512)
            kb.mmg(mps[:], [(onesf[:], accs[0][:, sl]), (onesf[:], accs[1][:, sl])])
            for cc in range(2):
                kb.op("act", lambda e: e.activation(out=sq[cc][:], in_=accs[cc][:, sl], func=AF.Square),
                      reads=[accs[cc][:]], writes=[sq[cc][:]])
            kb.mmg(sps[:], [(onesf[:], sq[0][:]), (onesf[:], sq[1][:])])
            kb.op("act", lambda e: e.mul(out=mean[:], in_=mps[:], mul=1.0 / 256), reads=[mps[:]], writes=[mean[:]])
            kb.op("dve", lambda e: e.tensor_tensor(out=var[:], in0=mean[:], in1=mean[:], op=ALU.mult),
                  reads=[mean[:]], writes=[var[:]])
            kb.op("dve", lambda e: e.scalar_tensor_tensor(out=var[:], in0=sps[:], scalar=1.0 / 256, in1=var[:],
                                                          op0=ALU.mult, op1=ALU.subtract),
                  reads=[sps[:], var[:]], writes=[var[:]])
            kb.op("act", lambda e: e.activation(out=var[:], in_=var[:], func=AF.Sqrt, bias=EPS), reads=[var[:], cst[:]], writes=[var[:]])
            kb.op("dve", lambda e: e.reciprocal(out=var[:], in_=var[:]), reads=[var[:]], writes=[var[:]])
            for cc in range(2):
                kb.op("dve", lambda e: e.tensor_tensor(out=yv[:], in0=accs[cc][:, sl], in1=mean[:], op=ALU.subtract),
                      reads=[accs[cc][:], mean[:]], writes=[yv[:]])
                kb.op("dve", lambda e: e.tensor_tensor(out=yv[:], in0=yv[:], in1=var[:], op=ALU.mult),
                      reads=[yv[:], var[:]], writes=[yv[:]])
                kb.op("act", lambda e: e.activation(out=yo[:, cc, :], in_=yv[:], func=AF.Silu, scale=cb[cc][:, 1:2], bias=cb[cc][:, 2:3]),
                      reads=[yv[:], cb[cc][:]], writes=[yo[:]], accum=(cc > 0))
            kb.dma("pool", ymixT[512:768, sl].rearrange("(c p) t -> p c t", p=128), yo[:])
        kb.pop()

    def phase_gla(l):
        gc = kb.sb("gc", [128, 3, 128], F32)
        bd = kb.sb("bd", [128, 256], F32)
        hm = kb.sb("hm", [128, 4], F32)
        waf = kb.sb("waf", [16, 128], F32)
        wab = kb.sb("wab", [16, 128], BF16)
        baf = kb.sb("baf", [1, 128], F32)
        bab = kb.sb("bab", [1, 128], BF16)
        one1 = kb.sb("one1", [1, 128], BF16)
        gng = kb.sb("gng", [128, 256], F32)
        gqs = kb.sb("gqs", [128, S], BF16)
        gks = kb.sb("gks", [128, S], BF16)
        gas = kb.sb("gas", [16, S], BF16)
        Sf = kb.sb("Sf", [128, 256], F32)
        Sb = kb.sb("Sb", [128, 256], BF16)
        gk_t_2 = [kb.sb("gk_t", [128, 128], BF16) for _ in range(2)]
        gv_t_2 = [kb.sb("gv_t", [128, 256], BF16) for _ in range(2)]
        gr_t_2 = [kb.sb("gr_t", [128, 256], BF16) for _ in range(2)]
        Lt_2 = [kb.sb("Lt", [128, 128], F32) for _ in range(2)]
        eb_2 = [kb.sb("eb", [128, 128], F32) for _ in range(2)]
        enb_2 = [kb.sb("enb", [128, 128], F32) for _ in range(2)]
        ekd_2 = [kb.sb("ekd", [128, 128], F32) for _ in range(2)]
        qf_2 = [kb.sb("qf", [128, 128], BF16) for _ in range(2)]
        qpad_2 = [kb.sb("qpad", [128, 2, 128], BF16) for _ in range(2)]
        ktf_2 = [kb.sb("ktf", [128, 128], F32) for _ in range(2)]
        ktm_2 = [kb.sb("ktm", [128, 4, 128], BF16) for _ in range(2)]
        kd_2 = [kb.sb("kd", [128, 128], BF16) for _ in range(2)]
        attnb_2 = [kb.sb("attnb", [128, 4, 128], BF16) for _ in range(2)]
        sqo_2 = [kb.sb("sqo", [128, 256], F32) for _ in range(2)]
        ss4_2 = [kb.sb("ss4", [128, 8], F32) for _ in range(2)]
        yt_2 = [kb.sb("yt", [128, 256], F32) for _ in range(2)]
        yb_2 = [kb.sb("yb", [128, 256], BF16) for _ in range(2)]
        yT_2 = [kb.sb("yT", [128, 2, 128], BF16) for _ in range(2)]
        gbank = kb.ps("gbank", [128, 512], F32)
        x_ps_2 = [gbank[:, 0:128]] * 2
        b_ps_2 = [gbank[:, 128:256]] * 2
        ku_ps_2 = [gbank[:, 256:384]] * 2
        tp2_2 = [gbank[:, 384:512].bitcast(BF16).rearrange("p (c q) -> p c q", q=128)] * 2
        at_ps_2 = [kb.ps("at_ps", [128, 4, 128], F32)] * 2
        o_ps = kb.ps("o_ps", [128, 256], F32)
        su_ps = kb.ps("su_ps", [128, 256], F32)
        kb.dma("sp", gc[:], C["gla_c"])
        yield
        kb.dma("sp", bd[:], C["gla_bd"])
        yield
        kb.dma("sp", hm[:], C["gla_hm"])
        yield
        kb.dma("sp", waf[:], W["gla_w_alpha"][l])
        yield
        kb.dma("sp", baf[:], W["gla_b_alpha"][l].rearrange("(o f) -> o f", o=1))
        yield
        kb.dma("sp", gng[:], AP(W["gla_norm_g"].tensor, l * 256, [[0, 128], [1, 256]]))
        yield
        kb.dma("sp", gqs[:], gqT)
        yield
        kb.dma("sp", gks[:], gkT)
        yield
        kb.dma("sp", gas[:], gaT)
        yield
        kb.op("dve", lambda e: e.tensor_copy(out=wab[:], in_=waf[:]), reads=[waf[:]], writes=[wab[:]])
        yield
        kb.op("dve", lambda e: e.tensor_copy(out=bab[:], in_=baf[:]), reads=[baf[:]], writes=[bab[:]])
        yield
        kb.op("pool", lambda e: e.memset(one1[:], 1.0), writes=[one1[:]])
        yield
        kb.op("pool", lambda e: e.memset(Sf[:], 0.0), writes=[Sf[:]])
        yield
        kb.op("pool", lambda e: e.memset(Sb[:], 0.0), writes=[Sb[:]])
        yield
        for qpad in qpad_2:
            kb.op("pool", lambda e: e.memset(qpad[:], 0.0), writes=[qpad[:]])
            yield
        ONE = cst[:, 1:2]
        for ti in range(NT):
            t0 = ti * 128
            ts = slice(t0, t0 + 128)
            (gk_t, gv_t, gr_t, Lt, eb, enb, ekd, qf, qpad, ktf, ktm, kd, attnb, sqo, ss4, yt, yb, yT, x_ps, b_ps, ku_ps, at_ps, tp2) = (gk_t_2[ti % 2], gv_t_2[ti % 2], gr_t_2[ti % 2], Lt_2[ti % 2], eb_2[ti % 2], enb_2[ti % 2], ekd_2[ti % 2], qf_2[ti % 2], qpad_2[ti % 2], ktf_2[ti % 2], ktm_2[ti % 2], kd_2[ti % 2], attnb_2[ti % 2], sqo_2[ti % 2], ss4_2[ti % 2], yt_2[ti % 2], yb_2[ti % 2], yT_2[ti % 2], x_ps_2[ti % 2], b_ps_2[ti % 2], ku_ps_2[ti % 2], at_ps_2[ti % 2], tp2_2[ti % 2])
            kb.dma("sp", gk_t[:], gk_d[ts, :])
            yield
            kb.dma("sp", gv_t[:], gv_d[ts, :])
            yield
            kb.dma("sp", gr_t[:], grs_d[ts, :])
            yield
            kb.mmg(x_ps, [(gas[:, ts], wab[:]), (one1[:], bab[:])])
            yield
            kb.op("act", lambda e: e.activation(out=Lt[:], in_=x_ps, func=AF.Exp, scale=-1.0), reads=[x_ps], writes=[Lt[:]])
            yield
            kb.op("act", lambda e: e.activation(out=Lt[:], in_=Lt[:], func=AF.Ln, bias=ONE), reads=[Lt[:], cst[:]], writes=[Lt[:]])
            yield
            kb.mmg(b_ps, [(Lt[:], gc[:, 0, :])])
            yield
            kb.mmg(ku_ps, [(gc[:, 1, :], Lt[:])])
            yield
            kb.op("act", lambda e: e.activation(out=eb[:], in_=b_ps, func=AF.Exp), reads=[b_ps], writes=[eb[:]])
            yield
            kb.op("act", lambda e: e.activation(out=enb[:], in_=b_ps, func=AF.Exp, scale=-1.0), reads=[b_ps], writes=[enb[:]])
            yield
            kb.op("act", lambda e: e.activation(out=ekd[:], in_=ku_ps, func=AF.Exp), reads=[ku_ps], writes=[ekd[:]])
            yield
            kb.op("dve", lambda e: e.scalar_tensor_tensor(out=qf[:], in0=gqs[:, ts], scalar=32.0 ** -0.5, in1=eb[:],
                                                          op0=ALU.mult, op1=ALU.mult),
                  reads=[gqs[:], eb[:]], writes=[qf[:]])
            yield
            kb.op("pool", lambda e: e.tensor_copy(out=qpad[:, 0, 0:64], in_=qf[:, 0:64]), reads=[qf[:]], writes=[qpad[:]])
            yield
            kb.op("pool", lambda e: e.tensor_copy(out=qpad[:, 1, 64:128], in_=qf[:, 64:128]), reads=[qf[:]], writes=[qpad[:]], accum=True)
            yield
            kb.op("dve", lambda e: e.tensor_tensor(out=ktf[:], in0=gks[:, ts], in1=enb[:], op=ALU.mult),
                  reads=[gks[:], enb[:]], writes=[ktf[:]])
            yield
            for h in range(4):
                kb.op("pool", lambda e: e.tensor_scalar(out=ktm[:, h, :], in0=ktf[:], scalar1=hm[:, h:h + 1], scalar2=None, op0=ALU.mult),
                      reads=[ktf[:], hm[:]], writes=[ktm[:]], accum=(h > 0))
                yield
            kb.op("dve", lambda e: e.tensor_tensor(out=kd[:], in0=gk_t[:], in1=ekd[:], op=ALU.mult),
                  reads=[gk_t[:], ekd[:]], writes=[kd[:]])
            yield
            for h in range(4):
                kb.mmg(at_ps[:, h, :], [(ktm[:, h, :], qf[:])])
                yield
            kb.op("dve", lambda e: e.tensor_tensor(out=attnb[:], in0=at_ps[:], in1=gc[:, 2, :].unsqueeze(1).to_broadcast([128, 4, 128]),
                                                   op=ALU.mult),
                  reads=[at_ps[:], gc[:]], writes=[attnb[:]])
            yield
            kb.mm(o_ps[:], qpad[:, 0, :], Sb[:], start=True, stop=False)
            yield
            for ch in range(2):
                cs = slice(ch * 64, (ch + 1) * 64)
                kb.mmg(su_ps[:], [(kd[cs, :], gv_t[cs, :])])
                yield
                dcol = eb[:, ch * 64 + 63:ch * 64 + 64]
                kb.op("dve", lambda e: e.scalar_tensor_tensor(out=Sf[:], in0=Sf[:], scalar=dcol, in1=su_ps[:], op0=ALU.mult, op1=ALU.add),
                      reads=[Sf[:], eb[:], su_ps[:]], writes=[Sf[:]])
                yield
                kb.op("pool", lambda e: e.tensor_tensor(out=Sb[:], in0=Sf[:], in1=bd[:], op=ALU.mult),
                      reads=[Sf[:], bd[:]], writes=[Sb[:]])
                yield
                if ch == 0:
                    kb.mm(o_ps[:], qpad[:, 1, :], Sb[:], start=False, stop=False)
                    yield
            for h in range(4):
                kb.mm(o_ps[:, h * 64:(h + 1) * 64], attnb[:, h, :], gv_t[:, h * 64:(h + 1) * 64], start=False, stop=(h == 3))
                yield
            kb.op("act", lambda e: e.activation(out=sqo[:], in_=o_ps[:], func=AF.Square), reads=[o_ps[:]], writes=[sqo[:]])
            yield
            kb.op("dve", lambda e: e.tensor_reduce(out=ss4[:, 0:4], in_=sqo[:].rearrange("p (h d) -> p h d", d=64), axis=AX.X, op=ALU.add),
                  reads=[sqo[:]], writes=[ss4[:]])
            yield
            kb.op("act", lambda e: e.activation(out=ss4[:, 4:8], in_=ss4[:, 0:4], func=AF.Sqrt, scale=1.0 / 64, bias=EPS),
                  reads=[ss4[:], cst[:]], writes=[ss4[:]])
            yield
            kb.op("dve", lambda e: e.reciprocal(out=ss4[:, 4:8], in_=ss4[:, 4:8]), reads=[ss4[:]], writes=[ss4[:]])
            yield
            kb.op("dve", lambda e: e.tensor_tensor(out=yt[:].rearrange("p (h d) -> p h d", d=64),
                                                   in0=o_ps[:].rearrange("p (h d) -> p h d", d=64),
                                                   in1=ss4[:, 4:8].unsqueeze(2).to_broadcast([128, 4, 64]), op=ALU.mult),
                  reads=[o_ps[:], ss4[:]], writes=[yt[:]])
            yield
            kb.op("pool", lambda e: e.tensor_tensor(out=yt[:], in0=yt[:], in1=gng[:], op=ALU.mult), reads=[yt[:], gng[:]], writes=[yt[:]])
            yield
            kb.op("dve", lambda e: e.tensor_tensor(out=yb[:], in0=yt[:], in1=gr_t[:], op=ALU.mult), reads=[yt[:], gr_t[:]], writes=[yb[:]])
            yield
            for c in range(2):
                kb.tr(tp2[:, c, :], yb[:, c * 128:(c + 1) * 128], ident[:])
                yield
            kb.op("act", lambda e: e.copy(out=yT[:], in_=tp2), reads=[tp2], writes=[yT[:]])
            yield
            kb.dma("pool", ymixT[768:1024, ts].rearrange("(c p) t -> p c t", p=128), yT[:])
            yield

    def phase_outproj(l, xsrc, xdst):
        kb.push()
        Wo = kb.sb("Wo", [128, 8, D], BF16)
        stg = [kb.sb("stg", [128, D], F32) for _ in range(2)]
        load_w(Wo, W["w_out"][l], 8, D, None, stg)
        yms = [kb.sb("ym", [128, 8, 512], BF16) for _ in range(2)]
        xts = [kb.sb("xt", [128, D], F32) for _ in range(2)]
        xos = [kb.sb("xo", [128, D], F32) for _ in range(2)]
        pss = [kb.ps("ps", [128, 512], F32) for _ in range(4)]
        kb.dma("sp", yms[0][:], ymixT[:, 0:512].rearrange("(c p) t -> p c t", p=128))
        for i in range(NQ5):
            t0 = i * 512
            ym = yms[i % 2]
            if i + 1 < NQ5:
                kb.dma("sp", yms[(i + 1) % 2][:], ymixT[:, t0 + 512:t0 + 1024].rearrange("(c p) t -> p c t", p=128))
            for s in range(4):
                xt, xo = xts[s % 2], xos[s % 2]
                rows = slice(t0 + s * 128, t0 + (s + 1) * 128)
                kb.dma("sp", xt[:], xsrc[rows, :])
                for half in range(2):
                    hs = slice(half * 512, (half + 1) * 512)
                    ps = pss[(s % 2) * 2 + half]
                    kb.mmg(ps[:], [(ym[:, c, s * 128:(s + 1) * 128], Wo[:, c, hs]) for c in range(8)])
                    kb.op("dve", lambda e: e.tensor_tensor(out=xo[:, hs], in0=xt[:, hs], in1=ps[:], op=ALU.add),
                          reads=[xt[:], ps[:]], writes=[xo[:]], accum=(half > 0))
                kb.dma("pool", xdst[rows, :], xo[:])
        kb.pop()

    act_d = dram("ffn_act", [DFF, S], BF16)

    def phase_ffn_a(l, xsrc):
        kb.push()
        Wg = kb.sb("Wg", [128, 8, DFF], BF16)
        Wu = kb.sb("Wu", [128, 8, DFF], BF16)
        stg = [kb.sb("stg", [128, DFF], F32) for _ in range(2)]
        gcol = kb.sb("gcol", [128, 8], F32)
        load_gcol(gcol, W["ffn_norm_g"][l])
        load_w(Wg, W["ffn_w_gate"][l], 8, DFF, gcol, stg)
        load_w(Wu, W["ffn_w_up"][l], 8, DFF, gcol, stg)
        nb = NormBufs()
        hTs = [kb.sb("hT", [128, 8, 512], BF16) for _ in range(2)]
        aTs = [kb.sb("aT", [128, 11, 512], BF16) for _ in range(2)]
        sg = [kb.sb("sg", [128, 512], BF16) for _ in range(2)]
        pg = [kb.ps("pg", [128, 512], F32) for _ in range(2)]
        pu = [kb.ps("pu", [128, 512], F32) for _ in range(2)]
        norm_pre(nb, xsrc, 0)
        norm_tr(nb, hTs[0])
        for i in range(NQ5):
            t0 = i * 512
            hT = hTs[i % 2]
            for f in range(22):
                if f == 2 and i + 1 < NQ5:
                    norm_pre(nb, xsrc, t0 + 512)
                if f == 14 and i + 1 < NQ5:
                    norm_tr(nb, hTs[(i + 1) % 2])
                fs = slice(f * 128, (f + 1) * 128)
                aT = aTs[f // 11]
                kb.mmg(pg[f % 2][:], [(Wg[:, c, fs], hT[:, c, :]) for c in range(8)])
                kb.mmg(pu[f % 2][:], [(Wu[:, c, fs], hT[:, c, :]) for c in range(8)])
                kb.op("act", lambda e: e.activation(out=sg[f % 2][:], in_=pg[f % 2][:], func=AF.Silu),
                      reads=[pg[f % 2][:]], writes=[sg[f % 2][:]])
                kb.op("dve", lambda e: e.tensor_tensor(out=aT[:, f % 11, :], in0=sg[f % 2][:], in1=pu[f % 2][:], op=ALU.mult),
                      reads=[sg[f % 2][:], pu[f % 2][:]], writes=[aT[:]], accum=True)
                if f % 11 == 10:
                    hf = f // 11
                    kb.dma("pool", act_d[hf * 1408:(hf + 1) * 1408, t0:t0 + 512].rearrange("(f p) t -> p f t", p=128), aT[:])
        kb.pop()

    def phase_ffn_b(l, xsrc, xdst):
        kb.push()
        Wd = kb.sb("Wd", [128, 22, D], BF16)
        stg = [kb.sb("stg", [128, D], F32) for _ in range(2)]
        load_w(Wd, W["ffn_w_down"][l], 22, D, None, stg)
        aTs = [kb.sb("aT", [128, 22, 512], BF16) for _ in range(2)]
        xts = [kb.sb("xt", [128, D], F32) for _ in range(2)]
        xos = [kb.sb("xo", [128, D], F32) for _ in range(2)]
        pss = [kb.ps("ps", [128, 512], F32) for _ in range(4)]
        kb.dma("sp", aTs[0][:], act_d[:, 0:512].rearrange("(f p) t -> p f t", p=128))
        for i in range(NQ5):
            t0 = i * 512
            aT = aTs[i % 2]
            if i + 1 < NQ5:
                kb.dma("sp", aTs[(i + 1) % 2][:], act_d[:, t0 + 512:t0 + 1024].rearrange("(f p) t -> p f t", p=128))
            for s in range(4):
                xt, xo = xts[s % 2], xos[s % 2]
                rows = slice(t0 + s * 128, t0 + (s + 1) * 128)
                kb.dma("sp", xt[:], xsrc[rows, :])
                for half in range(2):
                    hs = slice(half * 512, (half + 1) * 512)
                    ps = pss[(s % 2) * 2 + half]
                    kb.mmg(ps[:], [(aT[:, f, s * 128:(s + 1) * 128], Wd[:, f, hs]) for f in range(22)])
                    kb.op("dve", lambda e: e.tensor_tensor(out=xo[:, hs], in0=xt[:, hs], in1=ps[:], op=ALU.add),
                          reads=[xt[:], ps[:]], writes=[xo[:]], accum=(half > 0))
                kb.dma("pool", xdst[rows, :], xo[:])
        kb.pop()

    def phase_ple(l, xsrc, xdst):
        kb.push()
        Wpg = kb.sb("Wpg", [128, 8, D], BF16)
        Wpp = kb.sb("Wpp", [128, 2, D], BF16)
        stg = [kb.sb("stg", [128, D], F32) for _ in range(2)]
        gcol = kb.sb("gcol", [128, 8], F32)
        load_gcol(gcol, W["ple_norm_g"][l])
        load_w(Wpg, W["ple_w_gate"][l], 8, D, gcol, stg)
        load_w(Wpp, W["ple_w_proj"][l], 2, D, None, stg)
        nb = NormBufs()
        x2s = [kb.sb("x2", [128, D], F32) for _ in range(2)]
        hTs = [kb.sb("hT", [128, 8, 512], BF16) for _ in range(2)]
        pf = kb.sb("pf", [128, 256], F32)
        pb = kb.sb("pb", [128, 256], BF16)
        pTt = kb.sb("pTt", [128, 2, 128], BF16)
        sgm = kb.sb("sgm", [128, 512], F32)
        xos = [kb.sb("xo", [128, D], F32) for _ in range(2)]
        tp2 = kb.ps("tp2", [128, 2, 128], BF16)
        pg = kb.ps("pg", [128, 512], F32)
        pp = kb.ps("pp", [128, 512], F32)
        norm_pre(nb, xsrc, 0)
        norm_tr(nb, hTs[0])
        for i in range(NQ5):
            t0 = i * 512
            hT = hTs[i % 2]
            for s in range(4):
                if s == 1 and i + 1 < NQ5:
                    norm_pre(nb, xsrc, t0 + 512)
                if s == 3 and i + 1 < NQ5:
                    norm_tr(nb, hTs[(i + 1) % 2])
                x2, xo = x2s[s % 2], xos[s % 2]
                rows = slice(t0 + s * 128, t0 + (s + 1) * 128)
                kb.dma("sp", x2[:], xsrc[rows, :])
                kb.dma("sp", pf[:], p_in[l, rows, :])
                kb.op("dve", lambda e: e.tensor_copy(out=pb[:], in_=pf[:]), reads=[pf[:]], writes=[pb[:]])
                for c in range(2):
                    kb.tr(tp2[:, c, :], pb[:, c * 128:(c + 1) * 128], ident[:])
                kb.op("act", lambda e: e.copy(out=pTt[:], in_=tp2[:]), reads=[tp2[:]], writes=[pTt[:]])
                for half in range(2):
                    hs = slice(half * 512, (half + 1) * 512)
                    kb.mmg(pg[:], [(hT[:, c, s * 128:(s + 1) * 128], Wpg[:, c, hs]) for c in range(8)])
                    kb.mmg(pp[:], [(pTt[:, c, :], Wpp[:, c, hs]) for c in range(2)])
                    kb.op("act", lambda e: e.activation(out=sgm[:], in_=pg[:], func=AF.Sigmoid), reads=[pg[:]], writes=[sgm[:]])
                    kb.op("dve", lambda e: e.tensor_tensor(out=sgm[:], in0=sgm[:], in1=pp[:], op=ALU.mult),
                          reads=[sgm[:], pp[:]], writes=[sgm[:]])
                    kb.op("dve", lambda e: e.tensor_tensor(out=xo[:, hs], in0=x2[:, hs], in1=sgm[:], op=ALU.add),
                          reads=[x2[:], sgm[:]], writes=[xo[:]], accum=(half > 0))
                kb.dma("pool", xdst[rows, :], xo[:])
        kb.pop()

    def phase_final(xsrc):
        kb.push()
        gfin = kb.sb("gfin", [128, D], F32)
        kb.dma("sp", gfin[:], AP(W["final_norm_g"].tensor, 0, [[0, 128], [1, D]]))
        xt = kb.sb("xt", [128, D], F32)
        sq = kb.sb("sq", [128, D], F32)
        st4 = kb.sb("st4", [128, 4], F32)
        xo = kb.sb("xo", [128, D], F32)
        for ti in range(NT):
            rows = slice(ti * 128, (ti + 1) * 128)
            kb.dma("sp", xt[:], xsrc[rows, :])
            kb.op("act", lambda e: e.activation(out=sq[:], in_=xt[:], func=AF.Square, accum_out=st4[:, 0:1]),
                  reads=[xt[:]], writes=[sq[:], st4[:]])
            kb.op("act", lambda e: e.activation(out=st4[:, 1:2], in_=st4[:, 0:1], func=AF.Sqrt, scale=1.0 / D, bias=EPS),
                  reads=[st4[:], cst[:]], writes=[st4[:]])
            kb.op("dve", lambda e: e.reciprocal(out=st4[:, 2:3], in_=st4[:, 1:2]), reads=[st4[:]], writes=[st4[:]])
            kb.op("dve", lambda e: e.scalar_tensor_tensor(out=xo[:], in0=xt[:], scalar=st4[:, 2:3], in1=gfin[:], op0=ALU.mult, op1=ALU.mult),
                  reads=[xt[:], st4[:], gfin[:]], writes=[xo[:]])
            kb.dma("pool", out_d[rows, :], xo[:])
        kb.pop()

    if on("tables"):
        phase_tables()
    xcur = x_in
    bufs2 = [xa, xb]
    bi = 0
    for l in range(layers):
        if on("inproj"):
            phase_inproj(l, xcur)
        if on("compress"):
            phase_compress(l)
        if on("select") or on("gla"):
            kb.push()
            gens = []
            if on("select"):
                gens += [phase_select(l, [0]), phase_select(l, [1])]
            if on("gla"):
                gens.append(phase_gla(l))
            while gens:
                for gen in list(gens):
                    try:
                        next(gen)
                    except StopIteration:
                        gens.remove(gen)
            kb.pop()
        if on("attn"):
            phase_attn(l)
        if on("conv"):
            phase_conv(l)
        x1 = bufs2[bi]
        x2 = bufs2[1 - bi]
        if on("outproj"):
            phase_outproj(l, xcur, x1)
        if on("ffn"):
            phase_ffn_a(l, x1)
            phase_ffn_b(l, x1, x2)
        if on("ple"):
            phase_ple(l, x2, x1)
        xcur = x1
        bi = 1 - bi
    if final:
        phase_final(xcur)
    kb.pop()
    return nc, hc


_CACHE = {}


def kernel(**inputs):
    S = 8192
    if "nc" not in _CACHE:
        _CACHE["nc"] = build(S=S, layers=2)
    nc, hc = _CACHE["nc"]
    x = np.asarray(inputs["x"], dtype=np.float32)
    p = np.asarray(inputs["p"], dtype=np.float32)
    shared = {n: np.ascontiguousarray(np.asarray(inputs[n], dtype=np.float32)) for n in WNAMES}
    shared.update(hc)
    in_maps = []
    for b in range(8):
        m = dict(shared)
        m["x"] = np.ascontiguousarray(x[b])
        m["p"] = np.ascontiguousarray(p[:, b])
        in_maps.append(m)
    res = run_bass_kernel_spmd(nc, in_maps, core_ids=list(range(8)))
    return np.stack([np.asarray(r["out"], dtype=np.float32) for r in res.results], axis=0)
```

```python
import math
from contextlib import ExitStack
import numpy as np
import ml_dtypes
import concourse.bass as bass
import concourse.mybir as mybir
from concourse.bass_utils import run_bass_kernel_spmd
from concourse.ap import AP

F32, BF16 = mybir.dt.float32, mybir.dt.bfloat16
AF = mybir.ActivationFunctionType
ALU = mybir.AluOpType
AX = mybir.AxisListType

D = 1024
DIN = 2600
DFF = 2816
NEGV = -30000.0
OFFS = 2063
NTS = 4592
NTW = 1024


class Buf:
    __slots__ = ("w", "r", "sem", "cnt", "scope")

    def __init__(self):
        self.w = {}
        self.r = {}
        self.sem = None
        self.cnt = 0
        self.scope = 0


def _merge(d, toks):
    for k, (sem, v) in toks.items():
        if k not in d or d[k][1] < v:
            d[k] = (sem, v)


class KB:
    def __init__(self, nc):
        self.nc = nc
        self.eng = {"pe": nc.tensor, "act": nc.scalar, "dve": nc.vector, "pool": nc.gpsimd, "sp": nc.sync}
        self.esem = {e: nc.alloc_semaphore("s_" + e) for e in ("pe", "act", "dve", "pool")}
        self.ecnt = {e: 0 for e in self.esem}
        self.waited = {e: {} for e in self.eng}
        self.bufs = {}
        self.free_sems = {}
        self.stacks = []
        self.uid = 0
        self.nsem = 0

    def push(self):
        self.stacks.append((ExitStack(), []))

    def pop(self):
        self.barrier()
        st, names = self.stacks.pop()
        for n in names:
            b = self.bufs.pop(n, None)
            if b is not None and b.sem is not None:
                for q, sc in b.sem.items():
                    self.free_sems.setdefault(q, []).append(tuple(sc))
        st.close()

    def sb(self, name, shape, dt):
        self.uid += 1
        nm = f"{name}_{self.uid}"
        t = self.stacks[-1][0].enter_context(self.nc.sbuf_tensor(nm, list(shape), dt))
        self.stacks[-1][1].append(nm)
        return t

    def ps(self, name, shape, dt=F32):
        self.uid += 1
        nm = f"{name}_{self.uid}"
        t = self.stacks[-1][0].enter_context(self.nc.psum_tensor(nm, list(shape), dt))
        self.stacks[-1][1].append(nm)
        return t

    def buf(self, ap):
        n = ap.tensor.name
        b = self.bufs.get(n)
        if b is None:
            b = self.bufs[n] = Buf()
        return b

    def _need(self, reads, writes, accum):
        need = {}
        for a in reads:
            _merge(need, self.buf(a).w)
        for a in writes:
            b = self.buf(a)
            if not accum:
                _merge(need, b.w)
            _merge(need, b.r)
        return need

    def _wait(self, e, need):
        own = self.esem["pe"].num if e == "pe" else None
        wd = self.waited[e]
        for k, (sem, v) in need.items():
            if k == own:
                continue
            if wd.get(k, 0) < v:
                self.eng[e].wait_ge(sem, v)
                wd[k] = v

    def _update(self, tok, reads, writes, accum):
        for a in writes:
            b = self.buf(a)
            if accum:
                _merge(b.w, tok)
            else:
                b.w = dict(tok)
                b.r = {}
        for a in reads:
            _merge(self.buf(a).r, tok)

    def op(self, e, fn, reads=(), writes=(), accum=False):
        reads = [a for a in reads if isinstance(a, AP)]
        self._wait(e, self._need(reads, writes, accum))
        ins = fn(self.eng[e])
        self.ecnt[e] += 1
        sem = self.esem[e]
        ins.then_inc(sem, 1)
        self._update({sem.num: (sem, self.ecnt[e])}, reads, writes, accum)
        return ins

    def mmg(self, out, pairs, reads_extra=()):
        reads = []
        for l, r in pairs:
            reads += [l, r]
        self._wait("pe", self._need(reads, [out], False))
        n = len(pairs)
        for i, (l, r) in enumerate(pairs):
            ins = self.nc.tensor.matmul(out, lhsT=l, rhs=r, start=(i == 0), stop=(i == n - 1))
        self.ecnt["pe"] += 1
        sem = self.esem["pe"]
        ins.then_inc(sem, 1)
        self._update({sem.num: (sem, self.ecnt["pe"])}, reads, [out], False)

    def mm(self, out, l, r, start, stop, last=True):
        self._wait("pe", self._need([l, r], [out], not start))
        ins = self.nc.tensor.matmul(out, lhsT=l, rhs=r, start=start, stop=stop, skip_group_check=True)
        self.ecnt["pe"] += 1
        sem = self.esem["pe"]
        ins.then_inc(sem, 1)
        self._update({sem.num: (sem, self.ecnt["pe"])}, [l, r], [out], not start)

    def tr(self, out, in_, ident):
        k = in_.shape[0]
        return self.op("pe", lambda e: e.transpose(out=out, in_=in_, identity=ident[0:k, 0:k]),
                       reads=[in_, ident], writes=[out], accum=True)

    def dma(self, q, out, in_, accum=False, sbuf_side=None, **kw):
        sbap = sbuf_side
        if sbap is None:
            sbap = out if self.is_sb(out) else in_
        b = self.buf(sbap)
        if b.sem is None:
            b.sem = {}
        if q not in b.sem:
            fl = self.free_sems.setdefault(q, [])
            if fl:
                b.sem[q] = list(fl.pop())
            else:
                self.nsem += 1
                b.sem[q] = [self.nc.alloc_semaphore(f"sd{self.nsem}"), 0]
        acc = accum or (not self.is_sb(out))
        self._wait(q, self._need([in_], [out], acc))
        ins = self.eng[q].dma_start(out=out, in_=in_, **kw)
        sc = b.sem[q]
        sc[1] += 16
        ins.then_inc(sc[0], 16)
        self._update({sc[0].num: (sc[0], sc[1])}, [in_], [out], acc)

    def is_sb(self, ap):
        return ap.tensor.name in self.sbnames

    sbnames = None

    def barrier(self):
        toks = {}
        for e, sem in self.esem.items():
            toks[sem.num] = (sem, self.ecnt[e])
        for b in self.bufs.values():
            if b.sem is not None:
                for sc in b.sem.values():
                    if sc[1] > 0:
                        toks[sc[0].num] = (sc[0], sc[1])
        for e in self.eng:
            wd = self.waited[e]
            for k, (sem, v) in toks.items():
                if wd.get(k, 0) < v:
                    self.eng[e].wait_ge(sem, v)
                    wd[k] = v
        for b in self.bufs.values():
            b.w = {}
            b.r = {}


class SBNames:
    def __init__(self, dram_names):
        self.d = dram_names

    def __contains__(self, n):
        return n not in self.d


def _t5_bucket_np(n):
    n = np.maximum(n, 0)
    nf = np.maximum(n, 1).astype(np.float32)
    large = 16 + (np.log(nf / np.float32(16)) / np.float32(math.log(8.0)) * np.float32(16)).astype(np.int32)
    large = np.minimum(large, 31)
    return np.where(n < 16, n, large)


def host_consts(S):
    NB = S // 64
    c = {}
    dist = np.arange(NTS) - OFFS
    oh = np.zeros((33, NTS), np.float32)
    bk = _t5_bucket_np(dist)
    for i in range(NTS):
        if dist[i] < 0:
            oh[32, i] = 1.0
        else:
            oh[bk[i], i] = 1.0
    c["oh_sel"] = oh
    dist = np.arange(NTW)
    ohw = np.zeros((33, NTW), np.float32)
    bk = _t5_bucket_np(dist)
    for i in range(NTW):
        if dist[i] >= 512:
            ohw[32, i] = 1.0
        else:
            ohw[bk[i], i] = 1.0
    c["oh_win"] = ohw
    E = np.zeros((128, S), np.float32)
    for j in range(NB):
        E[j, 64 * j:64 * j + 64] = 1.0
    c["e_tab"] = E.astype(ml_dtypes.bfloat16)
    W = 2 * NB - 1
    fw = np.zeros((128, W), np.float32)
    for p in range(128):
        b = 1 if p >= 64 else 0
        for cc in range(W):
            rel = cc - (NB - 1)
            if rel > b:
                fw[p, cc] = -1e4
            elif rel == b or rel == b - 1:
                fw[p, cc] = 1e4
    c["fw_tab"] = fw
    g = np.zeros((128, 3, 128), np.float32)
    for s in range(128):
        for t in range(128):
            if s // 64 == t // 64:
                if s <= t:
                    g[s, 0, t] = -1.0 / 16
                    g[s, 2, t] = 1.0
                else:
                    pass
                if s > t:
                    pass
    for s in range(128):
        for t in range(128):
            if s // 64 == t // 64 and s > t:
                g[s, 1, t] = -1.0 / 16
    c["gla_c"] = g
    bd = np.zeros((128, 256), np.float32)
    for p in range(128):
        bd[p, (p // 32) * 64:(p // 32) * 64 + 64] = 1.0
    c["gla_bd"] = bd
    hm = np.zeros((128, 4), np.float32)
    for p in range(128):
        hm[p, p // 32] = 1.0
    c["gla_hm"] = hm
    return c


WNAMES = ["rel_bias", "mix_norm_g", "w_in", "w_out", "cmp_pos_k", "cmp_w1_k", "cmp_b1_k", "cmp_w2_k",
          "cmp_pos_v", "cmp_w1_v", "cmp_b1_v", "cmp_w2_v", "conv_w", "conv_b", "conv_ln_g", "conv_ln_b",
          "gla_w_alpha", "gla_b_alpha", "gla_norm_g", "ffn_norm_g", "ffn_w_gate", "ffn_w_up", "ffn_w_down",
          "ple_norm_g", "ple_w_gate", "ple_w_proj", "final_norm_g"]
WSHAPES = {"rel_bias": [32, 8], "mix_norm_g": [2, 1024], "w_in": [2, 1024, 2600], "w_out": [2, 1024, 1024],
           "cmp_pos_k": [2, 32, 64], "cmp_w1_k": [2, 2048, 256], "cmp_b1_k": [2, 256], "cmp_w2_k": [2, 256, 64],
           "cmp_pos_v": [2, 32, 64], "cmp_w1_v": [2, 2048, 256], "cmp_b1_v": [2, 256], "cmp_w2_v": [2, 256, 64],
           "conv_w": [2, 31, 256], "conv_b": [2, 256], "conv_ln_g": [2, 256], "conv_ln_b": [2, 256],
           "gla_w_alpha": [2, 16, 128], "gla_b_alpha": [2, 128], "gla_norm_g": [2, 256],
           "ffn_norm_g": [2, 1024], "ffn_w_gate": [2, 1024, 2816], "ffn_w_up": [2, 1024, 2816],
           "ffn_w_down": [2, 2816, 1024], "ple_norm_g": [2, 1024], "ple_w_gate": [2, 1024, 1024],
           "ple_w_proj": [2, 256, 1024], "final_norm_g": [1024]}


FM = [(0, 128, .125), (128, 128, .125), (256, 128, .125), (384, 128, .125), (512, 128, 1.), (640, 128, 1.),
      (768, 128, 1.), (1024, 128, 1.), (1304, 128, 1.), (1432, 128, 1.), (1560, 128, 1.), (1688, 128, 1.),
      (1816, 128, 1.), (1944, 128, 1.), (2328, 16, 1.)]


def build(S=8192, layers=2, dbg_in=(), dbg_out=(), phases=None, final=True):
    nc = bass.Bass("TRN2", target_bir_lowering=False)
    kb = KB(nc)
    dram_names = set()
    kb.sbnames = SBNames(dram_names)
    NT = S // 128
    NQ5 = S // 512
    NB = S // 64
    NC = S // 16
    NCT = NC // 128
    NCV = NC - 1
    PBW = NC + 8 * (NT - 1)

    def dram(name, shape, dt, kind=None):
        if kind is None:
            kind = "ExternalInput" if name in dbg_in else ("ExternalOutput" if name in dbg_out else "Internal")
        t = nc.dram_tensor(name, list(shape), dt, kind=kind)
        dram_names.add(t.name)
        return t.ap()

    def on(ph):
        return phases is None or ph in phases

    x_in = dram("x", [S, D], F32, "ExternalInput")
    p_in = dram("p", [2, S, 256], F32, "ExternalInput")
    W = {n: dram(n, WSHAPES[n], F32, "ExternalInput") for n in WNAMES}
    hc = host_consts(S)
    C = {n: dram(n, list(v.shape), BF16 if v.dtype != np.float32 else F32, "ExternalInput") for n, v in hc.items()}
    out_d = dram("out", [S, D], F32, "ExternalOutput")
    qT = dram("qT", [8, 64, S], BF16)
    kcT = dram("kcT", [2, 128, S], BF16)
    ksT = dram("ksT", [128, S], BF16)
    kwT = dram("kwT", [128, S], BF16)
    cuT = dram("cuT", [512, S], BF16)
    gqT = dram("gqT", [128, S], BF16)
    gkT = dram("gkT", [128, S], BF16)
    gaT = dram("gaT", [16, S], BF16)
    vs_d = dram("vs", [S, 128], BF16)
    vw_d = dram("vw", [S, 128], BF16)
    gk_d = dram("gk", [S, 128], BF16)
    gv_d = dram("gv", [S, 256], BF16)
    grs_d = dram("grs", [S, 256], BF16)
    gates_d = dram("gates", [S, 24], F32)
    kcc_d = dram("kcc", [2, 64, NC], BF16)
    vcc_d = dram("vcc", [2, NC, 64], BF16)
    negT_d = dram("negT", [2, 128, S], BF16)
    tbs_d = dram("tbs", [8, NTS], F32)
    tbw_d = dram("tbw", [8, NTW], F32)
    ymixT = dram("ymixT", [1024, S], BF16)
    xa = dram("xa", [S, D], F32)
    xb = dram("xb", [S, D], F32)

    kb.push()
    cst = kb.sb("cst", [128, 8], F32)
    ident = kb.sb("ident", [128, 128], BF16)
    anti = kb.sb("anti", [128, 128], BF16)
    identf = kb.sb("identf", [128, 128], F32)
    onesf = kb.sb("onesf", [128, 128], F32)
    for col, v in enumerate([1e-6, 1.0, 0.0, 1e-30]):
        kb.op("pool", lambda e: e.memset(cst[:, col:col + 1], v), writes=[cst[:]], accum=True)
    kb.op("pool", lambda e: e.memset(onesf[:], 1.0), writes=[onesf[:]])
    for t, base in ((ident, 0), (identf, 0), (anti, -127)):
        kb.op("pool", lambda e: e.memset(t[:], 0.0), writes=[t[:]])
        pat = [[-1, 128]] if base == 0 else [[1, 128]]
        kb.op("pool", lambda e: e.affine_select(out=t[:], in_=t[:], pattern=pat, compare_op=ALU.not_equal,
                                                fill=1.0, base=base, channel_multiplier=1),
              reads=[t[:]], writes=[t[:]])
    EPS = cst[:, 0:1]

    def load_w(dst, src_rows, nchunks, ncols, gcol=None, stg=None):
        for c in range(nchunks):
            st = stg[c % len(stg)]
            kb.dma("sp", st[:, 0:ncols], src_rows[c * 128:(c + 1) * 128, :])
            if gcol is not None:
                kb.op("dve", lambda e: e.tensor_scalar(out=dst[:, c, :], in0=st[:, 0:ncols], scalar1=gcol[:, c:c + 1],
                                                       scalar2=None, op0=ALU.mult),
                      reads=[st[:], gcol[:]], writes=[dst[:]], accum=True)
            else:
                kb.op("dve", lambda e: e.tensor_copy(out=dst[:, c, :], in_=st[:, 0:ncols]),
                      reads=[st[:]], writes=[dst[:]], accum=True)

    def load_gcol(gcol, src1d):
        kb.dma("sp", gcol[:], src1d.rearrange("(c p) -> p c", p=128), allow_slow_non_contiguous=True)

    class NormBufs:
        def __init__(self):
            self.xt = [kb.sb("nxt", [128, D], F32) for _ in range(2)]
            self.sq = kb.sb("nsq", [128, D], F32)
            self.st4 = [kb.sb("nst4", [128, 4], F32) for _ in range(4)]
            self.xs = [kb.sb("nxs", [128, D], BF16) for _ in range(4)]
            self.pT = [kb.ps("npT", [128, 8, 128], BF16) for _ in range(2)]

    def norm_pre(nb, xsrc, t0):
        for s in range(4):
            xt = nb.xt[s % 2]
            st4 = nb.st4[s]
            xs = nb.xs[s]
            kb.dma("sp", xt[:], xsrc[t0 + s * 128:t0 + (s + 1) * 128, :])
            kb.op("act", lambda e: e.activation(out=nb.sq[:], in_=xt[:], func=AF.Square, accum_out=st4[:, 0:1]),
                  reads=[xt[:]], writes=[nb.sq[:], st4[:]])
            kb.op("act", lambda e: e.activation(out=st4[:, 1:2], in_=st4[:, 0:1], func=AF.Sqrt, scale=1.0 / D, bias=EPS),
                  reads=[st4[:], cst[:]], writes=[st4[:]])
            kb.op("dve", lambda e: e.reciprocal(out=st4[:, 2:3], in_=st4[:, 1:2]), reads=[st4[:]], writes=[st4[:]])
            kb.op("dve", lambda e: e.tensor_scalar(out=xs[:], in0=xt[:], scalar1=st4[:, 2:3], scalar2=None, op0=ALU.mult),
                  reads=[xt[:], st4[:]], writes=[xs[:]])

    def norm_tr(nb, hT):
        for s in range(4):
            pT = nb.pT[s % 2]
            for c in range(8):
                kb.tr(pT[:, c, :], nb.xs[s][:, c * 128:(c + 1) * 128], ident[:])
            kb.op("dve", lambda e: e.tensor_copy(out=hT[:, :, s * 128:(s + 1) * 128], in_=pT[:]),
                  reads=[pT[:]], writes=[hT[:]], accum=(s > 0))

    def phase_inproj(l, xsrc):
        kb.push()
        Wb = kb.sb("Wb", [128, 8, DIN], BF16)
        stg = [kb.sb("stg", [128, DIN], F32) for _ in range(2)]
        gcol = kb.sb("gcol", [128, 8], F32)
        load_gcol(gcol, W["mix_norm_g"][l])
        load_w(Wb, W["w_in"][l], 8, DIN, gcol, stg)
        nb = NormBufs()
        hTs = [kb.sb("hT", [128, 8, 512], BF16) for _ in range(2)]
        zTs = [kb.sb("zT", [128, 15, 512], BF16) for _ in range(2)]
        toks = [kb.sb("tok", [128, 4, 896], BF16) for _ in range(2)]
        gtt = [kb.sb("gt", [128, 4, 24], F32) for _ in range(2)]
        psA = [kb.ps("psA", [128, 512], F32) for _ in range(2)]
        psB = [kb.ps("psB", [128, 512], F32) for _ in range(4)]
        qT2 = qT.rearrange("h d t -> (h d) t")
        norm_pre(nb, xsrc, 0)
        norm_tr(nb, hTs[0])
        for i in range(NQ5):
            t0 = i * 512
            hT, zT, tok, gt = hTs[i % 2], zTs[i % 2], toks[i % 2], gtt[i % 2]
            for ci, (c0, m, scale) in enumerate(FM):
                if ci == 2 and i + 1 < NQ5:
                    norm_pre(nb, xsrc, t0 + 512)
                if ci == 12 and i + 1 < NQ5:
                    norm_tr(nb, hTs[(i + 1) % 2])
                ps = psA[ci % 2]
                kb.mmg(ps[0:m, :], [(Wb[:, c, c0:c0 + m], hT[:, c, :]) for c in range(8)])
                if ci % 2 == 0:
                    kb.op("act", lambda e: e.mul(out=zT[0:m, ci, :], in_=ps[0:m, :], mul=scale),
                          reads=[ps[:]], writes=[zT[:]], accum=True)
                else:
                    kb.op("dve", lambda e: e.tensor_scalar(out=zT[0:m, ci, :], in0=ps[0:m, :], scalar1=scale, scalar2=None,
                                                           op0=ALU.mult),
                          reads=[ps[:]], writes=[zT[:]], accum=True)
            for s in range(4):
                hs = lambda c: hT[:, c, s * 128:(s + 1) * 128]
                kb.mmg(psB[0][:, 0:128], [(hs(c), Wb[:, c, 896:1024]) for c in range(8)])
                kb.mmg(psB[1][:, 0:152], [(hs(c), Wb[:, c, 1152:1304]) for c in range(8)])
                kb.mmg(psB[2][:, 0:384], [(hs(c), Wb[:, c, 1944:2328]) for c in range(8)])
                kb.mmg(psB[3][:, 0:256], [(hs(c), Wb[:, c, 2344:2600]) for c in range(8)])
                kb.op("dve", lambda e: e.tensor_copy(out=tok[:, s, 0:128], in_=psB[0][:, 0:128]),
                      reads=[psB[0][:]], writes=[tok[:]], accum=True)
                kb.op("dve", lambda e: e.tensor_copy(out=tok[:, s, 128:256], in_=psB[1][:, 0:128]),
                      reads=[psB[1][:]], writes=[tok[:]], accum=True)
                kb.op("act", lambda e: e.activation(out=gt[:, s, :], in_=psB[1][:, 128:152], func=AF.Sigmoid),
                      reads=[psB[1][:]], writes=[gt[:]], accum=True)
                kb.op("dve", lambda e: e.tensor_copy(out=tok[:, s, 256:640], in_=psB[2][:, 0:384]),
                      reads=[psB[2][:]], writes=[tok[:]], accum=True)
                kb.op("act", lambda e: e.activation(out=tok[:, s, 640:896], in_=psB[3][:, 0:256], func=AF.Silu),
                      reads=[psB[3][:]], writes=[tok[:]], accum=True)
            sl = slice(t0, t0 + 512)
            kb.dma("pool", qT2[:, sl].rearrange("(c p) t -> p c t", p=128), zT[:, 0:4, :])
            kb.dma("pool", kcT[:, :, sl].rearrange("k p t -> p k t"), zT[:, 4:6, :])
            kb.dma("pool", ksT[:, sl], zT[:, 6, :])
            kb.dma("pool", kwT[:, sl], zT[:, 7, :])
            kb.dma("pool", cuT[:, sl].rearrange("(c p) t -> p c t", p=128), zT[:, 8:12, :])
            kb.dma("pool", gqT[:, sl], zT[:, 12, :])
            kb.dma("pool", gkT[:, sl], zT[:, 13, :])
            kb.dma("pool", gaT[:, sl], zT[0:16, 14, :])
            tv = lambda d: d[sl, :].rearrange("(s p) c -> p s c", p=128)
            kb.dma("pool", tv(vs_d), tok[:, :, 0:128])
            kb.dma("pool", tv(vw_d), tok[:, :, 128:256])
            kb.dma("pool", tv(gk_d), tok[:, :, 256:384])
            kb.dma("pool", tv(gv_d), tok[:, :, 384:640])
            kb.dma("pool", tv(grs_d), tok[:, :, 640:896])
            kb.dma("pool", tv(gates_d), gt[:])
        kb.pop()

    def phase_compress(l):
        kb.push()
        w1b = kb.sb("w1b", [64, 32, 256], BF16)
        w1s = kb.sb("w1s", [64, 8, 256], F32)
        posf = kb.sb("posf", [64, 32], F32)
        posb = kb.sb("posb", [64, 32], BF16)
        b1c = kb.sb("b1c", [128, 2], F32)
        bias = kb.sb("bias", [128, 2], F32)
        w2s = kb.sb("w2s", [128, 2, 64], F32)
        w2b = kb.sb("w2b", [128, 2, 64], BF16)
        src = kb.sb("src", [64, S], BF16)
        hid = kb.sb("hid", [128, 2, NC], BF16)
        osb = kb.sb("osb", [128, max(NC, NCT * 64)], BF16)
        pw = kb.ps("pw", [128, 2], F32)
        ph = kb.ps("ph", [128, 512], F32)
        po = kb.ps("po", [128, 512], F32)
        kb.op("pool", lambda e: e.memset(hid[:, :, NCV:NC], 0.0), writes=[hid[:]])
        for kv, sfx in ((0, "k"), (1, "v")):
            w1 = W["cmp_w1_" + sfx][l].rearrange("(l d) n -> d l n", d=64)
            for q4 in range(4):
                kb.dma("sp", w1s[:], w1[:, q4 * 8:(q4 + 1) * 8, :])
                kb.op("dve", lambda e: e.tensor_copy(out=w1b[:, q4 * 8:(q4 + 1) * 8, :], in_=w1s[:]),
                      reads=[w1s[:]], writes=[w1b[:]], accum=(q4 > 0))
            kb.dma("sp", posf[:], W["cmp_pos_" + sfx][l].rearrange("l d -> d l"), allow_slow_non_contiguous=True)
            kb.op("dve", lambda e: e.tensor_copy(out=posb[:], in_=posf[:]), reads=[posf[:]], writes=[posb[:]])
            kb.dma("sp", b1c[:], W["cmp_b1_" + sfx][l].rearrange("(c p) -> p c", p=128), allow_slow_non_contiguous=True)
            kb.dma("sp", w2s[:], W["cmp_w2_" + sfx][l].rearrange("(c p) d -> p c d", p=128))
            kb.op("dve", lambda e: e.tensor_copy(out=w2b[:], in_=w2s[:]), reads=[w2s[:]], writes=[w2b[:]])
            for half in range(2):
                kb.mmg(pw[:, half:half + 1], [(w1b[:, li, half * 128:(half + 1) * 128], posb[:, li:li + 1]) for li in range(32)])
            kb.op("dve", lambda e: e.tensor_tensor(out=bias[:], in0=pw[:], in1=b1c[:], op=ALU.add),
                  reads=[pw[:], b1c[:]], writes=[bias[:]])
            for h in range(2):
                kb.dma("sp", src[:], kcT[kv, h * 64:(h + 1) * 64, :])
                sv = src[:].rearrange("d (n s) -> d n s", s=16)
                for half in range(2):
                    pairs = []
                    for li in range(32):
                        rhs = sv[:, 0:NCV, li] if li < 16 else sv[:, 1:NCV + 1, li - 16]
                        pairs.append((w1b[:, li, half * 128:(half + 1) * 128], rhs))
                    kb.mmg(ph[:, 0:NCV], pairs)
                    kb.op("act", lambda e: e.activation(out=hid[:, half, 0:NCV], in_=ph[:, 0:NCV], func=AF.Silu,
                                                        bias=bias[:, half:half + 1]),
                          reads=[ph[:], bias[:]], writes=[hid[:]], accum=(half > 0))
                if kv == 0:
                    kb.mmg(po[0:64, 0:NC], [(w2b[:, half, :], hid[:, half, :]) for half in range(2)])
                    kb.op("dve", lambda e: e.tensor_copy(out=osb[0:64, 0:NC], in_=po[0:64, 0:NC]),
                          reads=[po[:]], writes=[osb[:]])
                    kb.dma("pool", kcc_d[h], osb[0:64, 0:NC])
                else:
                    for nt in range(NCT):
                        kb.mmg(po[:, nt * 64:(nt + 1) * 64],
                               [(hid[:, half, nt * 128:(nt + 1) * 128], w2b[:, half, :]) for half in range(2)])
                    kb.op("dve", lambda e: e.tensor_copy(out=osb[:, 0:NCT * 64], in_=po[:, 0:NCT * 64]),
                          reads=[po[:]], writes=[osb[:]])
                    kb.dma("pool", vcc_d[h].rearrange("(nt p) d -> p nt d", p=128),
                           osb[:, 0:NCT * 64].rearrange("p (nt d) -> p nt d", d=64))
        kb.pop()

    def phase_tables():
        kb.push()
        rel = kb.sb("rel", [33, 8], F32)
        r31 = kb.sb("r31", [32, 8], F32)
        oh = kb.sb("oh", [33, 512], F32)
        tsb = kb.sb("tsb", [8, 512], F32)
        pt = kb.ps("pt", [8, 512], F32)
        kb.dma("sp", rel[0:32, :], W["rel_bias"])
        kb.dma("sp", r31[:], AP(W["rel_bias"].tensor, 31 * 8, [[0, 32], [1, 8]]))
        kb.op("dve", lambda e: e.tensor_tensor(out=rel[0:32, :], in0=rel[0:32, :], in1=r31[:], op=ALU.subtract),
              reads=[rel[:], r31[:]], writes=[rel[:]])
        kb.op("pool", lambda e: e.memset(rel[32:33, :], NEGV), writes=[rel[:]])
        for tab, dst, n in ((C["oh_sel"], tbs_d, NTS), (C["oh_win"], tbw_d, NTW)):
            for c0 in range(0, n, 512):
                w = min(512, n - c0)
                kb.dma("sp", oh[:, 0:w], tab[:, c0:c0 + w])
                kb.mmg(pt[:, 0:w], [(rel[:], oh[:, 0:w])])
                kb.op("dve", lambda e: e.tensor_copy(out=tsb[:, 0:w], in_=pt[:, 0:w]), reads=[pt[:]], writes=[tsb[:]])
                kb.dma("pool", dst[:, c0:c0 + w], tsb[:, 0:w])
        kb.pop()

    def phase_select(l, gs):
        kc_sb = kb.sb("kc", [64, NC], BF16)
        pbw = kb.sb("pbw", [128, 4, PBW], F32)
        pbwb = kb.sb("pbwb", [128, 4, PBW], BF16)
        ptmp = kb.sb("ptmp", [128, 17], F32)
        fw = kb.sb("fw", [128, 2 * NB - 1], F32)
        qt_2 = [kb.sb("qt", [64, 4, 128], BF16) for _ in range(2)]
        sc_2 = [kb.sb("sc", [128, 4, 512], F32) for _ in range(2)]
        rs_2 = [kb.sb("rs", [128, 8], F32) for _ in range(2)]
        ipad_2 = [kb.sb("ipad", [128, NC + 4], F32) for _ in range(2)]
        sl_2 = [kb.sb("sl", [128, NB], F32) for _ in range(2)]
        t1_2 = [kb.sb("t1", [128, NB], F32) for _ in range(2)]
        sc2_2 = [kb.sb("sc2", [128, NB], F32) for _ in range(2)]
        m8_2 = [kb.sb("m8", [128, 16], F32) for _ in range(2)]
        negb_2 = [kb.sb("negb", [128, 128], BF16) for _ in range(2)]
        ntsb_2 = [kb.sb("ntsb", [128, 128], BF16) for _ in range(2)]
        S_ps = kb.ps("S_ps", [128, 2, 512], F32)
        tp = S_ps[:].rearrange("p a n -> p (a n)").bitcast(BF16)[:, 0:128]
        kb.dma("sp", fw[:], C["fw_tab"])
        yield
        cpat = 8 * (NT - 1) - 9
        for g in gs:
            kb.dma("sp", kc_sb[:], kcc_d[g])
            yield
            kb.op("pool", lambda e: e.memset(pbw[:], 0.0), writes=[pbw[:]])
            yield
            kb.op("pool", lambda e: e.memset(pbw[:, :, NC:PBW], NEGV), writes=[pbw[:]])
            yield
            for h in range(4):
                hg = 4 * g + h
                kb.dma("sp", ptmp[:], AP(tbs_d.tensor, hg * NTS + OFFS - 143, [[1, 128], [16, 17]]),
                       allow_slow_non_contiguous=True)
                yield
                for k2 in range(17):
                    col = cpat + 16 - k2
                    kb.op("pool", lambda e: e.tensor_copy(out=pbw[:, h, col:col + 1], in_=ptmp[:, k2:k2 + 1]),
                          reads=[ptmp[:]], writes=[pbw[:]])
                    yield
            kb.op("pool", lambda e: e.tensor_copy(out=pbwb[:], in_=pbw[:]), reads=[pbw[:]], writes=[pbwb[:]])
            yield
            for ipad in ipad_2:
                kb.op("pool", lambda e: e.memset(ipad[:], 0.0), writes=[ipad[:]])
                yield
            for ti in range(NT):
                t0 = ti * 128
                qt, sc, rs, ipad, sl, t1, sc2, m8, negb, ntsb = (qt_2[ti % 2], sc_2[ti % 2], rs_2[ti % 2], ipad_2[ti % 2], sl_2[ti % 2],
                                                                 t1_2[ti % 2], sc2_2[ti % 2], m8_2[ti % 2], negb_2[ti % 2], ntsb_2[ti % 2])
                ncol = min(NC, 8 * ti + 8)
                st = 8 * (NT - 1) - 8 * ti
                kb.dma("sp", qt[:], qT[4 * g:4 * g + 4, :, t0:t0 + 128].rearrange("h d t -> d h t"))
                yield
                for hh in range(2):
                    for h in (2 * hh, 2 * hh + 1):
                        kb.mmg(S_ps[:, h % 2, 0:ncol], [(qt[:, h, :], kc_sb[:, 0:ncol]), (ident[:], pbwb[:, h, st:st + ncol])])
                        yield
                    for h in (2 * hh, 2 * hh + 1):
                        kb.op("act", lambda e: e.activation(out=sc[:, h, 0:ncol], in_=S_ps[:, h % 2, 0:ncol], func=AF.Exp, accum_out=rs[:, h:h + 1]),
                              reads=[S_ps[:]], writes=[sc[:], rs[:]], accum=(h > 0))
                        yield
                kb.op("dve", lambda e: e.tensor_scalar_max(out=rs[:, 0:4], in0=rs[:, 0:4], scalar1=1e-30),
                      reads=[rs[:]], writes=[rs[:]])
                yield
                kb.op("dve", lambda e: e.reciprocal(out=rs[:, 4:8], in_=rs[:, 0:4]), reads=[rs[:]], writes=[rs[:]])
                yield
                iw = ipad[:, 1:1 + ncol]
                kb.op("dve", lambda e: e.tensor_scalar(out=iw, in0=sc[:, 0, 0:ncol], scalar1=rs[:, 4:5], scalar2=None, op0=ALU.mult),
                      reads=[sc[:], rs[:]], writes=[ipad[:]])
                yield
                for h in range(1, 4):
                    kb.op("dve", lambda e: e.scalar_tensor_tensor(out=iw, in0=sc[:, h, 0:ncol], scalar=rs[:, 4 + h:5 + h], in1=iw,
                                                                  op0=ALU.mult, op1=ALU.add),
                          reads=[sc[:], rs[:], ipad[:]], writes=[ipad[:]])
                    yield
                iv = ipad[:, 0:4 * NB].rearrange("p (j f) -> p j f", f=4)
                iv4 = ipad[:, 4:4 + 4 * NB].rearrange("p (j f) -> p j f", f=4)
                kb.op("dve", lambda e: e.tensor_tensor(out=sl[:], in0=iv[:, :, 0], in1=iv4[:, :, 0], op=ALU.add),
                      reads=[ipad[:]], writes=[sl[:]])
                yield
                kb.op("dve", lambda e: e.tensor_tensor(out=t1[:], in0=iv[:, :, 1], in1=iv[:, :, 2], op=ALU.add),
                      reads=[ipad[:]], writes=[t1[:]])
                yield
                kb.op("dve", lambda e: e.tensor_tensor(out=t1[:], in0=t1[:], in1=iv[:, :, 3], op=ALU.add),
                      reads=[ipad[:], t1[:]], writes=[t1[:]])
                yield
                kb.op("dve", lambda e: e.scalar_tensor_tensor(out=sl[:], in0=t1[:], scalar=2.0, in1=sl[:], op0=ALU.mult, op1=ALU.add),
                      reads=[t1[:], sl[:]], writes=[sl[:]])
                yield
                off = NB - 1 - 2 * ti
                kb.op("dve", lambda e: e.tensor_tensor(out=sl[:], in0=sl[:], in1=fw[:, off:off + NB], op=ALU.add),
                      reads=[sl[:], fw[:]], writes=[sl[:]])
                yield
                kb.op("dve", lambda e: e.tensor_scalar_add(out=sl[:, 0:1], in0=sl[:, 0:1], scalar1=1e4),
                      reads=[sl[:]], writes=[sl[:]])
                yield
                kb.op("dve", lambda e: e.max(out=m8[:, 0:8], in_=sl[:]), reads=[sl[:]], writes=[m8[:]])
                yield
                kb.op("dve", lambda e: e.match_replace(out=sc2[:], in_to_replace=m8[:, 0:8], in_values=sl[:], imm_value=-3e4),
                      reads=[sl[:], m8[:]], writes=[sc2[:]])
                yield
                kb.op("dve", lambda e: e.max(out=m8[:, 8:16], in_=sc2[:]), reads=[sc2[:]], writes=[m8[:]])
                yield
                kb.op("dve", lambda e: e.tensor_scalar(out=t1[:], in0=sl[:], scalar1=m8[:, 15:16], scalar2=None, op0=ALU.is_ge),
                      reads=[sl[:], m8[:]], writes=[t1[:]])
                yield
                kb.op("dve", lambda e: e.tensor_scalar(out=sc2[:], in0=sl[:], scalar1=-5000.0, scalar2=None, op0=ALU.is_gt),
                      reads=[sl[:]], writes=[sc2[:]])
                yield
                kb.op("dve", lambda e: e.tensor_tensor(out=t1[:], in0=t1[:], in1=sc2[:], op=ALU.mult),
                      reads=[t1[:], sc2[:]], writes=[t1[:]])
                yield
                kb.op("dve", lambda e: e.tensor_copy(out=negb[:, 0:NB], in_=t1[:]), reads=[t1[:]], writes=[negb[:]])
                yield
                kb.tr(tp[0:NB, :], negb[:, 0:NB], ident[:])
                yield
                kb.op("act", lambda e: e.copy(out=ntsb[0:NB, :], in_=tp[0:NB, :]), reads=[tp], writes=[ntsb[:]])
                yield
                kb.dma("pool", negT_d[g, 0:NB, t0:t0 + 128], ntsb[0:NB, :])
                yield

    def phase_attn(l):
        kb.push()
        ksT_sb = kb.sb("ksTs", [128, S], BF16)
        kwT_sb = kb.sb("kwTs", [128, S], BF16)
        kc_sb = kb.sb("kcs", [128, NC], BF16)
        vsa = kb.sb("vsa", [128, NT, 65], BF16)
        vwa = kb.sb("vwa", [128, NT, 65], BF16)
        vca = kb.sb("vca", [128, NCT, 65], BF16)
        hst = kb.sb("hst", [128, 512], F32)
        selH = [[kb.sb("sH", [128, 512], BF16) for r in range(5)] for h in range(4)]
        winS = [kb.sb("wS", [128, 512], BF16) for r in range(3)]
        winH1 = [kb.sb("wH", [128, 512], BF16) for h in range(4)]
        cmpH = [[kb.sb("cH", [128, 512], BF16) for u in range(5)] for h in range(4)]
        qsb = [kb.sb("qs", [128, 2, 512], BF16) for _ in range(2)]
        gtb = [kb.sb("gts", [128, 4, 24], F32) for _ in range(2)]
        mks = [kb.sb("mk", [128, 512], BF16) for _ in range(4)]
        pTs = [kb.sb("pTt", [128, 2, 512], BF16) for _ in range(4)]
        ynsa = kb.sb("ynsa", [128, 4, 256], F32)
        ynb = kb.sb("ynb", [128, 4, 256], BF16)
        ynT = kb.sb("ynT", [128, 2, 512], BF16)
        recs = [kb.sb("rec", [128, 8], F32) for _ in range(4)]
        tmps = [kb.sb("tmp", [128, 4, 64], F32) for _ in range(4)]
        Sps = [kb.ps("Sps", [128, 2, 512], F32) for _ in range(2)]
        accs = [kb.ps("acc", [128, 4, 65], F32) for _ in range(4)]
        tps = Sps[0][:].rearrange("p a n -> p (a n)").bitcast(BF16)[:, 0:512].rearrange("p (s q) -> p s q", q=128)
        for va in (vsa, vwa, vca):
            kb.op("pool", lambda e: e.memset(va[:, :, 64:65], 1.0), writes=[va[:]], accum=True)

        def load_h(dst, tens, offset, pstep):
            kb.dma("sp", hst[:], AP(tens, offset, [[pstep, 128], [1, 512]]))
            kb.op("dve", lambda e: e.tensor_copy(out=dst[:], in_=hst[:]), reads=[hst[:]], writes=[dst[:]])

        def load_q(g, qi, slot):
            q0 = qi * 512
            first = True
            for hp in range(2):
                for b2 in range(2):
                    kb.dma("sp", qsb[slot][b2 * 64:(b2 + 1) * 64, hp, :], qT[4 * g + 2 * hp + b2, :, q0:q0 + 512], accum=not first)
                    first = False
            kb.dma("sp", gtb[slot][:], gates_d[q0:q0 + 512, :].rearrange("(s p) c -> p s c", p=128))

        mkc = 0
        it = 0
        for g in range(2):
            for b2 in range(2):
                ps_ = slice(b2 * 64, (b2 + 1) * 64)
                kb.dma("sp", ksT_sb[ps_, :], ksT[g * 64:(g + 1) * 64, :], accum=(b2 > 0))
                kb.dma("sp", kwT_sb[ps_, :], kwT[g * 64:(g + 1) * 64, :], accum=(b2 > 0))
                kb.dma("sp", kc_sb[ps_, :], kcc_d[g], accum=(b2 > 0))
            kb.dma("sp", vsa[:, :, 0:64], vs_d[:, g * 64:(g + 1) * 64].rearrange("(kt p) d -> p kt d", p=128), accum=True)
            kb.dma("sp", vwa[:, :, 0:64], vw_d[:, g * 64:(g + 1) * 64].rearrange("(kt p) d -> p kt d", p=128), accum=True)
            kb.dma("sp", vca[:, :, 0:64], vcc_d[g].rearrange("(kt p) d -> p kt d", p=128), accum=True)
            for h in range(4):
                hg = 4 * g + h
                for r in range(-1, 4):
                    load_h(selH[h][r + 1], tbs_d.tensor, hg * NTS + OFFS - 127 - 128 * r, 1)
                load_h(winH1[h], tbw_d.tensor, hg * NTW + 1, 1)
                for u in range(5):
                    load_h(cmpH[h][u], tbs_d.tensor, hg * NTS + 512 * u, 16)
            for r in range(-4, -1):
                load_h(winS[r + 4], tbw_d.tensor, 4 * g * NTW - 127 - 128 * r, 1)
            load_q(g, 0, it % 2)
            for qi in range(NQ5):
                q0 = qi * 512
                kq = q0 // 128
                qs = qsb[it % 2]
                gts = gtb[it % 2]
                it += 1
                if qi + 1 < NQ5:
                    load_q(g, qi + 1, it % 2)
                for br in range(3):
                    if br == 0:
                        kts = [(nt, qi - 4 * nt) for nt in range(NCT) if qi - 4 * nt >= 0]
                    elif br == 1:
                        kts = [(kt, kt - kq) for kt in range(0, kq + 4)]
                    else:
                        kts = [(kt, kt - kq) for kt in range(max(0, kq - 4), kq + 4)]
                    units = []
                    for ki, (kt, r) in enumerate(kts):
                        for hp in range(2):
                            exs = []
                            for b2 in range(2):
                                h = 2 * hp + b2
                                if br == 0:
                                    exs.append(cmpH[h][r][:] if r <= 4 else None)
                                elif br == 1:
                                    exs.append(selH[h][r + 1][:] if r >= -1 else None)
                                elif r <= -2:
                                    exs.append(winS[r + 4][:])
                                elif r == -1:
                                    exs.append(winH1[h][:])
                                else:
                                    exs.append(selH[h][r + 1][:])
                            if br == 0:
                                units.append((hp, kc_sb, kt, exs, vca[:, kt, :], 0, 3, None, ki))
                            elif br == 1:
                                units.append((hp, ksT_sb, kt, exs, vsa[:, kt, :], max(r, 0), 3, kt, ki))
                            else:
                                units.append((hp, kwT_sb, kt, exs, vwa[:, kt, :], max(r, 0), min(r + 4, 3), None, ki))
                    lasts = {}
                    for ui, un in enumerate(units):
                        for s in range(un[5], un[6] + 1):
                            lasts[(un[0], s)] = ui
                    started = set()
                    n = len(units)
                    curmk = {}
                    for i in range(n + 2):
                        if i < n:
                            hp, ksb, kt, exs, va, smin, smax, mkt, ki = units[i]
                            if mkt is not None and hp == 0:
                                mk = mks[mkc % 4]
                                mkc += 1
                                curmk[ki] = mk
                                base = (g * 128 + 2 * mkt) * S + q0
                                kb.dma("sp", mk[0:64, :], AP(negT_d.tensor, base, [[0, 64], [1, 512]]))
                                kb.dma("sp", mk[64:128, :], AP(negT_d.tensor, base + S, [[0, 64], [1, 512]]), accum=True)
                            sp = Sps[i % 2]
                            pt = pTs[i % 4]
                            ksl = slice(kt * 128, (kt + 1) * 128)
                            cs = slice(smin * 128, (smax + 1) * 128)
                            ncs = (smax + 1 - smin) * 128
                            for b2 in range(2):
                                ps_ = slice(b2 * 64, (b2 + 1) * 64)
                                kb.mm(sp[:, b2, cs], ksb[ps_, ksl], qs[ps_, hp, cs], start=True, stop=(exs[b2] is None))
                            for b2 in range(2):
                                if exs[b2] is not None:
                                    kb.mm(sp[:, b2, cs], anti[:], exs[b2][:, cs], start=False, stop=True)
                            kb.op("act", lambda e: e.activation(out=pt[:, :, cs], in_=sp[:, :, cs], func=AF.Exp), reads=[sp[:]], writes=[pt[:]])
                            if mkt is not None:
                                mk = curmk[ki]
                                kb.op("dve", lambda e: e.tensor_tensor(out=pt[:, :, cs], in0=pt[:, :, cs],
                                                                       in1=mk[:, cs].unsqueeze(1).to_broadcast([128, 2, ncs]), op=ALU.mult),
                                      reads=[pt[:], mk[:]], writes=[pt[:]])
                        j = i - 2
                        if j >= 0:
                            hp, ksb, kt, exs, va, smin, smax, mkt, ki = units[j]
                            pt = pTs[j % 4]
                            for b2 in range(2):
                                h = 2 * hp + b2
                                a = accs[h]
                                for s in range(smin, smax + 1):
                                    st_ = h not in started
                                    started.add(h)
                                    kb.mm(a[:, s, :], pt[:, b2, s * 128:(s + 1) * 128], va, start=st_, stop=(lasts[(hp, s)] == j))
                    for h in range(4):
                        a = accs[h]
                        rec = recs[h]
                        tmp = tmps[h]
                        gcol = (4 * g + h) * 3 + br
                        kb.op("dve", lambda e: e.tensor_scalar_max(out=rec[:, 0:4], in0=a[:, :, 64], scalar1=1e-30),
                              reads=[a[:]], writes=[rec[:]])
                        kb.op("dve", lambda e: e.reciprocal(out=rec[:, 0:4], in_=rec[:, 0:4]), reads=[rec[:]], writes=[rec[:]])
                        kb.op("dve", lambda e: e.tensor_tensor(out=rec[:, 4:8], in0=rec[:, 0:4], in1=gts[:, :, gcol], op=ALU.mult),
                              reads=[rec[:], gts[:]], writes=[rec[:]])
                        dst = ynsa[:, :, h * 64:(h + 1) * 64]
                        rb = rec[:, 4:8].unsqueeze(2).to_broadcast([128, 4, 64])
                        if br == 0:
                            kb.op("dve", lambda e: e.tensor_tensor(out=dst, in0=a[:, :, 0:64], in1=rb, op=ALU.mult),
                                  reads=[a[:], rec[:]], writes=[ynsa[:]], accum=True)
                        else:
                            kb.op("dve", lambda e: e.tensor_tensor(out=tmp[:], in0=a[:, :, 0:64], in1=rb, op=ALU.mult),
                                  reads=[a[:], rec[:]], writes=[tmp[:]])
                            kb.op("pool", lambda e: e.tensor_tensor(out=dst, in0=dst, in1=tmp[:], op=ALU.add),
                                  reads=[ynsa[:], tmp[:]], writes=[ynsa[:]])
                kb.op("dve", lambda e: e.tensor_copy(out=ynb[:], in_=ynsa[:]), reads=[ynsa[:]], writes=[ynb[:]])
                for c in range(2):
                    for s in range(4):
                        kb.tr(tps[:, s, :], ynb[:, s, c * 128:(c + 1) * 128], ident[:])
                    kb.op("act", lambda e: e.copy(out=ynT[:, c, :], in_=tps.rearrange("p s q -> p (s q)")),
                          reads=[tps], writes=[ynT[:]], accum=(c > 0))
                kb.dma("pool", ymixT[g * 256:(g + 1) * 256, q0:q0 + 512].rearrange("(c p) t -> p c t", p=128), ynT[:])
        kb.pop()

    def phase_conv(l):
        kb.push()
        accs = [kb.sb("cacc", [128, S], F32) for _ in range(2)]
        cb = [kb.sb("cb", [128, 3], F32) for _ in range(2)]
        wT = kb.sb("wT", [128, 31], F32)
        dg = kb.sb("dg", [128, 31, 128], BF16)
        cps = [kb.ps("cps", [128, 512], F32) for _ in range(2)]
        for cc in range(2):
            kb.push()
            aT = kb.sb("aT", [128, S], BF16)
            gT = kb.sb("gT", [128, S], BF16)
            sg = kb.sb("sg", [128, S], BF16)
            hp = kb.sb("hp", [128, 32 + S], BF16)
            kb.dma("sp", aT[:], cuT[cc * 128:(cc + 1) * 128, :])
            kb.dma("sp", gT[:], cuT[256 + cc * 128:256 + (cc + 1) * 128, :])
            kb.dma("sp", wT[:], W["conv_w"][l][:, cc * 128:(cc + 1) * 128].rearrange("k c -> c k"), allow_slow_non_contiguous=True)
            for j, nm in enumerate(("conv_b", "conv_ln_g", "conv_ln_b")):
                kb.dma("sp", cb[cc][:, j:j + 1], W[nm][l][cc * 128:(cc + 1) * 128].rearrange("(c o) -> c o", o=1),
                       accum=(j > 0), allow_slow_non_contiguous=True)
            for k in range(31):
                kb.op("pool", lambda e: e.tensor_scalar(out=dg[:, k, :], in0=identf[:], scalar1=wT[:, k:k + 1], scalar2=None, op0=ALU.mult),
                      reads=[identf[:], wT[:]], writes=[dg[:]], accum=(k > 0))
            kb.op("pool", lambda e: e.memset(hp[:, 0:32], 0.0), writes=[hp[:]])
            kb.op("act", lambda e: e.activation(out=sg[:], in_=gT[:], func=AF.Sigmoid), reads=[gT[:]], writes=[sg[:]])
            kb.op("dve", lambda e: e.tensor_tensor(out=hp[:, 32:32 + S], in0=sg[:], in1=aT[:], op=ALU.mult),
                  reads=[sg[:], aT[:]], writes=[hp[:]], accum=True)
            acc = accs[cc]
            for j in range(NQ5):
                ps = cps[j % 2]
                kb.mmg(ps[:], [(dg[:, k, :], hp[:, j * 512 + 2 + k:j * 512 + 2 + k + 512]) for k in range(31)])
                kb.op("act", lambda e: e.activation(out=acc[:, j * 512:(j + 1) * 512], in_=ps[:], func=AF.Identity, bias=cb[cc][:, 0:1]),
                      reads=[ps[:], cb[cc][:]], writes=[acc[:]], accum=True)
            kb.pop()
        sq = [kb.sb("csq", [128, 512], F32) for _ in range(2)]
        mean = kb.sb("cmean", [128, 512], F32)
        var = kb.sb("cvar", [128, 512], F32)
        yv = kb.sb("cyv", [128, 512], F32)
        yo = kb.sb("cyo", [128, 2, 512], BF16)
        mps = kb.ps("mps", [128, 512], F32)
        sps = kb.ps("sps", [128, 512], F32)
        for j in range(NQ5):
            sl = slice(j * 512, (j + 1) * 512)
            kb.mmg(mps[:], [(onesf[:], accs[0][:, sl]), (onesf[:], accs[1][:, sl])])
            for cc in range(2):
                kb.op("act", lambda e: e.activation(out=sq[cc][:], in_=accs[cc][:, sl], func=AF.Square),
                      reads=[accs[cc][:]], writes=[sq[cc][:]])
            kb.mmg(sps[:], [(onesf[:], sq[0][:]), (onesf[:], sq[1][:])])
            kb.op("act", lambda e: e.mul(out=mean[:], in_=mps[:], mul=1.0 / 256), reads=[mps[:]], writes=[mean[:]])
            kb.op("dve", lambda e: e.tensor_tensor(out=var[:], in0=mean[:], in1=mean[:], op=ALU.mult),
                  reads=[mean[:]], writes=[var[:]])
            kb.op("dve", lambda e: e.scalar_tensor_tensor(out=var[:], in0=sps[:], scalar=1.0 / 256, in1=var[:],
                                                          op0=ALU.mult, op1=ALU.subtract),
                  reads=[sps[:], var[:]], writes=[var[:]])
            kb.op("act", lambda e: e.activation(out=var[:], in_=var[:], func=AF.Sqrt, bias=EPS), reads=[var[:], cst[:]], writes=[var[:]])
            kb.op("dve", lambda e: e.reciprocal(out=var[:], in_=var[:]), reads=[var[:]], writes=[var[:]])
            for cc in range(2):
                kb.op("dve", lambda e: e.tensor_tensor(out=yv[:], in0=accs[cc][:, sl], in1=mean[:], op=ALU.subtract),
                      reads=[accs[cc][:], mean[:]], writes=[yv[:]])
                kb.op("dve", lambda e: e.tensor_tensor(out=yv[:], in0=yv[:], in1=var[:], op=ALU.mult),
                      reads=[yv[:], var[:]], writes=[yv[:]])
                kb.op("act", lambda e: e.activation(out=yo[:, cc, :], in_=yv[:], func=AF.Silu, scale=cb[cc][:, 1:2], bias=cb[cc][:, 2:3]),
                      reads=[yv[:], cb[cc][:]], writes=[yo[:]], accum=(cc > 0))
            kb.dma("pool", ymixT[512:768, sl].rearrange("(c p) t -> p c t", p=128), yo[:])
        kb.pop()

    def phase_gla(l):
        gc = kb.sb("gc", [128, 3, 128], F32)
        bd = kb.sb("bd", [128, 256], F32)
        hm = kb.sb("hm", [128, 4], F32)
        waf = kb.sb("waf", [16, 128], F32)
        wab = kb.sb("wab", [16, 128], BF16)
        baf = kb.sb("baf", [1, 128], F32)
        bab = kb.sb("bab", [1, 128], BF16)
        one1 = kb.sb("one1", [1, 128], BF16)
        gng = kb.sb("gng", [128, 256], F32)
        gqs = kb.sb("gqs", [128, S], BF16)
        gks = kb.sb("gks", [128, S], BF16)
        gas = kb.sb("gas", [16, S], BF16)
        Sf = kb.sb("Sf", [128, 256], F32)
        Sb = kb.sb("Sb", [128, 256], BF16)
        gk_t_2 = [kb.sb("gk_t", [128, 128], BF16) for _ in range(2)]
        gv_t_2 = [kb.sb("gv_t", [128, 256], BF16) for _ in range(2)]
        gr_t_2 = [kb.sb("gr_t", [128, 256], BF16) for _ in range(2)]
        Lt_2 = [kb.sb("Lt", [128, 128], F32) for _ in range(2)]
        eb_2 = [kb.sb("eb", [128, 128], F32) for _ in range(2)]
        enb_2 = [kb.sb("enb", [128, 128], F32) for _ in range(2)]
        ekd_2 = [kb.sb("ekd", [128, 128], F32) for _ in range(2)]
        qf_2 = [kb.sb("qf", [128, 128], BF16) for _ in range(2)]
        qpad_2 = [kb.sb("qpad", [128, 2, 128], BF16) for _ in range(2)]
        ktf_2 = [kb.sb("ktf", [128, 128], F32) for _ in range(2)]
        ktm_2 = [kb.sb("ktm", [128, 4, 128], BF16) for _ in range(2)]
        kd_2 = [kb.sb("kd", [128, 128], BF16) for _ in range(2)]
        attnb_2 = [kb.sb("attnb", [128, 4, 128], BF16) for _ in range(2)]
        sqo_2 = [kb.sb("sqo", [128, 256], F32) for _ in range(2)]
        ss4_2 = [kb.sb("ss4", [128, 8], F32) for _ in range(2)]
        yt_2 = [kb.sb("yt", [128, 256], F32) for _ in range(2)]
        yb_2 = [kb.sb("yb", [128, 256], BF16) for _ in range(2)]
        yT_2 = [kb.sb("yT", [128, 2, 128], BF16) for _ in range(2)]
        gbank = kb.ps("gbank", [128, 512], F32)
        x_ps_2 = [gbank[:, 0:128]] * 2
        b_ps_2 = [gbank[:, 128:256]] * 2
        ku_ps_2 = [gbank[:, 256:384]] * 2
        at_ps_2 = [kb.ps("at_ps", [128, 4, 128], F32)] * 2
        o_ps = kb.ps("o_ps", [128, 256], F32)
        subank = kb.ps("subank", [128, 512], F32)
        su_ps = subank[:, 0:256]
        tp2_2 = [subank[:, 256:384].bitcast(BF16).rearrange("p (c q) -> p c q", q=128)] * 2
        kb.dma("sp", gc[:], C["gla_c"])
        yield
        kb.dma("sp", bd[:], C["gla_bd"])
        yield
        kb.dma("sp", hm[:], C["gla_hm"])
        yield
        kb.dma("sp", waf[:], W["gla_w_alpha"][l])
        yield
        kb.dma("sp", baf[:], W["gla_b_alpha"][l].rearrange("(o f) -> o f", o=1))
        yield
        kb.dma("sp", gng[:], AP(W["gla_norm_g"].tensor, l * 256, [[0, 128], [1, 256]]))
        yield
        kb.dma("sp", gqs[:], gqT)
        yield
        kb.dma("sp", gks[:], gkT)
        yield
        kb.dma("sp", gas[:], gaT)
        yield
        kb.op("dve", lambda e: e.tensor_copy(out=wab[:], in_=waf[:]), reads=[waf[:]], writes=[wab[:]])
        yield
        kb.op("dve", lambda e: e.tensor_copy(out=bab[:], in_=baf[:]), reads=[baf[:]], writes=[bab[:]])
        yield
        kb.op("pool", lambda e: e.memset(one1[:], 1.0), writes=[one1[:]])
        yield
        kb.op("pool", lambda e: e.memset(Sf[:], 0.0), writes=[Sf[:]])
        yield
        kb.op("pool", lambda e: e.memset(Sb[:], 0.0), writes=[Sb[:]])
        yield
        for qpad in qpad_2:
            kb.op("pool", lambda e: e.memset(qpad[:], 0.0), writes=[qpad[:]])
            yield
        ONE = cst[:, 1:2]
        def fe(ti):
            t0 = ti * 128
            ts = slice(t0, t0 + 128)
            (gk_t, gv_t, gr_t, Lt, eb, enb, ekd, qf, qpad, ktf, ktm, kd, attnb, sqo, ss4, yt, yb, yT, x_ps, b_ps, ku_ps, at_ps, tp2) = (gk_t_2[ti % 2], gv_t_2[ti % 2], gr_t_2[ti % 2], Lt_2[ti % 2], eb_2[ti % 2], enb_2[ti % 2], ekd_2[ti % 2], qf_2[ti % 2], qpad_2[ti % 2], ktf_2[ti % 2], ktm_2[ti % 2], kd_2[ti % 2], attnb_2[ti % 2], sqo_2[ti % 2], ss4_2[ti % 2], yt_2[ti % 2], yb_2[ti % 2], yT_2[ti % 2], x_ps_2[ti % 2], b_ps_2[ti % 2], ku_ps_2[ti % 2], at_ps_2[ti % 2], tp2_2[ti % 2])
            kb.dma("sp", gk_t[:], gk_d[ts, :])
            yield
            kb.dma("sp", gv_t[:], gv_d[ts, :])
            yield
            kb.dma("sp", gr_t[:], grs_d[ts, :])
            yield
            kb.mmg(x_ps, [(gas[:, ts], wab[:]), (one1[:], bab[:])])
            yield
            kb.op("act", lambda e: e.activation(out=Lt[:], in_=x_ps, func=AF.Exp, scale=-1.0), reads=[x_ps], writes=[Lt[:]])
            yield
            kb.op("act", lambda e: e.activation(out=Lt[:], in_=Lt[:], func=AF.Ln, bias=ONE), reads=[Lt[:], cst[:]], writes=[Lt[:]])
            yield
            kb.mmg(b_ps, [(Lt[:], gc[:, 0, :])])
            yield
            kb.mmg(ku_ps, [(gc[:, 1, :], Lt[:])])
            yield
            kb.op("act", lambda e: e.activation(out=eb[:], in_=b_ps, func=AF.Exp), reads=[b_ps], writes=[eb[:]])
            yield
            kb.op("act", lambda e: e.activation(out=enb[:], in_=b_ps, func=AF.Exp, scale=-1.0), reads=[b_ps], writes=[enb[:]])
            yield
            kb.op("act", lambda e: e.activation(out=ekd[:], in_=ku_ps, func=AF.Exp), reads=[ku_ps], writes=[ekd[:]])
            yield
            kb.op("dve", lambda e: e.scalar_tensor_tensor(out=qf[:], in0=gqs[:, ts], scalar=32.0 ** -0.5, in1=eb[:],
                                                          op0=ALU.mult, op1=ALU.mult),
                  reads=[gqs[:], eb[:]], writes=[qf[:]])
            yield
            kb.op("pool", lambda e: e.tensor_copy(out=qpad[:, 0, 0:64], in_=qf[:, 0:64]), reads=[qf[:]], writes=[qpad[:]])
            yield
            kb.op("pool", lambda e: e.tensor_copy(out=qpad[:, 1, 64:128], in_=qf[:, 64:128]), reads=[qf[:]], writes=[qpad[:]], accum=True)
            yield
            kb.op("dve", lambda e: e.tensor_tensor(out=ktf[:], in0=gks[:, ts], in1=enb[:], op=ALU.mult),
                  reads=[gks[:], enb[:]], writes=[ktf[:]])
            yield
            for h in range(4):
                kb.op("pool", lambda e: e.tensor_scalar(out=ktm[:, h, :], in0=ktf[:], scalar1=hm[:, h:h + 1], scalar2=None, op0=ALU.mult),
                      reads=[ktf[:], hm[:]], writes=[ktm[:]], accum=(h > 0))
                yield
            kb.op("dve", lambda e: e.tensor_tensor(out=kd[:], in0=gk_t[:], in1=ekd[:], op=ALU.mult),
                  reads=[gk_t[:], ekd[:]], writes=[kd[:]])
            yield
            for h in range(4):
                kb.mmg(at_ps[:, h, :], [(ktm[:, h, :], qf[:])])
                yield
            kb.op("dve", lambda e: e.tensor_tensor(out=attnb[:], in0=at_ps[:], in1=gc[:, 2, :].unsqueeze(1).to_broadcast([128, 4, 128]),
                                                   op=ALU.mult),
                  reads=[at_ps[:], gc[:]], writes=[attnb[:]])
            yield

        def be(ti):
            t0 = ti * 128
            ts = slice(t0, t0 + 128)
            (gk_t, gv_t, gr_t, Lt, eb, enb, ekd, qf, qpad, ktf, ktm, kd, attnb, sqo, ss4, yt, yb, yT, x_ps, b_ps, ku_ps, at_ps, tp2) = (gk_t_2[ti % 2], gv_t_2[ti % 2], gr_t_2[ti % 2], Lt_2[ti % 2], eb_2[ti % 2], enb_2[ti % 2], ekd_2[ti % 2], qf_2[ti % 2], qpad_2[ti % 2], ktf_2[ti % 2], ktm_2[ti % 2], kd_2[ti % 2], attnb_2[ti % 2], sqo_2[ti % 2], ss4_2[ti % 2], yt_2[ti % 2], yb_2[ti % 2], yT_2[ti % 2], x_ps_2[ti % 2], b_ps_2[ti % 2], ku_ps_2[ti % 2], at_ps_2[ti % 2], tp2_2[ti % 2])
            kb.mm(o_ps[:], qpad[:, 0, :], Sb[:], start=True, stop=False)
            yield
            for ch in range(2):
                cs = slice(ch * 64, (ch + 1) * 64)
                kb.mmg(su_ps, [(kd[cs, :], gv_t[cs, :])])
                yield
                dcol = eb[:, ch * 64 + 63:ch * 64 + 64]
                kb.op("dve", lambda e: e.scalar_tensor_tensor(out=Sf[:], in0=Sf[:], scalar=dcol, in1=su_ps, op0=ALU.mult, op1=ALU.add),
                      reads=[Sf[:], eb[:], su_ps], writes=[Sf[:]])
                yield
                kb.op("pool", lambda e: e.tensor_tensor(out=Sb[:], in0=Sf[:], in1=bd[:], op=ALU.mult),
                      reads=[Sf[:], bd[:]], writes=[Sb[:]])
                yield
                if ch == 0:
                    kb.mm(o_ps[:], qpad[:, 1, :], Sb[:], start=False, stop=False)
                    yield
            for h in range(4):
                kb.mm(o_ps[:, h * 64:(h + 1) * 64], attnb[:, h, :], gv_t[:, h * 64:(h + 1) * 64], start=False, stop=(h == 3))
                yield
            kb.op("act", lambda e: e.activation(out=sqo[:], in_=o_ps[:], func=AF.Square), reads=[o_ps[:]], writes=[sqo[:]])
            yield
            kb.op("dve", lambda e: e.tensor_reduce(out=ss4[:, 0:4], in_=sqo[:].rearrange("p (h d) -> p h d", d=64), axis=AX.X, op=ALU.add),
                  reads=[sqo[:]], writes=[ss4[:]])
            yield
            kb.op("act", lambda e: e.activation(out=ss4[:, 4:8], in_=ss4[:, 0:4], func=AF.Sqrt, scale=1.0 / 64, bias=EPS),
                  reads=[ss4[:], cst[:]], writes=[ss4[:]])
            yield
            kb.op("dve", lambda e: e.reciprocal(out=ss4[:, 4:8], in_=ss4[:, 4:8]), reads=[ss4[:]], writes=[ss4[:]])
            yield
            kb.op("dve", lambda e: e.tensor_tensor(out=yt[:].rearrange("p (h d) -> p h d", d=64),
                                                   in0=o_ps[:].rearrange("p (h d) -> p h d", d=64),
                                                   in1=ss4[:, 4:8].unsqueeze(2).to_broadcast([128, 4, 64]), op=ALU.mult),
                  reads=[o_ps[:], ss4[:]], writes=[yt[:]])
            yield
            kb.op("pool", lambda e: e.tensor_tensor(out=yt[:], in0=yt[:], in1=gng[:], op=ALU.mult), reads=[yt[:], gng[:]], writes=[yt[:]])
            yield
            kb.op("dve", lambda e: e.tensor_tensor(out=yb[:], in0=yt[:], in1=gr_t[:], op=ALU.mult), reads=[yt[:], gr_t[:]], writes=[yb[:]])
            yield
            for c in range(2):
                kb.tr(tp2[:, c, :], yb[:, c * 128:(c + 1) * 128], ident[:])
                yield
            kb.op("act", lambda e: e.copy(out=yT[:], in_=tp2), reads=[tp2], writes=[yT[:]])
            yield
            kb.dma("pool", ymixT[768:1024, ts].rearrange("(c p) t -> p c t", p=128), yT[:])
            yield

        yield from fe(0)
        for ti in range(NT):
            if ti + 1 < NT:
                yield from fe(ti + 1)
            yield from be(ti)

    def phase_outproj(l, xsrc, xdst):
        kb.push()
        Wo = kb.sb("Wo", [128, 8, D], BF16)
        stg = [kb.sb("stg", [128, D], F32) for _ in range(2)]
        load_w(Wo, W["w_out"][l], 8, D, None, stg)
        yms = [kb.sb("ym", [128, 8, 512], BF16) for _ in range(2)]
        xts = [kb.sb("xt", [128, D], F32) for _ in range(2)]
        xos = [kb.sb("xo", [128, D], F32) for _ in range(2)]
        pss = [kb.ps("ps", [128, 512], F32) for _ in range(4)]
        kb.dma("sp", yms[0][:], ymixT[:, 0:512].rearrange("(c p) t -> p c t", p=128))
        for i in range(NQ5):
            t0 = i * 512
            ym = yms[i % 2]
            if i + 1 < NQ5:
                kb.dma("sp", yms[(i + 1) % 2][:], ymixT[:, t0 + 512:t0 + 1024].rearrange("(c p) t -> p c t", p=128))
            for s in range(4):
                xt, xo = xts[s % 2], xos[s % 2]
                rows = slice(t0 + s * 128, t0 + (s + 1) * 128)
                kb.dma("sp", xt[:], xsrc[rows, :])
                for half in range(2):
                    hs = slice(half * 512, (half + 1) * 512)
                    ps = pss[(s % 2) * 2 + half]
                    kb.mmg(ps[:], [(ym[:, c, s * 128:(s + 1) * 128], Wo[:, c, hs]) for c in range(8)])
                    kb.op("dve", lambda e: e.tensor_tensor(out=xo[:, hs], in0=xt[:, hs], in1=ps[:], op=ALU.add),
                          reads=[xt[:], ps[:]], writes=[xo[:]], accum=(half > 0))
                kb.dma("pool", xdst[rows, :], xo[:])
        kb.pop()

    act_d = dram("ffn_act", [DFF, S], BF16)

    def phase_ffn_a(l, xsrc):
        kb.push()
        Wg = kb.sb("Wg", [128, 8, DFF], BF16)
        Wu = kb.sb("Wu", [128, 8, DFF], BF16)
        stg = [kb.sb("stg", [128, DFF], F32) for _ in range(2)]
        gcol = kb.sb("gcol", [128, 8], F32)
        load_gcol(gcol, W["ffn_norm_g"][l])
        load_w(Wg, W["ffn_w_gate"][l], 8, DFF, gcol, stg)
        load_w(Wu, W["ffn_w_up"][l], 8, DFF, gcol, stg)
        nb = NormBufs()
        hTs = [kb.sb("hT", [128, 8, 512], BF16) for _ in range(2)]
        aTs = [kb.sb("aT", [128, 11, 512], BF16) for _ in range(2)]
        sg = [kb.sb("sg", [128, 512], BF16) for _ in range(2)]
        pg = [kb.ps("pg", [128, 512], F32) for _ in range(2)]
        pu = [kb.ps("pu", [128, 512], F32) for _ in range(2)]
        norm_pre(nb, xsrc, 0)
        norm_tr(nb, hTs[0])
        for i in range(NQ5):
            t0 = i * 512
            hT = hTs[i % 2]
            for f in range(22):
                if f == 2 and i + 1 < NQ5:
                    norm_pre(nb, xsrc, t0 + 512)
                if f == 14 and i + 1 < NQ5:
                    norm_tr(nb, hTs[(i + 1) % 2])
                fs = slice(f * 128, (f + 1) * 128)
                aT = aTs[f // 11]
                kb.mmg(pg[f % 2][:], [(Wg[:, c, fs], hT[:, c, :]) for c in range(8)])
                kb.mmg(pu[f % 2][:], [(Wu[:, c, fs], hT[:, c, :]) for c in range(8)])
                kb.op("act", lambda e: e.activation(out=sg[f % 2][:], in_=pg[f % 2][:], func=AF.Silu),
                      reads=[pg[f % 2][:]], writes=[sg[f % 2][:]])
                kb.op("dve", lambda e: e.tensor_tensor(out=aT[:, f % 11, :], in0=sg[f % 2][:], in1=pu[f % 2][:], op=ALU.mult),
                      reads=[sg[f % 2][:], pu[f % 2][:]], writes=[aT[:]], accum=True)
                if f % 11 == 10:
                    hf = f // 11
                    kb.dma("pool", act_d[hf * 1408:(hf + 1) * 1408, t0:t0 + 512].rearrange("(f p) t -> p f t", p=128), aT[:])
        kb.pop()

    def phase_ffn_b(l, xsrc, xdst):
        kb.push()
        Wd = kb.sb("Wd", [128, 22, D], BF16)
        stg = [kb.sb("stg", [128, D], F32) for _ in range(2)]
        load_w(Wd, W["ffn_w_down"][l], 22, D, None, stg)
        aTs = [kb.sb("aT", [128, 22, 512], BF16) for _ in range(2)]
        xts = [kb.sb("xt", [128, D], F32) for _ in range(2)]
        xos = [kb.sb("xo", [128, D], F32) for _ in range(2)]
        pss = [kb.ps("ps", [128, 512], F32) for _ in range(4)]
        kb.dma("sp", aTs[0][:], act_d[:, 0:512].rearrange("(f p) t -> p f t", p=128))
        for i in range(NQ5):
            t0 = i * 512
            aT = aTs[i % 2]
            if i + 1 < NQ5:
                kb.dma("sp", aTs[(i + 1) % 2][:], act_d[:, t0 + 512:t0 + 1024].rearrange("(f p) t -> p f t", p=128))
            for s in range(4):
                xt, xo = xts[s % 2], xos[s % 2]
                rows = slice(t0 + s * 128, t0 + (s + 1) * 128)
                kb.dma("sp", xt[:], xsrc[rows, :])
                for half in range(2):
                    hs = slice(half * 512, (half + 1) * 512)
                    ps = pss[(s % 2) * 2 + half]
                    kb.mmg(ps[:], [(aT[:, f, s * 128:(s + 1) * 128], Wd[:, f, hs]) for f in range(22)])
                    kb.op("dve", lambda e: e.tensor_tensor(out=xo[:, hs], in0=xt[:, hs], in1=ps[:], op=ALU.add),
                          reads=[xt[:], ps[:]], writes=[xo[:]], accum=(half > 0))
                kb.dma("pool", xdst[rows, :], xo[:])
        kb.pop()

    def phase_ple(l, xsrc, xdst):
        kb.push()
        Wpg = kb.sb("Wpg", [128, 8, D], BF16)
        Wpp = kb.sb("Wpp", [128, 2, D], BF16)
        stg = [kb.sb("stg", [128, D], F32) for _ in range(2)]
        gcol = kb.sb("gcol", [128, 8], F32)
        load_gcol(gcol, W["ple_norm_g"][l])
        load_w(Wpg, W["ple_w_gate"][l], 8, D, gcol, stg)
        load_w(Wpp, W["ple_w_proj"][l], 2, D, None, stg)
        nb = NormBufs()
        x2s = [kb.sb("x2", [128, D], F32) for _ in range(2)]
        hTs = [kb.sb("hT", [128, 8, 512], BF16) for _ in range(2)]
        pf = kb.sb("pf", [128, 256], F32)
        pb = kb.sb("pb", [128, 256], BF16)
        pTt = kb.sb("pTt", [128, 2, 128], BF16)
        sgm = kb.sb("sgm", [128, 512], F32)
        xos = [kb.sb("xo", [128, D], F32) for _ in range(2)]
        tp2 = kb.ps("tp2", [128, 2, 128], BF16)
        pg = kb.ps("pg", [128, 512], F32)
        pp = kb.ps("pp", [128, 512], F32)
        norm_pre(nb, xsrc, 0)
        norm_tr(nb, hTs[0])
        for i in range(NQ5):
            t0 = i * 512
            hT = hTs[i % 2]
            for s in range(4):
                if s == 1 and i + 1 < NQ5:
                    norm_pre(nb, xsrc, t0 + 512)
                if s == 3 and i + 1 < NQ5:
                    norm_tr(nb, hTs[(i + 1) % 2])
                x2, xo = x2s[s % 2], xos[s % 2]
                rows = slice(t0 + s * 128, t0 + (s + 1) * 128)
                kb.dma("sp", x2[:], xsrc[rows, :])
                kb.dma("sp", pf[:], p_in[l, rows, :])
                kb.op("dve", lambda e: e.tensor_copy(out=pb[:], in_=pf[:]), reads=[pf[:]], writes=[pb[:]])
                for c in range(2):
                    kb.tr(tp2[:, c, :], pb[:, c * 128:(c + 1) * 128], ident[:])
                kb.op("act", lambda e: e.copy(out=pTt[:], in_=tp2[:]), reads=[tp2[:]], writes=[pTt[:]])
                for half in range(2):
                    hs = slice(half * 512, (half + 1) * 512)
                    kb.mmg(pg[:], [(hT[:, c, s * 128:(s + 1) * 128], Wpg[:, c, hs]) for c in range(8)])
                    kb.mmg(pp[:], [(pTt[:, c, :], Wpp[:, c, hs]) for c in range(2)])
                    kb.op("act", lambda e: e.activation(out=sgm[:], in_=pg[:], func=AF.Sigmoid), reads=[pg[:]], writes=[sgm[:]])
                    kb.op("dve", lambda e: e.tensor_tensor(out=sgm[:], in0=sgm[:], in1=pp[:], op=ALU.mult),
                          reads=[sgm[:], pp[:]], writes=[sgm[:]])
                    kb.op("dve", lambda e: e.tensor_tensor(out=xo[:, hs], in0=x2[:, hs], in1=sgm[:], op=ALU.add),
                          reads=[x2[:], sgm[:]], writes=[xo[:]], accum=(half > 0))
                kb.dma("pool", xdst[rows, :], xo[:])
        kb.pop()

    def phase_final(xsrc):
        kb.push()
        gfin = kb.sb("gfin", [128, D], F32)
        kb.dma("sp", gfin[:], AP(W["final_norm_g"].tensor, 0, [[0, 128], [1, D]]))
        xt = kb.sb("xt", [128, D], F32)
        sq = kb.sb("sq", [128, D], F32)
        st4 = kb.sb("st4", [128, 4], F32)
        xo = kb.sb("xo", [128, D], F32)
        for ti in range(NT):
            rows = slice(ti * 128, (ti + 1) * 128)
            kb.dma("sp", xt[:], xsrc[rows, :])
            kb.op("act", lambda e: e.activation(out=sq[:], in_=xt[:], func=AF.Square, accum_out=st4[:, 0:1]),
                  reads=[xt[:]], writes=[sq[:], st4[:]])
            kb.op("act", lambda e: e.activation(out=st4[:, 1:2], in_=st4[:, 0:1], func=AF.Sqrt, scale=1.0 / D, bias=EPS),
                  reads=[st4[:], cst[:]], writes=[st4[:]])
            kb.op("dve", lambda e: e.reciprocal(out=st4[:, 2:3], in_=st4[:, 1:2]), reads=[st4[:]], writes=[st4[:]])
            kb.op("dve", lambda e: e.scalar_tensor_tensor(out=xo[:], in0=xt[:], scalar=st4[:, 2:3], in1=gfin[:], op0=ALU.mult, op1=ALU.mult),
                  reads=[xt[:], st4[:], gfin[:]], writes=[xo[:]])
            kb.dma("pool", out_d[rows, :], xo[:])
        kb.pop()

    if on("tables"):
        phase_tables()
    xcur = x_in
    bufs2 = [xa, xb]
    bi = 0
    for l in range(layers):
        if on("inproj"):
            phase_inproj(l, xcur)
        if on("compress"):
            phase_compress(l)
        if on("select") or on("gla"):
            kb.push()
            gens = []
            if on("select"):
                gens += [phase_select(l, [0]), phase_select(l, [1])]
            if on("gla"):
                gens.append(phase_gla(l))
            while gens:
                for gen in list(gens):
                    try:
                        next(gen)
                    except StopIteration:
                        gens.remove(gen)
            kb.pop()
        if on("attn"):
            phase_attn(l)
        if on("conv"):
            phase_conv(l)
        x1 = bufs2[bi]
        x2 = bufs2[1 - bi]
        if on("outproj"):
            phase_outproj(l, xcur, x1)
        if on("ffn"):
            phase_ffn_a(l, x1)
            phase_ffn_b(l, x1, x2)
        if on("ple"):
            phase_ple(l, x2, x1)
        xcur = x1
        bi = 1 - bi
    if final:
        phase_final(xcur)
    kb.pop()
    return nc, hc


_CACHE = {}


def kernel(**inputs):
    S = 8192
    if "nc" not in _CACHE:
        _CACHE["nc"] = build(S=S, layers=2)
    nc, hc = _CACHE["nc"]
    x = np.asarray(inputs["x"], dtype=np.float32)
    p = np.asarray(inputs["p"], dtype=np.float32)
    shared = {n: np.ascontiguousarray(np.asarray(inputs[n], dtype=np.float32)) for n in WNAMES}
    shared.update(hc)
    in_maps = []
    for b in range(8):
        m = dict(shared)
        m["x"] = np.ascontiguousarray(x[b])
        m["p"] = np.ascontiguousarray(p[:, b])
        in_maps.append(m)
    res = run_bass_kernel_spmd(nc, in_maps, core_ids=list(range(8)))
    return np.stack([np.asarray(r["out"], dtype=np.float32) for r in res.results], axis=0)
```

```python
import math
from contextlib import ExitStack
import numpy as np
import ml_dtypes
import concourse.bass as bass
import concourse.mybir as mybir
from concourse.bass_utils import run_bass_kernel_spmd
from concourse.ap import AP

F32, BF16 = mybir.dt.float32, mybir.dt.bfloat16
AF = mybir.ActivationFunctionType
ALU = mybir.AluOpType
AX = mybir.AxisListType

D = 1024
DIN = 2600
DFF = 2816
NEGV = -30000.0
OFFS = 2063
NTS = 4592
NTW = 1024


class Buf:
    __slots__ = ("w", "r", "sem", "cnt", "scope")

    def __init__(self):
        self.w = {}
        self.r = {}
        self.sem = None
        self.cnt = 0
        self.scope = 0


def _merge(d, toks):
    for k, (sem, v) in toks.items():
        if k not in d or d[k][1] < v:
            d[k] = (sem, v)


class KB:
    def __init__(self, nc):
        self.nc = nc
        self.eng = {"pe": nc.tensor, "act": nc.scalar, "dve": nc.vector, "pool": nc.gpsimd, "sp": nc.sync}
        self.esem = {e: nc.alloc_semaphore("s_" + e) for e in ("pe", "act", "dve", "pool")}
        self.ecnt = {e: 0 for e in self.esem}
        self.waited = {e: {} for e in self.eng}
        self.bufs = {}
        self.free_sems = {}
        self.stacks = []
        self.uid = 0
        self.nsem = 0

    def push(self):
        self.stacks.append((ExitStack(), []))

    def pop(self):
        self.barrier()
        st, names = self.stacks.pop()
        for n in names:
            b = self.bufs.pop(n, None)
            if b is not None and b.sem is not None:
                for q, sc in b.sem.items():
                    self.free_sems.setdefault(q, []).append(tuple(sc))
        st.close()

    def sb(self, name, shape, dt):
        self.uid += 1
        nm = f"{name}_{self.uid}"
        t = self.stacks[-1][0].enter_context(self.nc.sbuf_tensor(nm, list(shape), dt))
        self.stacks[-1][1].append(nm)
        return t

    def ps(self, name, shape, dt=F32):
        self.uid += 1
        nm = f"{name}_{self.uid}"
        t = self.stacks[-1][0].enter_context(self.nc.psum_tensor(nm, list(shape), dt))
        self.stacks[-1][1].append(nm)
        return t

    def buf(self, ap):
        n = ap.tensor.name
        b = self.bufs.get(n)
        if b is None:
            b = self.bufs[n] = Buf()
        return b

    def _need(self, reads, writes, accum):
        need = {}
        for a in reads:
            _merge(need, self.buf(a).w)
        for a in writes:
            b = self.buf(a)
            if not accum:
                _merge(need, b.w)
            _merge(need, b.r)
        return need

    def _wait(self, e, need):
        own = self.esem["pe"].num if e == "pe" else None
        wd = self.waited[e]
        for k, (sem, v) in need.items():
            if k == own:
                continue
            if wd.get(k, 0) < v:
                self.eng[e].wait_ge(sem, v)
                wd[k] = v

    def _update(self, tok, reads, writes, accum):
        for a in writes:
            b = self.buf(a)
            if accum:
                _merge(b.w, tok)
            else:
                b.w = dict(tok)
                b.r = {}
        for a in reads:
            _merge(self.buf(a).r, tok)

    def op(self, e, fn, reads=(), writes=(), accum=False):
        reads = [a for a in reads if isinstance(a, AP)]
        self._wait(e, self._need(reads, writes, accum))
        ins = fn(self.eng[e])
        self.ecnt[e] += 1
        sem = self.esem[e]
        ins.then_inc(sem, 1)
        self._update({sem.num: (sem, self.ecnt[e])}, reads, writes, accum)
        return ins

    def mmg(self, out, pairs, reads_extra=()):
        reads = []
        for l, r in pairs:
            reads += [l, r]
        self._wait("pe", self._need(reads, [out], False))
        n = len(pairs)
        for i, (l, r) in enumerate(pairs):
            ins = self.nc.tensor.matmul(out, lhsT=l, rhs=r, start=(i == 0), stop=(i == n - 1))
        self.ecnt["pe"] += 1
        sem = self.esem["pe"]
        ins.then_inc(sem, 1)
        self._update({sem.num: (sem, self.ecnt["pe"])}, reads, [out], False)

    def mm(self, out, l, r, start, stop, last=True):
        self._wait("pe", self._need([l, r], [out], not start))
        ins = self.nc.tensor.matmul(out, lhsT=l, rhs=r, start=start, stop=stop, skip_group_check=True)
        self.ecnt["pe"] += 1
        sem = self.esem["pe"]
        ins.then_inc(sem, 1)
        self._update({sem.num: (sem, self.ecnt["pe"])}, [l, r], [out], not start)

    def tr(self, out, in_, ident):
        k = in_.shape[0]
        return self.op("pe", lambda e: e.transpose(out=out, in_=in_, identity=ident[0:k, 0:k]),
                       reads=[in_, ident], writes=[out], accum=True)

    def dma(self, q, out, in_, accum=False, sbuf_side=None, **kw):
        sbap = sbuf_side
        if sbap is None:
            sbap = out if self.is_sb(out) else in_
        b = self.buf(sbap)
        if b.sem is None:
            b.sem = {}
        if q not in b.sem:
            fl = self.free_sems.setdefault(q, [])
            if fl:
                b.sem[q] = list(fl.pop())
            else:
                self.nsem += 1
                b.sem[q] = [self.nc.alloc_semaphore(f"sd{self.nsem}"), 0]
        acc = accum or (not self.is_sb(out))
        self._wait(q, self._need([in_], [out], acc))
        ins = self.eng[q].dma_start(out=out, in_=in_, **kw)
        sc = b.sem[q]
        sc[1] += 16
        ins.then_inc(sc[0], 16)
        self._update({sc[0].num: (sc[0], sc[1])}, [in_], [out], acc)

    def is_sb(self, ap):
        return ap.tensor.name in self.sbnames

    sbnames = None

    def barrier(self):
        toks = {}
        for e, sem in self.esem.items():
            toks[sem.num] = (sem, self.ecnt[e])
        for b in self.bufs.values():
            if b.sem is not None:
                for sc in b.sem.values():
                    if sc[1] > 0:
                        toks[sc[0].num] = (sc[0], sc[1])
        for e in self.eng:
            wd = self.waited[e]
            for k, (sem, v) in toks.items():
                if wd.get(k, 0) < v:
                    self.eng[e].wait_ge(sem, v)
                    wd[k] = v
        for b in self.bufs.values():
            b.w = {}
            b.r = {}


class SBNames:
    def __init__(self, dram_names):
        self.d = dram_names

    def __contains__(self, n):
        return n not in self.d


def _t5_bucket_np(n):
    n = np.maximum(n, 0)
    nf = np.maximum(n, 1).astype(np.float32)
    large = 16 + (np.log(nf / np.float32(16)) / np.float32(math.log(8.0)) * np.float32(16)).astype(np.int32)
    large = np.minimum(large, 31)
    return np.where(n < 16, n, large)


def host_consts(S):
    NB = S // 64
    c = {}
    dist = np.arange(NTS) - OFFS
    oh = np.zeros((33, NTS), np.float32)
    bk = _t5_bucket_np(dist)
    for i in range(NTS):
        if dist[i] < 0:
            oh[32, i] = 1.0
        else:
            oh[bk[i], i] = 1.0
    c["oh_sel"] = oh
    dist = np.arange(NTW)
    ohw = np.zeros((33, NTW), np.float32)
    bk = _t5_bucket_np(dist)
    for i in range(NTW):
        if dist[i] >= 512:
            ohw[32, i] = 1.0
        else:
            ohw[bk[i], i] = 1.0
    c["oh_win"] = ohw
    E = np.zeros((128, S), np.float32)
    for j in range(NB):
        E[j, 64 * j:64 * j + 64] = 1.0
    c["e_tab"] = E.astype(ml_dtypes.bfloat16)
    W = 2 * NB - 1
    fw = np.zeros((128, W), np.float32)
    for p in range(128):
        b = 1 if p >= 64 else 0
        for cc in range(W):
            rel = cc - (NB - 1)
            if rel > b:
                fw[p, cc] = -1e4
            elif rel == b or rel == b - 1:
                fw[p, cc] = 1e4
    c["fw_tab"] = fw
    g = np.zeros((128, 3, 128), np.float32)
    for s in range(128):
        for t in range(128):
            if s // 64 == t // 64:
                if s <= t:
                    g[s, 0, t] = -1.0 / 16
                    g[s, 2, t] = 1.0
                else:
                    pass
                if s > t:
                    pass
    for s in range(128):
        for t in range(128):
            if s // 64 == t // 64 and s > t:
                g[s, 1, t] = -1.0 / 16
    c["gla_c"] = g
    bd = np.zeros((128, 256), np.float32)
    for p in range(128):
        bd[p, (p // 32) * 64:(p // 32) * 64 + 64] = 1.0
    c["gla_bd"] = bd
    hm = np.zeros((128, 4), np.float32)
    for p in range(128):
        hm[p, p // 32] = 1.0
    c["gla_hm"] = hm
    return c


WNAMES = ["rel_bias", "mix_norm_g", "w_in", "w_out", "cmp_pos_k", "cmp_w1_k", "cmp_b1_k", "cmp_w2_k",
          "cmp_pos_v", "cmp_w1_v", "cmp_b1_v", "cmp_w2_v", "conv_w", "conv_b", "conv_ln_g", "conv_ln_b",
          "gla_w_alpha", "gla_b_alpha", "gla_norm_g", "ffn_norm_g", "ffn_w_gate", "ffn_w_up", "ffn_w_down",
          "ple_norm_g", "ple_w_gate", "ple_w_proj", "final_norm_g"]
WSHAPES = {"rel_bias": [32, 8], "mix_norm_g": [2, 1024], "w_in": [2, 1024, 2600], "w_out": [2, 1024, 1024],
           "cmp_pos_k": [2, 32, 64], "cmp_w1_k": [2, 2048, 256], "cmp_b1_k": [2, 256], "cmp_w2_k": [2, 256, 64],
           "cmp_pos_v": [2, 32, 64], "cmp_w1_v": [2, 2048, 256], "cmp_b1_v": [2, 256], "cmp_w2_v": [2, 256, 64],
           "conv_w": [2, 31, 256], "conv_b": [2, 256], "conv_ln_g": [2, 256], "conv_ln_b": [2, 256],
           "gla_w_alpha": [2, 16, 128], "gla_b_alpha": [2, 128], "gla_norm_g": [2, 256],
           "ffn_norm_g": [2, 1024], "ffn_w_gate": [2, 1024, 2816], "ffn_w_up": [2, 1024, 2816],
           "ffn_w_down": [2, 2816, 1024], "ple_norm_g": [2, 1024], "ple_w_gate": [2, 1024, 1024],
           "ple_w_proj": [2, 256, 1024], "final_norm_g": [1024]}


FM = [(0, 128, .125), (128, 128, .125), (256, 128, .125), (384, 128, .125), (512, 128, 1.), (640, 128, 1.),
      (768, 128, 1.), (1024, 128, 1.), (1304, 128, 1.), (1432, 128, 1.), (1560, 128, 1.), (1688, 128, 1.),
      (1816, 128, 1.), (1944, 128, 1.), (2328, 16, 1.)]


def build(S=8192, layers=2, dbg_in=(), dbg_out=(), phases=None, final=True):
    nc = bass.Bass("TRN2", target_bir_lowering=False)
    kb = KB(nc)
    dram_names = set()
    kb.sbnames = SBNames(dram_names)
    NT = S // 128
    NQ5 = S // 512
    NB = S // 64
    NC = S // 16
    NCT = NC // 128
    NCV = NC - 1
    PBW = NC + 8 * (NT - 1)

    def dram(name, shape, dt, kind=None):
        if kind is None:
            kind = "ExternalInput" if name in dbg_in else ("ExternalOutput" if name in dbg_out else "Internal")
        t = nc.dram_tensor(name, list(shape), dt, kind=kind)
        dram_names.add(t.name)
        return t.ap()

    def on(ph):
        return phases is None or ph in phases

    x_in = dram("x", [S, D], F32, "ExternalInput")
    p_in = dram("p", [2, S, 256], F32, "ExternalInput")
    W = {n: dram(n, WSHAPES[n], F32, "ExternalInput") for n in WNAMES}
    hc = host_consts(S)
    C = {n: dram(n, list(v.shape), BF16 if v.dtype != np.float32 else F32, "ExternalInput") for n, v in hc.items()}
    out_d = dram("out", [S, D], F32, "ExternalOutput")
    qT = dram("qT", [8, 64, S], BF16)
    kcT = dram("kcT", [2, 128, S], BF16)
    ksT = dram("ksT", [128, S], BF16)
    kwT = dram("kwT", [128, S], BF16)
    cuT = dram("cuT", [512, S], BF16)
    gqT = dram("gqT", [128, S], BF16)
    gkT = dram("gkT", [128, S], BF16)
    gaT = dram("gaT", [16, S], BF16)
    vs_d = dram("vs", [S, 128], BF16)
    vw_d = dram("vw", [S, 128], BF16)
    gk_d = dram("gk", [S, 128], BF16)
    gv_d = dram("gv", [S, 256], BF16)
    grs_d = dram("grs", [S, 256], BF16)
    gates_d = dram("gates", [S, 24], F32)
    kcc_d = dram("kcc", [2, 64, NC], BF16)
    vcc_d = dram("vcc", [2, NC, 64], BF16)
    negT_d = dram("negT", [2, 128, S], BF16)
    tbs_d = dram("tbs", [8, NTS], F32)
    tbw_d = dram("tbw", [8, NTW], F32)
    ymixT = dram("ymixT", [1024, S], BF16)
    xa = dram("xa", [S, D], F32)
    xb = dram("xb", [S, D], F32)

    kb.push()
    cst = kb.sb("cst", [128, 8], F32)
    ident = kb.sb("ident", [128, 128], BF16)
    anti = kb.sb("anti", [128, 128], BF16)
    identf = kb.sb("identf", [128, 128], F32)
    onesf = kb.sb("onesf", [128, 128], F32)
    for col, v in enumerate([1e-6, 1.0, 0.0, 1e-30]):
        kb.op("pool", lambda e: e.memset(cst[:, col:col + 1], v), writes=[cst[:]], accum=True)
    kb.op("pool", lambda e: e.memset(onesf[:], 1.0), writes=[onesf[:]])
    for t, base in ((ident, 0), (identf, 0), (anti, -127)):
        kb.op("pool", lambda e: e.memset(t[:], 0.0), writes=[t[:]])
        pat = [[-1, 128]] if base == 0 else [[1, 128]]
        kb.op("pool", lambda e: e.affine_select(out=t[:], in_=t[:], pattern=pat, compare_op=ALU.not_equal,
                                                fill=1.0, base=base, channel_multiplier=1),
              reads=[t[:]], writes=[t[:]])
    EPS = cst[:, 0:1]

    def load_w(dst, src_rows, nchunks, ncols, gcol=None, stg=None):
        for c in range(nchunks):
            st = stg[c % len(stg)]
            kb.dma("sp", st[:, 0:ncols], src_rows[c * 128:(c + 1) * 128, :])
            if gcol is not None:
                kb.op("dve", lambda e: e.tensor_scalar(out=dst[:, c, :], in0=st[:, 0:ncols], scalar1=gcol[:, c:c + 1],
                                                       scalar2=None, op0=ALU.mult),
                      reads=[st[:], gcol[:]], writes=[dst[:]], accum=True)
            else:
                kb.op("dve", lambda e: e.tensor_copy(out=dst[:, c, :], in_=st[:, 0:ncols]),
                      reads=[st[:]], writes=[dst[:]], accum=True)

    def load_gcol(gcol, src1d):
        kb.dma("sp", gcol[:], src1d.rearrange("(c p) -> p c", p=128), allow_slow_non_contiguous=True)

    class NormBufs:
        def __init__(self):
            self.xt = [kb.sb("nxt", [128, D], F32) for _ in range(2)]
            self.sq = kb.sb("nsq", [128, D], F32)
            self.st4 = [kb.sb("nst4", [128, 4], F32) for _ in range(4)]
            self.xs = [kb.sb("nxs", [128, D], BF16) for _ in range(4)]
            self.pT = [kb.ps("npT", [128, 8, 128], BF16) for _ in range(2)]

    def norm_pre(nb, xsrc, t0):
        for s in range(4):
            xt = nb.xt[s % 2]
            st4 = nb.st4[s]
            xs = nb.xs[s]
            kb.dma("sp", xt[:], xsrc[t0 + s * 128:t0 + (s + 1) * 128, :])
            kb.op("act", lambda e: e.activation(out=nb.sq[:], in_=xt[:], func=AF.Square, accum_out=st4[:, 0:1]),
                  reads=[xt[:]], writes=[nb.sq[:], st4[:]])
            kb.op("act", lambda e: e.activation(out=st4[:, 1:2], in_=st4[:, 0:1], func=AF.Sqrt, scale=1.0 / D, bias=EPS),
                  reads=[st4[:], cst[:]], writes=[st4[:]])
            kb.op("dve", lambda e: e.reciprocal(out=st4[:, 2:3], in_=st4[:, 1:2]), reads=[st4[:]], writes=[st4[:]])
            kb.op("dve", lambda e: e.tensor_scalar(out=xs[:], in0=xt[:], scalar1=st4[:, 2:3], scalar2=None, op0=ALU.mult),
                  reads=[xt[:], st4[:]], writes=[xs[:]])

    def norm_tr(nb, hT):
        for s in range(4):
            pT = nb.pT[s % 2]
            for c in range(8):
                kb.tr(pT[:, c, :], nb.xs[s][:, c * 128:(c + 1) * 128], ident[:])
            kb.op("dve", lambda e: e.tensor_copy(out=hT[:, :, s * 128:(s + 1) * 128], in_=pT[:]),
                  reads=[pT[:]], writes=[hT[:]], accum=(s > 0))

    def phase_inproj(l, xsrc):
        kb.push()
        Wb = kb.sb("Wb", [128, 8, DIN], BF16)
        stg = [kb.sb("stg", [128, DIN], F32) for _ in range(2)]
        gcol = kb.sb("gcol", [128, 8], F32)
        load_gcol(gcol, W["mix_norm_g"][l])
        load_w(Wb, W["w_in"][l], 8, DIN, gcol, stg)
        nb = NormBufs()
        hTs = [kb.sb("hT", [128, 8, 512], BF16) for _ in range(2)]
        zTs = [kb.sb("zT", [128, 15, 512], BF16) for _ in range(2)]
        toks = [kb.sb("tok", [128, 4, 896], BF16) for _ in range(2)]
        gtt = [kb.sb("gt", [128, 4, 24], F32) for _ in range(2)]
        psA = [kb.ps("psA", [128, 512], F32) for _ in range(2)]
        psB = [kb.ps("psB", [128, 512], F32) for _ in range(4)]
        qT2 = qT.rearrange("h d t -> (h d) t")
        norm_pre(nb, xsrc, 0)
        norm_tr(nb, hTs[0])
        for i in range(NQ5):
            t0 = i * 512
            hT, zT, tok, gt = hTs[i % 2], zTs[i % 2], toks[i % 2], gtt[i % 2]
            for ci, (c0, m, scale) in enumerate(FM):
                if ci == 2 and i + 1 < NQ5:
                    norm_pre(nb, xsrc, t0 + 512)
                if ci == 12 and i + 1 < NQ5:
                    norm_tr(nb, hTs[(i + 1) % 2])
                ps = psA[ci % 2]
                kb.mmg(ps[0:m, :], [(Wb[:, c, c0:c0 + m], hT[:, c, :]) for c in range(8)])
                if ci % 2 == 0:
                    kb.op("act", lambda e: e.mul(out=zT[0:m, ci, :], in_=ps[0:m, :], mul=scale),
                          reads=[ps[:]], writes=[zT[:]], accum=True)
                else:
                    kb.op("dve", lambda e: e.tensor_scalar(out=zT[0:m, ci, :], in0=ps[0:m, :], scalar1=scale, scalar2=None,
                                                           op0=ALU.mult),
                          reads=[ps[:]], writes=[zT[:]], accum=True)
            for s in range(4):
                hs = lambda c: hT[:, c, s * 128:(s + 1) * 128]
                kb.mmg(psB[0][:, 0:128], [(hs(c), Wb[:, c, 896:1024]) for c in range(8)])
                kb.mmg(psB[1][:, 0:152], [(hs(c), Wb[:, c, 1152:1304]) for c in range(8)])
                kb.mmg(psB[2][:, 0:384], [(hs(c), Wb[:, c, 1944:2328]) for c in range(8)])
                kb.mmg(psB[3][:, 0:256], [(hs(c), Wb[:, c, 2344:2600]) for c in range(8)])
                kb.op("dve", lambda e: e.tensor_copy(out=tok[:, s, 0:128], in_=psB[0][:, 0:128]),
                      reads=[psB[0][:]], writes=[tok[:]], accum=True)
                kb.op("dve", lambda e: e.tensor_copy(out=tok[:, s, 128:256], in_=psB[1][:, 0:128]),
                      reads=[psB[1][:]], writes=[tok[:]], accum=True)
                kb.op("act", lambda e: e.activation(out=gt[:, s, :], in_=psB[1][:, 128:152], func=AF.Sigmoid),
                      reads=[psB[1][:]], writes=[gt[:]], accum=True)
                kb.op("dve", lambda e: e.tensor_copy(out=tok[:, s, 256:640], in_=psB[2][:, 0:384]),
                      reads=[psB[2][:]], writes=[tok[:]], accum=True)
                kb.op("act", lambda e: e.activation(out=tok[:, s, 640:896], in_=psB[3][:, 0:256], func=AF.Silu),
                      reads=[psB[3][:]], writes=[tok[:]], accum=True)
            sl = slice(t0, t0 + 512)
            kb.dma("pool", qT2[:, sl].rearrange("(c p) t -> p c t", p=128), zT[:, 0:4, :])
            kb.dma("pool", kcT[:, :, sl].rearrange("k p t -> p k t"), zT[:, 4:6, :])
            kb.dma("pool", ksT[:, sl], zT[:, 6, :])
            kb.dma("pool", kwT[:, sl], zT[:, 7, :])
            kb.dma("pool", cuT[:, sl].rearrange("(c p) t -> p c t", p=128), zT[:, 8:12, :])
            kb.dma("pool", gqT[:, sl], zT[:, 12, :])
            kb.dma("pool", gkT[:, sl], zT[:, 13, :])
            kb.dma("pool", gaT[:, sl], zT[0:16, 14, :])
            tv = lambda d: d[sl, :].rearrange("(s p) c -> p s c", p=128)
            kb.dma("pool", tv(vs_d), tok[:, :, 0:128])
            kb.dma("pool", tv(vw_d), tok[:, :, 128:256])
            kb.dma("pool", tv(gk_d), tok[:, :, 256:384])
            kb.dma("pool", tv(gv_d), tok[:, :, 384:640])
            kb.dma("pool", tv(grs_d), tok[:, :, 640:896])
            kb.dma("pool", tv(gates_d), gt[:])
        kb.pop()

    def phase_compress(l):
        kb.push()
        w1b = kb.sb("w1b", [64, 32, 256], BF16)
        w1s = kb.sb("w1s", [64, 8, 256], F32)
        posf = kb.sb("posf", [64, 32], F32)
        posb = kb.sb("posb", [64, 32], BF16)
        b1c = kb.sb("b1c", [128, 2], F32)
        bias = kb.sb("bias", [128, 2], F32)
        w2s = kb.sb("w2s", [128, 2, 64], F32)
        w2b = kb.sb("w2b", [128, 2, 64], BF16)
        src = kb.sb("src", [64, S], BF16)
        hid = kb.sb("hid", [128, 2, NC], BF16)
        osb = kb.sb("osb", [128, max(NC, NCT * 64)], BF16)
        pw = kb.ps("pw", [128, 2], F32)
        ph = kb.ps("ph", [128, 512], F32)
        po = kb.ps("po", [128, 512], F32)
        kb.op("pool", lambda e: e.memset(hid[:, :, NCV:NC], 0.0), writes=[hid[:]])
        for kv, sfx in ((0, "k"), (1, "v")):
            w1 = W["cmp_w1_" + sfx][l].rearrange("(l d) n -> d l n", d=64)
            for q4 in range(4):
                kb.dma("sp", w1s[:], w1[:, q4 * 8:(q4 + 1) * 8, :])
                kb.op("dve", lambda e: e.tensor_copy(out=w1b[:, q4 * 8:(q4 + 1) * 8, :], in_=w1s[:]),
                      reads=[w1s[:]], writes=[w1b[:]], accum=(q4 > 0))
            kb.dma("sp", posf[:], W["cmp_pos_" + sfx][l].rearrange("l d -> d l"), allow_slow_non_contiguous=True)
            kb.op("dve", lambda e: e.tensor_copy(out=posb[:], in_=posf[:]), reads=[posf[:]], writes=[posb[:]])
            kb.dma("sp", b1c[:], W["cmp_b1_" + sfx][l].rearrange("(c p) -> p c", p=128), allow_slow_non_contiguous=True)
            kb.dma("sp", w2s[:], W["cmp_w2_" + sfx][l].rearrange("(c p) d -> p c d", p=128))
            kb.op("dve", lambda e: e.tensor_copy(out=w2b[:], in_=w2s[:]), reads=[w2s[:]], writes=[w2b[:]])
            for half in range(2):
                kb.mmg(pw[:, half:half + 1], [(w1b[:, li, half * 128:(half + 1) * 128], posb[:, li:li + 1]) for li in range(32)])
            kb.op("dve", lambda e: e.tensor_tensor(out=bias[:], in0=pw[:], in1=b1c[:], op=ALU.add),
                  reads=[pw[:], b1c[:]], writes=[bias[:]])
            for h in range(2):
                kb.dma("sp", src[:], kcT[kv, h * 64:(h + 1) * 64, :])
                sv = src[:].rearrange("d (n s) -> d n s", s=16)
                for half in range(2):
                    pairs = []
                    for li in range(32):
                        rhs = sv[:, 0:NCV, li] if li < 16 else sv[:, 1:NCV + 1, li - 16]
                        pairs.append((w1b[:, li, half * 128:(half + 1) * 128], rhs))
                    kb.mmg(ph[:, 0:NCV], pairs)
                    kb.op("act", lambda e: e.activation(out=hid[:, half, 0:NCV], in_=ph[:, 0:NCV], func=AF.Silu,
                                                        bias=bias[:, half:half + 1]),
                          reads=[ph[:], bias[:]], writes=[hid[:]], accum=(half > 0))
                if kv == 0:
                    kb.mmg(po[0:64, 0:NC], [(w2b[:, half, :], hid[:, half, :]) for half in range(2)])
                    kb.op("dve", lambda e: e.tensor_copy(out=osb[0:64, 0:NC], in_=po[0:64, 0:NC]),
                          reads=[po[:]], writes=[osb[:]])
                    kb.dma("pool", kcc_d[h], osb[0:64, 0:NC])
                else:
                    for nt in range(NCT):
                        kb.mmg(po[:, nt * 64:(nt + 1) * 64],
                               [(hid[:, half, nt * 128:(nt + 1) * 128], w2b[:, half, :]) for half in range(2)])
                    kb.op("dve", lambda e: e.tensor_copy(out=osb[:, 0:NCT * 64], in_=po[:, 0:NCT * 64]),
                          reads=[po[:]], writes=[osb[:]])
                    kb.dma("pool", vcc_d[h].rearrange("(nt p) d -> p nt d", p=128),
                           osb[:, 0:NCT * 64].rearrange("p (nt d) -> p nt d", d=64))
        kb.pop()

    def phase_tables():
        kb.push()
        rel = kb.sb("rel", [33, 8], F32)
        r31 = kb.sb("r31", [32, 8], F32)
        oh = kb.sb("oh", [33, 512], F32)
        tsb = kb.sb("tsb", [8, 512], F32)
        pt = kb.ps("pt", [8, 512], F32)
        kb.dma("sp", rel[0:32, :], W["rel_bias"])
        kb.dma("sp", r31[:], AP(W["rel_bias"].tensor, 31 * 8, [[0, 32], [1, 8]]))
        kb.op("dve", lambda e: e.tensor_tensor(out=rel[0:32, :], in0=rel[0:32, :], in1=r31[:], op=ALU.subtract),
              reads=[rel[:], r31[:]], writes=[rel[:]])
        kb.op("pool", lambda e: e.memset(rel[32:33, :], NEGV), writes=[rel[:]])
        for tab, dst, n in ((C["oh_sel"], tbs_d, NTS), (C["oh_win"], tbw_d, NTW)):
            for c0 in range(0, n, 512):
                w = min(512, n - c0)
                kb.dma("sp", oh[:, 0:w], tab[:, c0:c0 + w])
                kb.mmg(pt[:, 0:w], [(rel[:], oh[:, 0:w])])
                kb.op("dve", lambda e: e.tensor_copy(out=tsb[:, 0:w], in_=pt[:, 0:w]), reads=[pt[:]], writes=[tsb[:]])
                kb.dma("pool", dst[:, c0:c0 + w], tsb[:, 0:w])
        kb.pop()

    def phase_select(l, gs):
        kc_sb = kb.sb("kc", [64, NC], BF16)
        pbw = kb.sb("pbw", [128, 4, PBW], F32)
        pbwb = kb.sb("pbwb", [128, 4, PBW], BF16)
        ptmp = kb.sb("ptmp", [128, 17], F32)
        fw = kb.sb("fw", [128, 2 * NB - 1], F32)
        qt_2 = [kb.sb("qt", [64, 4, 128], BF16) for _ in range(2)]
        sc_2 = [kb.sb("sc", [128, 4, 512], F32) for _ in range(2)]
        rs_2 = [kb.sb("rs", [128, 8], F32) for _ in range(2)]
        ipad_2 = [kb.sb("ipad", [128, NC + 4], F32) for _ in range(2)]
        sl_2 = [kb.sb("sl", [128, NB], F32) for _ in range(2)]
        t1_2 = [kb.sb("t1", [128, NB], F32) for _ in range(2)]
        sc2_2 = [kb.sb("sc2", [128, NB], F32) for _ in range(2)]
        m8_2 = [kb.sb("m8", [128, 16], F32) for _ in range(2)]
        negb_2 = [kb.sb("negb", [128, 128], BF16) for _ in range(2)]
        ntsb_2 = [kb.sb("ntsb", [128, 128], BF16) for _ in range(2)]
        S_ps = kb.ps("S_ps", [128, 2, 512], F32)
        tp = S_ps[:].rearrange("p a n -> p (a n)").bitcast(BF16)[:, 0:128]
        kb.dma("sp", fw[:], C["fw_tab"])
        yield
        cpat = 8 * (NT - 1) - 9
        for g in gs:
            kb.dma("sp", kc_sb[:], kcc_d[g])
            yield
            kb.op("pool", lambda e: e.memset(pbw[:], 0.0), writes=[pbw[:]])
            yield
            kb.op("pool", lambda e: e.memset(pbw[:, :, NC:PBW], NEGV), writes=[pbw[:]])
            yield
            for h in range(4):
                hg = 4 * g + h
                kb.dma("sp", ptmp[:], AP(tbs_d.tensor, hg * NTS + OFFS - 143, [[1, 128], [16, 17]]),
                       allow_slow_non_contiguous=True)
                yield
                for k2 in range(17):
                    col = cpat + 16 - k2
                    kb.op("pool", lambda e: e.tensor_copy(out=pbw[:, h, col:col + 1], in_=ptmp[:, k2:k2 + 1]),
                          reads=[ptmp[:]], writes=[pbw[:]])
                    yield
            kb.op("pool", lambda e: e.tensor_copy(out=pbwb[:], in_=pbw[:]), reads=[pbw[:]], writes=[pbwb[:]])
            yield
            for ipad in ipad_2:
                kb.op("pool", lambda e: e.memset(ipad[:], 0.0), writes=[ipad[:]])
                yield
            for ti in range(NT):
                t0 = ti * 128
                qt, sc, rs, ipad, sl, t1, sc2, m8, negb, ntsb = (qt_2[ti % 2], sc_2[ti % 2], rs_2[ti % 2], ipad_2[ti % 2], sl_2[ti % 2],
                                                                 t1_2[ti % 2], sc2_2[ti % 2], m8_2[ti % 2], negb_2[ti % 2], ntsb_2[ti % 2])
                ncol = min(NC, 8 * ti + 8)
                st = 8 * (NT - 1) - 8 * ti
                kb.dma("sp", qt[:], qT[4 * g:4 * g + 4, :, t0:t0 + 128].rearrange("h d t -> d h t"))
                yield
                for hh in range(2):
                    for h in (2 * hh, 2 * hh + 1):
                        kb.mmg(S_ps[:, h % 2, 0:ncol], [(qt[:, h, :], kc_sb[:, 0:ncol]), (ident[:], pbwb[:, h, st:st + ncol])])
                        yield
                    for h in (2 * hh, 2 * hh + 1):
                        kb.op("act", lambda e: e.activation(out=sc[:, h, 0:ncol], in_=S_ps[:, h % 2, 0:ncol], func=AF.Exp, accum_out=rs[:, h:h + 1]),
                              reads=[S_ps[:]], writes=[sc[:], rs[:]], accum=(h > 0))
                        yield
                kb.op("dve", lambda e: e.tensor_scalar_max(out=rs[:, 0:4], in0=rs[:, 0:4], scalar1=1e-30),
                      reads=[rs[:]], writes=[rs[:]])
                yield
                kb.op("dve", lambda e: e.reciprocal(out=rs[:, 4:8], in_=rs[:, 0:4]), reads=[rs[:]], writes=[rs[:]])
                yield
                iw = ipad[:, 1:1 + ncol]
                kb.op("dve", lambda e: e.tensor_scalar(out=iw, in0=sc[:, 0, 0:ncol], scalar1=rs[:, 4:5], scalar2=None, op0=ALU.mult),
                      reads=[sc[:], rs[:]], writes=[ipad[:]])
                yield
                for h in range(1, 4):
                    kb.op("dve", lambda e: e.scalar_tensor_tensor(out=iw, in0=sc[:, h, 0:ncol], scalar=rs[:, 4 + h:5 + h], in1=iw,
                                                                  op0=ALU.mult, op1=ALU.add),
                          reads=[sc[:], rs[:], ipad[:]], writes=[ipad[:]])
                    yield
                iv = ipad[:, 0:4 * NB].rearrange("p (j f) -> p j f", f=4)
                iv4 = ipad[:, 4:4 + 4 * NB].rearrange("p (j f) -> p j f", f=4)
                kb.op("dve", lambda e: e.tensor_tensor(out=sl[:], in0=iv[:, :, 0], in1=iv4[:, :, 0], op=ALU.add),
                      reads=[ipad[:]], writes=[sl[:]])
                yield
                kb.op("dve", lambda e: e.tensor_tensor(out=t1[:], in0=iv[:, :, 1], in1=iv[:, :, 2], op=ALU.add),
                      reads=[ipad[:]], writes=[t1[:]])
                yield
                kb.op("dve", lambda e: e.tensor_tensor(out=t1[:], in0=t1[:], in1=iv[:, :, 3], op=ALU.add),
                      reads=[ipad[:], t1[:]], writes=[t1[:]])
                yield
                kb.op("dve", lambda e: e.scalar_tensor_tensor(out=sl[:], in0=t1[:], scalar=2.0, in1=sl[:], op0=ALU.mult, op1=ALU.add),
                      reads=[t1[:], sl[:]], writes=[sl[:]])
                yield
                off = NB - 1 - 2 * ti
                kb.op("dve", lambda e: e.tensor_tensor(out=sl[:], in0=sl[:], in1=fw[:, off:off + NB], op=ALU.add),
                      reads=[sl[:], fw[:]], writes=[sl[:]])
                yield
                kb.op("dve", lambda e: e.tensor_scalar_add(out=sl[:, 0:1], in0=sl[:, 0:1], scalar1=1e4),
                      reads=[sl[:]], writes=[sl[:]])
                yield
                kb.op("dve", lambda e: e.max(out=m8[:, 0:8], in_=sl[:]), reads=[sl[:]], writes=[m8[:]])
                yield
                kb.op("dve", lambda e: e.match_replace(out=sc2[:], in_to_replace=m8[:, 0:8], in_values=sl[:], imm_value=-3e4),
                      reads=[sl[:], m8[:]], writes=[sc2[:]])
                yield
                kb.op("dve", lambda e: e.max(out=m8[:, 8:16], in_=sc2[:]), reads=[sc2[:]], writes=[m8[:]])
                yield
                kb.op("dve", lambda e: e.tensor_scalar(out=t1[:], in0=sl[:], scalar1=m8[:, 15:16], scalar2=None, op0=ALU.is_ge),
                      reads=[sl[:], m8[:]], writes=[t1[:]])
                yield
                kb.op("dve", lambda e: e.tensor_scalar(out=sc2[:], in0=sl[:], scalar1=-5000.0, scalar2=None, op0=ALU.is_gt),
                      reads=[sl[:]], writes=[sc2[:]])
                yield
                kb.op("dve", lambda e: e.tensor_tensor(out=t1[:], in0=t1[:], in1=sc2[:], op=ALU.mult),
                      reads=[t1[:], sc2[:]], writes=[t1[:]])
                yield
                kb.op("dve", lambda e: e.tensor_copy(out=negb[:, 0:NB], in_=t1[:]), reads=[t1[:]], writes=[negb[:]])
                yield
                kb.tr(tp[0:NB, :], negb[:, 0:NB], ident[:])
                yield
                kb.op("act", lambda e: e.copy(out=ntsb[0:NB, :], in_=tp[0:NB, :]), reads=[tp], writes=[ntsb[:]])
                yield
                kb.dma("pool", negT_d[g, 0:NB, t0:t0 + 128], ntsb[0:NB, :])
                yield

    def phase_attn(l):
        kb.push()
        ksT_sb = kb.sb("ksTs", [128, S], BF16)
        kwT_sb = kb.sb("kwTs", [128, S], BF16)
        kc_sb = kb.sb("kcs", [128, NC], BF16)
        vsa = kb.sb("vsa", [128, NT, 65], BF16)
        vwa = kb.sb("vwa", [128, NT, 65], BF16)
        vca = kb.sb("vca", [128, NCT, 65], BF16)
        hst = kb.sb("hst", [128, 512], F32)
        selH = [[kb.sb("sH", [128, 512], BF16) for r in range(5)] for h in range(4)]
        winS = [kb.sb("wS", [128, 512], BF16) for r in range(3)]
        winH1 = [kb.sb("wH", [128, 512], BF16) for h in range(4)]
        cmpH = [[kb.sb("cH", [128, 512], BF16) for u in range(5)] for h in range(4)]
        qsb = [kb.sb("qs", [128, 2, 512], BF16) for _ in range(2)]
        gtb = [kb.sb("gts", [128, 4, 24], F32) for _ in range(2)]
        mks = [kb.sb("mk", [128, 512], BF16) for _ in range(4)]
        pTs = [kb.sb("pTt", [128, 2, 512], BF16) for _ in range(4)]
        ynsa = kb.sb("ynsa", [128, 4, 256], F32)
        ynb = kb.sb("ynb", [128, 4, 256], BF16)
        ynT = kb.sb("ynT", [128, 2, 512], BF16)
        recs = [kb.sb("rec", [128, 8], F32) for _ in range(4)]
        tmps = [kb.sb("tmp", [128, 4, 64], F32) for _ in range(4)]
        Sps = [kb.ps("Sps", [128, 2, 512], F32) for _ in range(2)]
        accs = [kb.ps("acc", [128, 4, 65], F32) for _ in range(4)]
        tps = Sps[0][:].rearrange("p a n -> p (a n)").bitcast(BF16)[:, 0:512].rearrange("p (s q) -> p s q", q=128)
        for va in (vsa, vwa, vca):
            kb.op("pool", lambda e: e.memset(va[:, :, 64:65], 1.0), writes=[va[:]], accum=True)

        def load_h(dst, tens, offset, pstep):
            kb.dma("sp", hst[:], AP(tens, offset, [[pstep, 128], [1, 512]]))
            kb.op("dve", lambda e: e.tensor_copy(out=dst[:], in_=hst[:]), reads=[hst[:]], writes=[dst[:]])

        def load_q(g, qi, slot):
            q0 = qi * 512
            first = True
            for hp in range(2):
                for b2 in range(2):
                    kb.dma("sp", qsb[slot][b2 * 64:(b2 + 1) * 64, hp, :], qT[4 * g + 2 * hp + b2, :, q0:q0 + 512], accum=not first)
                    first = False
            kb.dma("sp", gtb[slot][:], gates_d[q0:q0 + 512, :].rearrange("(s p) c -> p s c", p=128))

        mkc = 0
        it = 0
        for g in range(2):
            for b2 in range(2):
                ps_ = slice(b2 * 64, (b2 + 1) * 64)
                kb.dma("sp", ksT_sb[ps_, :], ksT[g * 64:(g + 1) * 64, :], accum=(b2 > 0))
                kb.dma("sp", kwT_sb[ps_, :], kwT[g * 64:(g + 1) * 64, :], accum=(b2 > 0))
                kb.dma("sp", kc_sb[ps_, :], kcc_d[g], accum=(b2 > 0))
            kb.dma("sp", vsa[:, :, 0:64], vs_d[:, g * 64:(g + 1) * 64].rearrange("(kt p) d -> p kt d", p=128), accum=True)
            kb.dma("sp", vwa[:, :, 0:64], vw_d[:, g * 64:(g + 1) * 64].rearrange("(kt p) d -> p kt d", p=128), accum=True)
            kb.dma("sp", vca[:, :, 0:64], vcc_d[g].rearrange("(kt p) d -> p kt d", p=128), accum=True)
            for h in range(4):
                hg = 4 * g + h
                for r in range(-1, 4):
                    load_h(selH[h][r + 1], tbs_d.tensor, hg * NTS + OFFS - 127 - 128 * r, 1)
                load_h(winH1[h], tbw_d.tensor, hg * NTW + 1, 1)
                for u in range(5):
                    load_h(cmpH[h][u], tbs_d.tensor, hg * NTS + 512 * u, 16)
            for r in range(-4, -1):
                load_h(winS[r + 4], tbw_d.tensor, 4 * g * NTW - 127 - 128 * r, 1)
            load_q(g, 0, it % 2)
            for qi in range(NQ5):
                q0 = qi * 512
                kq = q0 // 128
                qs = qsb[it % 2]
                gts = gtb[it % 2]
                it += 1
                if qi + 1 < NQ5:
                    load_q(g, qi + 1, it % 2)
                for br in range(3):
                    if br == 0:
                        kts = [(nt, qi - 4 * nt) for nt in range(NCT) if qi - 4 * nt >= 0]
                    elif br == 1:
                        kts = [(kt, kt - kq) for kt in range(0, kq + 4)]
                    else:
                        kts = [(kt, kt - kq) for kt in range(max(0, kq - 4), kq + 4)]
                    units = []
                    for ki, (kt, r) in enumerate(kts):
                        for hp in range(2):
                            exs = []
                            for b2 in range(2):
                                h = 2 * hp + b2
                                if br == 0:
                                    exs.append(cmpH[h][r][:] if r <= 4 else None)
                                elif br == 1:
                                    exs.append(selH[h][r + 1][:] if r >= -1 else None)
                                elif r <= -2:
                                    exs.append(winS[r + 4][:])
                                elif r == -1:
                                    exs.append(winH1[h][:])
                                else:
                                    exs.append(selH[h][r + 1][:])
                            if br == 0:
                                units.append((hp, kc_sb, kt, exs, vca[:, kt, :], 0, 3, None, ki))
                            elif br == 1:
                                units.append((hp, ksT_sb, kt, exs, vsa[:, kt, :], max(r, 0), 3, kt, ki))
                            else:
                                units.append((hp, kwT_sb, kt, exs, vwa[:, kt, :], max(r, 0), min(r + 4, 3), None, ki))
                    lasts = {}
                    for ui, un in enumerate(units):
                        for s in range(un[5], un[6] + 1):
                            lasts[(un[0], s)] = ui
                    started = set()
                    n = len(units)
                    curmk = {}
                    for i in range(n + 2):
                        if i < n:
                            hp, ksb, kt, exs, va, smin, smax, mkt, ki = units[i]
                            if mkt is not None and hp == 0:
                                mk = mks[mkc % 4]
                                mkc += 1
                                curmk[ki] = mk
                                base = (g * 128 + 2 * mkt) * S + q0
                                kb.dma("sp", mk[0:64, :], AP(negT_d.tensor, base, [[0, 64], [1, 512]]))
                                kb.dma("sp", mk[64:128, :], AP(negT_d.tensor, base + S, [[0, 64], [1, 512]]), accum=True)
                            sp = Sps[i % 2]
                            pt = pTs[i % 4]
                            ksl = slice(kt * 128, (kt + 1) * 128)
                            cs = slice(smin * 128, (smax + 1) * 128)
                            ncs = (smax + 1 - smin) * 128
                            for b2 in range(2):
                                ps_ = slice(b2 * 64, (b2 + 1) * 64)
                                kb.mm(sp[:, b2, cs], ksb[ps_, ksl], qs[ps_, hp, cs], start=True, stop=(exs[b2] is None))
                            for b2 in range(2):
                                if exs[b2] is not None:
                                    kb.mm(sp[:, b2, cs], anti[:], exs[b2][:, cs], start=False, stop=True)
                            kb.op("act", lambda e: e.activation(out=pt[:, :, cs], in_=sp[:, :, cs], func=AF.Exp), reads=[sp[:]], writes=[pt[:]])
                            if mkt is not None:
                                mk = curmk[ki]
                                kb.op("dve", lambda e: e.tensor_tensor(out=pt[:, :, cs], in0=pt[:, :, cs],
                                                                       in1=mk[:, cs].unsqueeze(1).to_broadcast([128, 2, ncs]), op=ALU.mult),
                                      reads=[pt[:], mk[:]], writes=[pt[:]])
                        j = i - 2
                        if j >= 0:
                            hp, ksb, kt, exs, va, smin, smax, mkt, ki = units[j]
                            pt = pTs[j % 4]
                            for b2 in range(2):
                                h = 2 * hp + b2
                                a = accs[h]
                                for s in range(smin, smax + 1):
                                    st_ = h not in started
                                    started.add(h)
                                    kb.mm(a[:, s, :], pt[:, b2, s * 128:(s + 1) * 128], va, start=st_, stop=(lasts[(hp, s)] == j))
                    for h in range(4):
                        a = accs[h]
                        rec = recs[h]
                        tmp = tmps[h]
                        gcol = (4 * g + h) * 3 + br
                        kb.op("dve", lambda e: e.tensor_scalar_max(out=rec[:, 0:4], in0=a[:, :, 64], scalar1=1e-30),
                              reads=[a[:]], writes=[rec[:]])
                        kb.op("dve", lambda e: e.reciprocal(out=rec[:, 0:4], in_=rec[:, 0:4]), reads=[rec[:]], writes=[rec[:]])
                        kb.op("dve", lambda e: e.tensor_tensor(out=rec[:, 4:8], in0=rec[:, 0:4], in1=gts[:, :, gcol], op=ALU.mult),
                              reads=[rec[:], gts[:]], writes=[rec[:]])
                        dst = ynsa[:, :, h * 64:(h + 1) * 64]
                        rb = rec[:, 4:8].unsqueeze(2).to_broadcast([128, 4, 64])
                        if br == 0:
                            kb.op("dve", lambda e: e.tensor_tensor(out=dst, in0=a[:, :, 0:64], in1=rb, op=ALU.mult),
                                  reads=[a[:], rec[:]], writes=[ynsa[:]], accum=True)
                        else:
                            kb.op("dve", lambda e: e.tensor_tensor(out=tmp[:], in0=a[:, :, 0:64], in1=rb, op=ALU.mult),
                                  reads=[a[:], rec[:]], writes=[tmp[:]])
                            kb.op("pool", lambda e: e.tensor_tensor(out=dst, in0=dst, in1=tmp[:], op=ALU.add),
                                  reads=[ynsa[:], tmp[:]], writes=[ynsa[:]])
                kb.op("dve", lambda e: e.tensor_copy(out=ynb[:], in_=ynsa[:]), reads=[ynsa[:]], writes=[ynb[:]])
                for c in range(2):
                    for s in range(4):
                        kb.tr(tps[:, s, :], ynb[:, s, c * 128:(c + 1) * 128], ident[:])
                    kb.op("act", lambda e: e.copy(out=ynT[:, c, :], in_=tps.rearrange("p s q -> p (s q)")),
                          reads=[tps], writes=[ynT[:]], accum=(c > 0))
                kb.dma("pool", ymixT[g * 256:(g + 1) * 256, q0:q0 + 512].rearrange("(c p) t -> p c t", p=128), ynT[:])
        kb.pop()

    def phase_conv(l):
        kb.push()
        accs = [kb.sb("cacc", [128, S], F32) for _ in range(2)]
        cb = [kb.sb("cb", [128, 3], F32) for _ in range(2)]
        wT = kb.sb("wT", [128, 31], F32)
        dg = kb.sb("dg", [128, 31, 128], BF16)
        cps = [kb.ps("cps", [128, 512], F32) for _ in range(2)]
        for cc in range(2):
            kb.push()
            aT = kb.sb("aT", [128, S], BF16)
            gT = kb.sb("gT", [128, S], BF16)
            sg = kb.sb("sg", [128, S], BF16)
            hp = kb.sb("hp", [128, 32 + S], BF16)
            kb.dma("sp", aT[:], cuT[cc * 128:(cc + 1) * 128, :])
            kb.dma("sp", gT[:], cuT[256 + cc * 128:256 + (cc + 1) * 128, :])
            kb.dma("sp", wT[:], W["conv_w"][l][:, cc * 128:(cc + 1) * 128].rearrange("k c -> c k"), allow_slow_non_contiguous=True)
            for j, nm in enumerate(("conv_b", "conv_ln_g", "conv_ln_b")):
                kb.dma("sp", cb[cc][:, j:j + 1], W[nm][l][cc * 128:(cc + 1) * 128].rearrange("(c o) -> c o", o=1),
                       accum=(j > 0), allow_slow_non_contiguous=True)
            for k in range(31):
                kb.op("pool", lambda e: e.tensor_scalar(out=dg[:, k, :], in0=identf[:], scalar1=wT[:, k:k + 1], scalar2=None, op0=ALU.mult),
                      reads=[identf[:], wT[:]], writes=[dg[:]], accum=(k > 0))
            kb.op("pool", lambda e: e.memset(hp[:, 0:32], 0.0), writes=[hp[:]])
            kb.op("act", lambda e: e.activation(out=sg[:], in_=gT[:], func=AF.Sigmoid), reads=[gT[:]], writes=[sg[:]])
            kb.op("dve", lambda e: e.tensor_tensor(out=hp[:, 32:32 + S], in0=sg[:], in1=aT[:], op=ALU.mult),
                  reads=[sg[:], aT[:]], writes=[hp[:]], accum=True)
            acc = accs[cc]
            for j in range(NQ5):
                ps = cps[j % 2]
                kb.mmg(ps[:], [(dg[:, k, :], hp[:, j * 512 + 2 + k:j * 512 + 2 + k + 512]) for k in range(31)])
                kb.op("act", lambda e: e.activation(out=acc[:, j * 512:(j + 1) * 512], in_=ps[:], func=AF.Identity, bias=cb[cc][:, 0:1]),
                      reads=[ps[:], cb[cc][:]], writes=[acc[:]], accum=True)
            kb.pop()
        sq_2 = [[kb.sb("csq", [128, 512], F32) for _ in range(2)] for _ in range(2)]
        mean_2 = [kb.sb("cmean", [128, 512], F32) for _ in range(2)]
        var_2 = [kb.sb("cvar", [128, 512], F32) for _ in range(2)]
        yv_2 = [kb.sb("cyv", [128, 512], F32) for _ in range(4)]
        yo_2 = [kb.sb("cyo", [128, 2, 512], BF16) for _ in range(2)]
        mps_2 = [kb.ps("mps", [128, 512], F32) for _ in range(2)]
        sps_2 = [kb.ps("sps", [128, 512], F32) for _ in range(2)]
        for j in range(NQ5):
            sl = slice(j * 512, (j + 1) * 512)
            sq, mean, var, yo, mps, sps = sq_2[j % 2], mean_2[j % 2], var_2[j % 2], yo_2[j % 2], mps_2[j % 2], sps_2[j % 2]
            kb.mmg(mps[:], [(onesf[:], accs[0][:, sl]), (onesf[:], accs[1][:, sl])])
            for cc in range(2):
                kb.op("act", lambda e: e.activation(out=sq[cc][:], in_=accs[cc][:, sl], func=AF.Square),
                      reads=[accs[cc][:]], writes=[sq[cc][:]])
            kb.mmg(sps[:], [(onesf[:], sq[0][:]), (onesf[:], sq[1][:])])
            kb.op("act", lambda e: e.mul(out=mean[:], in_=mps[:], mul=1.0 / 256), reads=[mps[:]], writes=[mean[:]])
            kb.op("dve", lambda e: e.tensor_tensor(out=var[:], in0=mean[:], in1=mean[:], op=ALU.mult),
                  reads=[mean[:]], writes=[var[:]])
            kb.op("dve", lambda e: e.scalar_tensor_tensor(out=var[:], in0=sps[:], scalar=1.0 / 256, in1=var[:],
                                                          op0=ALU.mult, op1=ALU.subtract),
                  reads=[sps[:], var[:]], writes=[var[:]])
            kb.op("act", lambda e: e.activation(out=var[:], in_=var[:], func=AF.Sqrt, bias=EPS), reads=[var[:], cst[:]], writes=[var[:]])
            kb.op("dve", lambda e: e.reciprocal(out=var[:], in_=var[:]), reads=[var[:]], writes=[var[:]])
            for cc in range(2):
                yv = yv_2[(j % 2) * 2 + cc]
                kb.op("dve", lambda e: e.tensor_tensor(out=yv[:], in0=accs[cc][:, sl], in1=mean[:], op=ALU.subtract),
                      reads=[accs[cc][:], mean[:]], writes=[yv[:]])
                kb.op("dve", lambda e: e.tensor_tensor(out=yv[:], in0=yv[:], in1=var[:], op=ALU.mult),
                      reads=[yv[:], var[:]], writes=[yv[:]])
                kb.op("act", lambda e: e.activation(out=yo[:, cc, :], in_=yv[:], func=AF.Silu, scale=cb[cc][:, 1:2], bias=cb[cc][:, 2:3]),
                      reads=[yv[:], cb[cc][:]], writes=[yo[:]], accum=(cc > 0))
            kb.dma("pool", ymixT[512:768, sl].rearrange("(c p) t -> p c t", p=128), yo[:])
        kb.pop()

    def phase_gla(l):
        gc = kb.sb("gc", [128, 3, 128], F32)
        bd = kb.sb("bd", [128, 256], F32)
        hm = kb.sb("hm", [128, 4], F32)
        waf = kb.sb("waf", [16, 128], F32)
        wab = kb.sb("wab", [16, 128], BF16)
        baf = kb.sb("baf", [1, 128], F32)
        bab = kb.sb("bab", [1, 128], BF16)
        one1 = kb.sb("one1", [1, 128], BF16)
        gng = kb.sb("gng", [128, 256], F32)
        gqs = kb.sb("gqs", [128, S], BF16)
        gks = kb.sb("gks", [128, S], BF16)
        gas = kb.sb("gas", [16, S], BF16)
        Sf = kb.sb("Sf", [128, 256], F32)
        Sb = kb.sb("Sb", [128, 256], BF16)
        gk_t_2 = [kb.sb("gk_t", [128, 128], BF16) for _ in range(2)]
        gv_t_2 = [kb.sb("gv_t", [128, 256], BF16) for _ in range(2)]
        gr_t_2 = [kb.sb("gr_t", [128, 256], BF16) for _ in range(2)]
        Lt_2 = [kb.sb("Lt", [128, 128], F32) for _ in range(2)]
        eb_2 = [kb.sb("eb", [128, 128], F32) for _ in range(2)]
        enb_2 = [kb.sb("enb", [128, 128], F32) for _ in range(2)]
        ekd_2 = [kb.sb("ekd", [128, 128], F32) for _ in range(2)]
        qf_2 = [kb.sb("qf", [128, 128], BF16) for _ in range(2)]
        qpad_2 = [kb.sb("qpad", [128, 2, 128], BF16) for _ in range(2)]
        ktf_2 = [kb.sb("ktf", [128, 128], F32) for _ in range(2)]
        ktm_2 = [kb.sb("ktm", [128, 4, 128], BF16) for _ in range(2)]
        kd_2 = [kb.sb("kd", [128, 128], BF16) for _ in range(2)]
        attnb_2 = [kb.sb("attnb", [128, 4, 128], BF16) for _ in range(2)]
        sqo_2 = [kb.sb("sqo", [128, 256], F32) for _ in range(2)]
        ss4_2 = [kb.sb("ss4", [128, 8], F32) for _ in range(2)]
        yt_2 = [kb.sb("yt", [128, 256], F32) for _ in range(2)]
        yb_2 = [kb.sb("yb", [128, 256], BF16) for _ in range(2)]
        yT_2 = [kb.sb("yT", [128, 2, 128], BF16) for _ in range(2)]
        gbank = kb.ps("gbank", [128, 512], F32)
        x_ps_2 = [gbank[:, 0:128]] * 2
        b_ps_2 = [gbank[:, 128:256]] * 2
        ku_ps_2 = [gbank[:, 256:384]] * 2
        at_ps_2 = [kb.ps("at_ps", [128, 4, 128], F32)] * 2
        o_ps = kb.ps("o_ps", [128, 256], F32)
        subank = kb.ps("subank", [128, 512], F32)
        su_ps = subank[:, 0:256]
        tp2_2 = [subank[:, 256:384].bitcast(BF16).rearrange("p (c q) -> p c q", q=128)] * 2
        kb.dma("sp", gc[:], C["gla_c"])
        yield
        kb.dma("sp", bd[:], C["gla_bd"])
        yield
        kb.dma("sp", hm[:], C["gla_hm"])
        yield
        kb.dma("sp", waf[:], W["gla_w_alpha"][l])
        yield
        kb.dma("sp", baf[:], W["gla_b_alpha"][l].rearrange("(o f) -> o f", o=1))
        yield
        kb.dma("sp", gng[:], AP(W["gla_norm_g"].tensor, l * 256, [[0, 128], [1, 256]]))
        yield
        kb.dma("sp", gqs[:], gqT)
        yield
        kb.dma("sp", gks[:], gkT)
        yield
        kb.dma("sp", gas[:], gaT)
        yield
        kb.op("dve", lambda e: e.tensor_copy(out=wab[:], in_=waf[:]), reads=[waf[:]], writes=[wab[:]])
        yield
        kb.op("dve", lambda e: e.tensor_copy(out=bab[:], in_=baf[:]), reads=[baf[:]], writes=[bab[:]])
        yield
        kb.op("pool", lambda e: e.memset(one1[:], 1.0), writes=[one1[:]])
        yield
        kb.op("pool", lambda e: e.memset(Sf[:], 0.0), writes=[Sf[:]])
        yield
        kb.op("pool", lambda e: e.memset(Sb[:], 0.0), writes=[Sb[:]])
        yield
        for qpad in qpad_2:
            kb.op("pool", lambda e: e.memset(qpad[:], 0.0), writes=[qpad[:]])
            yield
        ONE = cst[:, 1:2]
        def fe(ti):
            t0 = ti * 128
            ts = slice(t0, t0 + 128)
            (gk_t, gv_t, gr_t, Lt, eb, enb, ekd, qf, qpad, ktf, ktm, kd, attnb, sqo, ss4, yt, yb, yT, x_ps, b_ps, ku_ps, at_ps, tp2) = (gk_t_2[ti % 2], gv_t_2[ti % 2], gr_t_2[ti % 2], Lt_2[ti % 2], eb_2[ti % 2], enb_2[ti % 2], ekd_2[ti % 2], qf_2[ti % 2], qpad_2[ti % 2], ktf_2[ti % 2], ktm_2[ti % 2], kd_2[ti % 2], attnb_2[ti % 2], sqo_2[ti % 2], ss4_2[ti % 2], yt_2[ti % 2], yb_2[ti % 2], yT_2[ti % 2], x_ps_2[ti % 2], b_ps_2[ti % 2], ku_ps_2[ti % 2], at_ps_2[ti % 2], tp2_2[ti % 2])
            kb.dma("sp", gk_t[:], gk_d[ts, :])
            yield
            kb.dma("sp", gv_t[:], gv_d[ts, :])
            yield
            kb.dma("sp", gr_t[:], grs_d[ts, :])
            yield
            kb.mmg(x_ps, [(gas[:, ts], wab[:]), (one1[:], bab[:])])
            yield
            kb.op("act", lambda e: e.activation(out=Lt[:], in_=x_ps, func=AF.Exp, scale=-1.0), reads=[x_ps], writes=[Lt[:]])
            yield
            kb.op("act", lambda e: e.activation(out=Lt[:], in_=Lt[:], func=AF.Ln, bias=ONE), reads=[Lt[:], cst[:]], writes=[Lt[:]])
            yield
            kb.mmg(b_ps, [(Lt[:], gc[:, 0, :])])
            yield
            kb.mmg(ku_ps, [(gc[:, 1, :], Lt[:])])
            yield
            kb.op("act", lambda e: e.activation(out=eb[:], in_=b_ps, func=AF.Exp), reads=[b_ps], writes=[eb[:]])
            yield
            kb.op("act", lambda e: e.activation(out=enb[:], in_=b_ps, func=AF.Exp, scale=-1.0), reads=[b_ps], writes=[enb[:]])
            yield
            kb.op("act", lambda e: e.activation(out=ekd[:], in_=ku_ps, func=AF.Exp), reads=[ku_ps], writes=[ekd[:]])
            yield
            kb.op("dve", lambda e: e.scalar_tensor_tensor(out=qf[:], in0=gqs[:, ts], scalar=32.0 ** -0.5, in1=eb[:],
                                                          op0=ALU.mult, op1=ALU.mult),
                  reads=[gqs[:], eb[:]], writes=[qf[:]])
            yield
            kb.op("pool", lambda e: e.tensor_copy(out=qpad[:, 0, 0:64], in_=qf[:, 0:64]), reads=[qf[:]], writes=[qpad[:]])
            yield
            kb.op("pool", lambda e: e.tensor_copy(out=qpad[:, 1, 64:128], in_=qf[:, 64:128]), reads=[qf[:]], writes=[qpad[:]], accum=True)
            yield
            kb.op("dve", lambda e: e.tensor_tensor(out=ktf[:], in0=gks[:, ts], in1=enb[:], op=ALU.mult),
                  reads=[gks[:], enb[:]], writes=[ktf[:]])
            yield
            for h in range(4):
                kb.op("pool", lambda e: e.tensor_scalar(out=ktm[:, h, :], in0=ktf[:], scalar1=hm[:, h:h + 1], scalar2=None, op0=ALU.mult),
                      reads=[ktf[:], hm[:]], writes=[ktm[:]], accum=(h > 0))
                yield
            kb.op("dve", lambda e: e.tensor_tensor(out=kd[:], in0=gk_t[:], in1=ekd[:], op=ALU.mult),
                  reads=[gk_t[:], ekd[:]], writes=[kd[:]])
            yield
            for h in range(4):
                kb.mmg(at_ps[:, h, :], [(ktm[:, h, :], qf[:])])
                yield
            kb.op("dve", lambda e: e.tensor_tensor(out=attnb[:], in0=at_ps[:], in1=gc[:, 2, :].unsqueeze(1).to_broadcast([128, 4, 128]),
                                                   op=ALU.mult),
                  reads=[at_ps[:], gc[:]], writes=[attnb[:]])
            yield

        def be(ti):
            t0 = ti * 128
            ts = slice(t0, t0 + 128)
            (gk_t, gv_t, gr_t, Lt, eb, enb, ekd, qf, qpad, ktf, ktm, kd, attnb, sqo, ss4, yt, yb, yT, x_ps, b_ps, ku_ps, at_ps, tp2) = (gk_t_2[ti % 2], gv_t_2[ti % 2], gr_t_2[ti % 2], Lt_2[ti % 2], eb_2[ti % 2], enb_2[ti % 2], ekd_2[ti % 2], qf_2[ti % 2], qpad_2[ti % 2], ktf_2[ti % 2], ktm_2[ti % 2], kd_2[ti % 2], attnb_2[ti % 2], sqo_2[ti % 2], ss4_2[ti % 2], yt_2[ti % 2], yb_2[ti % 2], yT_2[ti % 2], x_ps_2[ti % 2], b_ps_2[ti % 2], ku_ps_2[ti % 2], at_ps_2[ti % 2], tp2_2[ti % 2])
            kb.mm(o_ps[:], qpad[:, 0, :], Sb[:], start=True, stop=False)
            yield
            for ch in range(2):
                cs = slice(ch * 64, (ch + 1) * 64)
                kb.mmg(su_ps, [(kd[cs, :], gv_t[cs, :])])
                yield
                dcol = eb[:, ch * 64 + 63:ch * 64 + 64]
                kb.op("dve", lambda e: e.scalar_tensor_tensor(out=Sf[:], in0=Sf[:], scalar=dcol, in1=su_ps, op0=ALU.mult, op1=ALU.add),
                      reads=[Sf[:], eb[:], su_ps], writes=[Sf[:]])
                yield
                kb.op("pool", lambda e: e.tensor_tensor(out=Sb[:], in0=Sf[:], in1=bd[:], op=ALU.mult),
                      reads=[Sf[:], bd[:]], writes=[Sb[:]])
                yield
                if ch == 0:
                    kb.mm(o_ps[:], qpad[:, 1, :], Sb[:], start=False, stop=False)
                    yield
            for h in range(4):
                kb.mm(o_ps[:, h * 64:(h + 1) * 64], attnb[:, h, :], gv_t[:, h * 64:(h + 1) * 64], start=False, stop=(h == 3))
                yield
            kb.op("act", lambda e: e.activation(out=sqo[:], in_=o_ps[:], func=AF.Square), reads=[o_ps[:]], writes=[sqo[:]])
            yield
            kb.op("dve", lambda e: e.tensor_reduce(out=ss4[:, 0:4], in_=sqo[:].rearrange("p (h d) -> p h d", d=64), axis=AX.X, op=ALU.add),
                  reads=[sqo[:]], writes=[ss4[:]])
            yield
            kb.op("act", lambda e: e.activation(out=ss4[:, 4:8], in_=ss4[:, 0:4], func=AF.Sqrt, scale=1.0 / 64, bias=EPS),
                  reads=[ss4[:], cst[:]], writes=[ss4[:]])
            yield
            kb.op("dve", lambda e: e.reciprocal(out=ss4[:, 4:8], in_=ss4[:, 4:8]), reads=[ss4[:]], writes=[ss4[:]])
            yield
            kb.op("dve", lambda e: e.tensor_tensor(out=yt[:].rearrange("p (h d) -> p h d", d=64),
                                                   in0=o_ps[:].rearrange("p (h d) -> p h d", d=64),
                                                   in1=ss4[:, 4:8].unsqueeze(2).to_broadcast([128, 4, 64]), op=ALU.mult),
                  reads=[o_ps[:], ss4[:]], writes=[yt[:]])
            yield
            kb.op("pool", lambda e: e.tensor_tensor(out=yt[:], in0=yt[:], in1=gng[:], op=ALU.mult), reads=[yt[:], gng[:]], writes=[yt[:]])
            yield
            kb.op("dve", lambda e: e.tensor_tensor(out=yb[:], in0=yt[:], in1=gr_t[:], op=ALU.mult), reads=[yt[:], gr_t[:]], writes=[yb[:]])
            yield
            for c in range(2):
                kb.tr(tp2[:, c, :], yb[:, c * 128:(c + 1) * 128], ident[:])
                yield
            kb.op("act", lambda e: e.copy(out=yT[:], in_=tp2), reads=[tp2], writes=[yT[:]])
            yield
            kb.dma("pool", ymixT[768:1024, ts].rearrange("(c p) t -> p c t", p=128), yT[:])
            yield

        yield from fe(0)
        for ti in range(NT):
            if ti + 1 < NT:
                yield from fe(ti + 1)
            yield from be(ti)

    def phase_outproj(l, xsrc, xdst):
        kb.push()
        Wo = kb.sb("Wo", [128, 8, D], BF16)
        stg = [kb.sb("stg", [128, D], F32) for _ in range(2)]
        load_w(Wo, W["w_out"][l], 8, D, None, stg)
        yms = [kb.sb("ym", [128, 8, 512], BF16) for _ in range(2)]
        xts = [kb.sb("xt", [128, D], F32) for _ in range(2)]
        xos = [kb.sb("xo", [128, D], F32) for _ in range(2)]
        pss = [kb.ps("ps", [128, 512], F32) for _ in range(4)]
        kb.dma("sp", yms[0][:], ymixT[:, 0:512].rearrange("(c p) t -> p c t", p=128))
        for i in range(NQ5):
            t0 = i * 512
            ym = yms[i % 2]
            if i + 1 < NQ5:
                kb.dma("sp", yms[(i + 1) % 2][:], ymixT[:, t0 + 512:t0 + 1024].rearrange("(c p) t -> p c t", p=128))
            for s in range(4):
                xt, xo = xts[s % 2], xos[s % 2]
                rows = slice(t0 + s * 128, t0 + (s + 1) * 128)
                kb.dma("sp", xt[:], xsrc[rows, :])
                for half in range(2):
                    hs = slice(half * 512, (half + 1) * 512)
                    ps = pss[(s % 2) * 2 + half]
                    kb.mmg(ps[:], [(ym[:, c, s * 128:(s + 1) * 128], Wo[:, c, hs]) for c in range(8)])
                    kb.op("dve", lambda e: e.tensor_tensor(out=xo[:, hs], in0=xt[:, hs], in1=ps[:], op=ALU.add),
                          reads=[xt[:], ps[:]], writes=[xo[:]], accum=(half > 0))
                kb.dma("pool", xdst[rows, :], xo[:])
        kb.pop()

    act_d = dram("ffn_act", [DFF, S], BF16)

    def phase_ffn_a(l, xsrc):
        kb.push()
        Wg = kb.sb("Wg", [128, 8, DFF], BF16)
        Wu = kb.sb("Wu", [128, 8, DFF], BF16)
        stg = [kb.sb("stg", [128, DFF], F32) for _ in range(2)]
        gcol = kb.sb("gcol", [128, 8], F32)
        load_gcol(gcol, W["ffn_norm_g"][l])
        load_w(Wg, W["ffn_w_gate"][l], 8, DFF, gcol, stg)
        load_w(Wu, W["ffn_w_up"][l], 8, DFF, gcol, stg)
        nb = NormBufs()
        hTs = [kb.sb("hT", [128, 8, 512], BF16) for _ in range(2)]
        aTs = [kb.sb("aT", [128, 11, 512], BF16) for _ in range(2)]
        sg = [kb.sb("sg", [128, 512], BF16) for _ in range(2)]
        pg = [kb.ps("pg", [128, 512], F32) for _ in range(2)]
        pu = [kb.ps("pu", [128, 512], F32) for _ in range(2)]
        norm_pre(nb, xsrc, 0)
        norm_tr(nb, hTs[0])
        for i in range(NQ5):
            t0 = i * 512
            hT = hTs[i % 2]
            for f in range(22):
                if f == 2 and i + 1 < NQ5:
                    norm_pre(nb, xsrc, t0 + 512)
                if f == 14 and i + 1 < NQ5:
                    norm_tr(nb, hTs[(i + 1) % 2])
                fs = slice(f * 128, (f + 1) * 128)
                aT = aTs[f // 11]
                kb.mmg(pg[f % 2][:], [(Wg[:, c, fs], hT[:, c, :]) for c in range(8)])
                kb.mmg(pu[f % 2][:], [(Wu[:, c, fs], hT[:, c, :]) for c in range(8)])
                kb.op("act", lambda e: e.activation(out=sg[f % 2][:], in_=pg[f % 2][:], func=AF.Silu),
                      reads=[pg[f % 2][:]], writes=[sg[f % 2][:]])
                kb.op("dve", lambda e: e.tensor_tensor(out=aT[:, f % 11, :], in0=sg[f % 2][:], in1=pu[f % 2][:], op=ALU.mult),
                      reads=[sg[f % 2][:], pu[f % 2][:]], writes=[aT[:]], accum=True)
                if f % 11 == 10:
                    hf = f // 11
                    kb.dma("pool", act_d[hf * 1408:(hf + 1) * 1408, t0:t0 + 512].rearrange("(f p) t -> p f t", p=128), aT[:])
        kb.pop()

    def phase_ffn_b(l, xsrc, xdst):
        kb.push()
        Wd = kb.sb("Wd", [128, 22, D], BF16)
        stg = [kb.sb("stg", [128, D], F32) for _ in range(2)]
        load_w(Wd, W["ffn_w_down"][l], 22, D, None, stg)
        aTs = [kb.sb("aT", [128, 22, 512], BF16) for _ in range(2)]
        xts = [kb.sb("xt", [128, D], F32) for _ in range(2)]
        xos = [kb.sb("xo", [128, D], F32) for _ in range(2)]
        pss = [kb.ps("ps", [128, 512], F32) for _ in range(4)]
        kb.dma("sp", aTs[0][:], act_d[:, 0:512].rearrange("(f p) t -> p f t", p=128))
        for i in range(NQ5):
            t0 = i * 512
            aT = aTs[i % 2]
            if i + 1 < NQ5:
                kb.dma("sp", aTs[(i + 1) % 2][:], act_d[:, t0 + 512:t0 + 1024].rearrange("(f p) t -> p f t", p=128))
            for s in range(4):
                xt, xo = xts[s % 2], xos[s % 2]
                rows = slice(t0 + s * 128, t0 + (s + 1) * 128)
                kb.dma("sp", xt[:], xsrc[rows, :])
                for half in range(2):
                    hs = slice(half * 512, (half + 1) * 512)
                    ps = pss[(s % 2) * 2 + half]
                    kb.mmg(ps[:], [(aT[:, f, s * 128:(s + 1) * 128], Wd[:, f, hs]) for f in range(22)])
                    kb.op("dve", lambda e: e.tensor_tensor(out=xo[:, hs], in0=xt[:, hs], in1=ps[:], op=ALU.add),
                          reads=[xt[:], ps[:]], writes=[xo[:]], accum=(half > 0))
                kb.dma("pool", xdst[rows, :], xo[:])
        kb.pop()

    def phase_ple(l, xsrc, xdst):
        kb.push()
        Wpg = kb.sb("Wpg", [128, 8, D], BF16)
        Wpp = kb.sb("Wpp", [128, 2, D], BF16)
        stg = [kb.sb("stg", [128, D], F32) for _ in range(2)]
        gcol = kb.sb("gcol", [128, 8], F32)
        load_gcol(gcol, W["ple_norm_g"][l])
        load_w(Wpg, W["ple_w_gate"][l], 8, D, gcol, stg)
        load_w(Wpp, W["ple_w_proj"][l], 2, D, None, stg)
        nb = NormBufs()
        x2s = [kb.sb("x2", [128, D], F32) for _ in range(2)]
        hTs = [kb.sb("hT", [128, 8, 512], BF16) for _ in range(2)]
        pf = kb.sb("pf", [128, 256], F32)
        pb = kb.sb("pb", [128, 256], BF16)
        pTt = kb.sb("pTt", [128, 2, 128], BF16)
        sgm = kb.sb("sgm", [128, 512], F32)
        xos = [kb.sb("xo", [128, D], F32) for _ in range(2)]
        tp2 = kb.ps("tp2", [128, 2, 128], BF16)
        pg = kb.ps("pg", [128, 512], F32)
        pp = kb.ps("pp", [128, 512], F32)
        norm_pre(nb, xsrc, 0)
        norm_tr(nb, hTs[0])
        for i in range(NQ5):
            t0 = i * 512
            hT = hTs[i % 2]
            for s in range(4):
                if s == 1 and i + 1 < NQ5:
                    norm_pre(nb, xsrc, t0 + 512)
                if s == 3 and i + 1 < NQ5:
                    norm_tr(nb, hTs[(i + 1) % 2])
                x2, xo = x2s[s % 2], xos[s % 2]
                rows = slice(t0 + s * 128, t0 + (s + 1) * 128)
                kb.dma("sp", x2[:], xsrc[rows, :])
                kb.dma("sp", pf[:], p_in[l, rows, :])
                kb.op("dve", lambda e: e.tensor_copy(out=pb[:], in_=pf[:]), reads=[pf[:]], writes=[pb[:]])
                for c in range(2):
                    kb.tr(tp2[:, c, :], pb[:, c * 128:(c + 1) * 128], ident[:])
                kb.op("act", lambda e: e.copy(out=pTt[:], in_=tp2[:]), reads=[tp2[:]], writes=[pTt[:]])
                for half in range(2):
                    hs = slice(half * 512, (half + 1) * 512)
                    kb.mmg(pg[:], [(hT[:, c, s * 128:(s + 1) * 128], Wpg[:, c, hs]) for c in range(8)])
                    kb.mmg(pp[:], [(pTt[:, c, :], Wpp[:, c, hs]) for c in range(2)])
                    kb.op("act", lambda e: e.activation(out=sgm[:], in_=pg[:], func=AF.Sigmoid), reads=[pg[:]], writes=[sgm[:]])
                    kb.op("dve", lambda e: e.tensor_tensor(out=sgm[:], in0=sgm[:], in1=pp[:], op=ALU.mult),
                          reads=[sgm[:], pp[:]], writes=[sgm[:]])
                    kb.op("dve", lambda e: e.tensor_tensor(out=xo[:, hs], in0=x2[:, hs], in1=sgm[:], op=ALU.add),
                          reads=[x2[:], sgm[:]], writes=[xo[:]], accum=(half > 0))
                kb.dma("pool", xdst[rows, :], xo[:])
        kb.pop()

    def phase_final(xsrc):
        kb.push()
        gfin = kb.sb("gfin", [128, D], F32)
        kb.dma("sp", gfin[:], AP(W["final_norm_g"].tensor, 0, [[0, 128], [1, D]]))
        xt = kb.sb("xt", [128, D], F32)
        sq = kb.sb("sq", [128, D], F32)
        st4 = kb.sb("st4", [128, 4], F32)
        xo = kb.sb("xo", [128, D], F32)
        for ti in range(NT):
            rows = slice(ti * 128, (ti + 1) * 128)
            kb.dma("sp", xt[:], xsrc[rows, :])
            kb.op("act", lambda e: e.activation(out=sq[:], in_=xt[:], func=AF.Square, accum_out=st4[:, 0:1]),
                  reads=[xt[:]], writes=[sq[:], st4[:]])
            kb.op("act", lambda e: e.activation(out=st4[:, 1:2], in_=st4[:, 0:1], func=AF.Sqrt, scale=1.0 / D, bias=EPS),
                  reads=[st4[:], cst[:]], writes=[st4[:]])
            kb.op("dve", lambda e: e.reciprocal(out=st4[:, 2:3], in_=st4[:, 1:2]), reads=[st4[:]], writes=[st4[:]])
            kb.op("dve", lambda e: e.scalar_tensor_tensor(out=xo[:], in0=xt[:], scalar=st4[:, 2:3], in1=gfin[:], op0=ALU.mult, op1=ALU.mult),
                  reads=[xt[:], st4[:], gfin[:]], writes=[xo[:]])
            kb.dma("pool", out_d[rows, :], xo[:])
        kb.pop()

    if on("tables"):
        phase_tables()
    xcur = x_in
    bufs2 = [xa, xb]
    bi = 0
    for l in range(layers):
        if on("inproj"):
            phase_inproj(l, xcur)
        if on("compress"):
            phase_compress(l)
        if on("select") or on("gla"):
            kb.push()
            gens = []
            if on("select"):
                gens += [phase_select(l, [0]), phase_select(l, [1])]
            if on("gla"):
                gens.append(phase_gla(l))
            while gens:
                for gen in list(gens):
                    try:
                        next(gen)
                    except StopIteration:
                        gens.remove(gen)
            kb.pop()
        if on("attn"):
            phase_attn(l)
        if on("conv"):
            phase_conv(l)
        x1 = bufs2[bi]
        x2 = bufs2[1 - bi]
        if on("outproj"):
            phase_outproj(l, xcur, x1)
        if on("ffn"):
            phase_ffn_a(l, x1)
            phase_ffn_b(l, x1, x2)
        if on("ple"):
            phase_ple(l, x2, x1)
        xcur = x1
        bi = 1 - bi
    if final:
        phase_final(xcur)
    kb.pop()
    return nc, hc


_CACHE = {}


def kernel(**inputs):
    S = 8192
    if "nc" not in _CACHE:
        _CACHE["nc"] = build(S=S, layers=2)
    nc, hc = _CACHE["nc"]
    x = np.asarray(inputs["x"], dtype=np.float32)
    p = np.asarray(inputs["p"], dtype=np.float32)
    shared = {n: np.ascontiguousarray(np.asarray(inputs[n], dtype=np.float32)) for n in WNAMES}
    shared.update(hc)
    in_maps = []
    for b in range(8):
        m = dict(shared)
        m["x"] = np.ascontiguousarray(x[b])
        m["p"] = np.ascontiguousarray(p[:, b])
        in_maps.append(m)
    res = run_bass_kernel_spmd(nc, in_maps, core_ids=list(range(8)))
    return np.stack([np.asarray(r["out"], dtype=np.float32) for r in res.results], axis=0)
```
